# Optimizing a Trainium2 kernel written in Bass

```python
import jax, jax.numpy as jnp
from jax import lax
import numpy as np

D_MODEL = 2048
BATCH = 2
SEQ = 8192
DEPTH = 2

CHUNK = 64
N_MIXERS = 2
D_FF = 4 * D_MODEL
NORM_EPS = 1e-6

HGRN_EXPAND = 128
HGRN_HEADS = D_MODEL // HGRN_EXPAND
HGRN_DF = HGRN_HEADS * HGRN_EXPAND
HGRN_DV = D_MODEL // HGRN_HEADS

GLA_HEADS = 4
GLA_KEY_DIM = D_MODEL // 2
GLA_VALUE_DIM = D_MODEL
GLA_DK = GLA_KEY_DIM // GLA_HEADS
GLA_DV = GLA_VALUE_DIM // GLA_HEADS
GLA_GATE_RANK = 16
GLA_GATE_NORMALIZER = 16.0

N_HGRN_LAYERS = (DEPTH + 1) // 2
N_GLA_LAYERS = DEPTH // 2

kernel_name = "hybrid_hgrn2_gla_sqrelu_sandwich"


def rmsnorm(x, w):
    xf = x.astype(jnp.float32)
    y = xf * lax.rsqrt(jnp.mean(xf * xf, axis=-1, keepdims=True) + NORM_EPS)
    return (y * w.astype(jnp.float32)).astype(x.dtype)


def gated_head_norm(o, gain, gate):
    o = o * lax.rsqrt(jnp.mean(o * o, axis=-1, keepdims=True) + NORM_EPS)
    o = o * gain.astype(jnp.float32) * jax.nn.silu(gate.astype(jnp.float32))
    B, T, H, dv = o.shape
    return o.reshape(B, T, H * dv)


def chunk_gated_linear_attention(q, k, v, log_decay):
    B, T, H, dk = q.shape
    dv = v.shape[-1]
    n_chunks = T // CHUNK

    def to_chunks(a):
        a = a.astype(jnp.float32).reshape(B, n_chunks, CHUNK, H, a.shape[-1])
        return a.transpose(1, 0, 3, 2, 4)

    causal = jnp.tril(jnp.ones((CHUNK, CHUNK), dtype=bool))[:, :, None]

    def step(S, inp):
        qi, ki, vi, gi = inp
        b = jnp.cumsum(gi, axis=2)
        o_inter = jnp.einsum('bhid,bhde->bhie', qi * jnp.exp(b), S)
        diff = b[:, :, :, None, :] - b[:, :, None, :, :]
        decay = jnp.exp(jnp.where(causal, diff, -jnp.inf))
        scores = jnp.einsum('bhid,bhjd,bhijd->bhij', qi, ki, decay)
        o = o_inter + jnp.einsum('bhij,bhje->bhie', scores, vi)
        b_last = b[:, :, -1:, :]
        S = jnp.exp(b_last[:, :, 0, :])[..., None] * S + jnp.einsum(
            'bhjd,bhje->bhde', ki * jnp.exp(b_last - b), vi)
        return S, o

    S0 = jnp.zeros((B, H, dk, dv), jnp.float32)
    _, o = lax.scan(step, S0, (to_chunks(q), to_chunks(k), to_chunks(v), to_chunks(log_decay)))
    return o.transpose(1, 0, 3, 2, 4).reshape(B, T, H, dv)


def hgrn2_mixer(h, w_in, lower_bound, norm_w, w_out):
    B, T, _ = h.shape
    proj = h @ w_in
    q, f, i, g = jnp.split(proj, [HGRN_DF, 2 * HGRN_DF, 2 * HGRN_DF + D_MODEL], axis=-1)
    q = jax.nn.silu(q.astype(jnp.float32))
    lb = lower_bound.astype(jnp.float32)
    forget = lb + (1.0 - lb) * jax.nn.sigmoid(f.astype(jnp.float32))
    k = 1.0 - forget
    log_f = jnp.log(forget)
    hd = lambda a, d: a.reshape(B, T, HGRN_HEADS, d)
    o = chunk_gated_linear_attention(hd(q, HGRN_EXPAND), hd(k, HGRN_EXPAND),
                                     hd(i, HGRN_DV), hd(log_f, HGRN_EXPAND))
    o = gated_head_norm(o, norm_w, hd(g, HGRN_DV)).astype(h.dtype)
    return o @ w_out


def gla_mixer(h, w_in, w_gk, b_gk, norm_w, w_out):
    B, T, _ = h.shape
    proj = h @ w_in
    q, k, v, g, r = jnp.split(proj, [GLA_KEY_DIM, 2 * GLA_KEY_DIM, 2 * GLA_KEY_DIM + GLA_VALUE_DIM,
                                     2 * GLA_KEY_DIM + 2 * GLA_VALUE_DIM], axis=-1)
    gk = jax.nn.log_sigmoid((r @ w_gk + b_gk).astype(jnp.float32)) / GLA_GATE_NORMALIZER
    q = q.astype(jnp.float32) * (GLA_DK ** -0.5)
    hd = lambda a, d: a.reshape(B, T, GLA_HEADS, d)
    o = chunk_gated_linear_attention(hd(q, GLA_DK), hd(k, GLA_DK), hd(v, GLA_DV), hd(gk, GLA_DK))
    o = gated_head_norm(o, norm_w, hd(g, GLA_DV)).astype(h.dtype)
    return o @ w_out


def sqrelu_mlp(h, w_up, w_down):
    u = jax.nn.relu(h @ w_up)
    return (u * u) @ w_down


def setup_inputs(seed: int = 0) -> dict:
    key = jax.random.key(seed)
    ks = jax.random.split(key, 20)
    nrm = lambda k, shape, s: jax.random.normal(k, shape, jnp.float32) * s
    gain = lambda k, shape: 1.0 + 0.02 * jax.random.normal(k, shape, jnp.float32)
    hgrn_in = 3 * HGRN_DF + D_MODEL
    gla_in = 2 * GLA_KEY_DIM + 2 * GLA_VALUE_DIM + GLA_GATE_RANK
    return {
        "x": jax.random.normal(ks[0], (BATCH, SEQ, D_MODEL), jnp.float32),
        "norm_mix_pre": gain(ks[1], (DEPTH, D_MODEL)),
        "norm_mix_post": gain(ks[2], (DEPTH, D_MODEL)),
        "norm_mlp_pre": gain(ks[3], (DEPTH, D_MODEL)),
        "norm_mlp_post": gain(ks[4], (DEPTH, D_MODEL)),
        "hgrn_w_in": nrm(ks[5], (N_HGRN_LAYERS, D_MODEL, hgrn_in), D_MODEL ** -0.5),
        "hgrn_lb_logits": nrm(ks[6], (DEPTH + 1, HGRN_DF), 0.1),
        "hgrn_norm": gain(ks[7], (N_HGRN_LAYERS, HGRN_DV)),
        "hgrn_w_out": nrm(ks[8], (N_HGRN_LAYERS, D_MODEL, D_MODEL), D_MODEL ** -0.5),
        "gla_w_in": nrm(ks[9], (N_GLA_LAYERS, D_MODEL, gla_in), D_MODEL ** -0.5),
        "gla_w_gk": nrm(ks[10], (N_GLA_LAYERS, GLA_GATE_RANK, GLA_KEY_DIM), GLA_GATE_RANK ** -0.5),
        "gla_b_gk": nrm(ks[11], (N_GLA_LAYERS, GLA_KEY_DIM), 0.1),
        "gla_norm": gain(ks[12], (N_GLA_LAYERS, GLA_DV)),
        "gla_w_out": nrm(ks[13], (N_GLA_LAYERS, D_MODEL, D_MODEL), D_MODEL ** -0.5),
        "mlp_w_up": nrm(ks[14], (DEPTH, D_MODEL, D_FF), D_MODEL ** -0.5),
        "mlp_w_down": nrm(ks[15], (DEPTH, D_FF, D_MODEL), D_FF ** -0.5),
    }


def reference(x, norm_mix_pre, norm_mix_post, norm_mlp_pre, norm_mlp_post,
              hgrn_w_in, hgrn_lb_logits, hgrn_norm, hgrn_w_out,
              gla_w_in, gla_w_gk, gla_b_gk, gla_norm, gla_w_out,
              mlp_w_up, mlp_w_down):
    lower_bounds = jnp.cumsum(jax.nn.softmax(hgrn_lb_logits.astype(jnp.float32), axis=0), axis=0)
    h = x
    for i in range(DEPTH):
        hn = rmsnorm(h, norm_mix_pre[i])
        j = i // N_MIXERS
        if i % N_MIXERS == 0:
            m = hgrn2_mixer(hn, hgrn_w_in[j], lower_bounds[i], hgrn_norm[j], hgrn_w_out[j])
        else:
            m = gla_mixer(hn, gla_w_in[j], gla_w_gk[j], gla_b_gk[j], gla_norm[j], gla_w_out[j])
        h = h + rmsnorm(m, norm_mix_post[i])
        f = sqrelu_mlp(rmsnorm(h, norm_mlp_pre[i]), mlp_w_up[i], mlp_w_down[i])
        h = h + rmsnorm(f, norm_mlp_post[i])
    return h
```

```python
from contextlib import ExitStack
import os
import math
import numpy as np
import concourse.bass as bass
import concourse.mybir as mybir
from concourse.bass_utils import run_bass_kernel_spmd

F32 = mybir.dt.float32
BF16 = mybir.dt.bfloat16
AF = mybir.ActivationFunctionType
ALU = mybir.AluOpType

NCORES = 8
D = 2048
KC = 16
TC = 2048
NB = 4
BT = 512
DFF = 8192
EPS = 1e-6
ENGINES = ("pe", "act", "dve", "pool", "sp")


class Sem:
    __slots__ = ("h", "v")

    def __init__(self, h):
        self.h = h
        self.v = 0


class Tok:
    __slots__ = ("w", "r")

    def __init__(self):
        self.w = None
        self.r = {}


class Sched:
    def __init__(self, sem_alloc):
        self.sem_alloc = sem_alloc
        self.q = {e: [] for e in ENGINES}
        self.esem = {e: Sem(sem_alloc("c_" + e)) for e in ENGINES}
        self.waited = {e: {} for e in ENGINES}
        self.allsems = list(self.esem.values())
        self.toks = {}

    def tk(self, *key):
        t = self.toks.get(key)
        if t is None:
            t = self.toks[key] = Tok()
        return t

    def new_dma_sem(self, name):
        s = Sem(self.sem_alloc(name))
        self.allsems.append(s)
        return s

    def _waits(self, eng, deps, skip_own):
        own = self.esem[eng]
        waits = []
        wd = self.waited[eng]
        for s, v in deps.items():
            if s is own and (skip_own or v > own.v):
                continue
            if wd.get(s, 0) < v:
                waits.append((s, v))
                wd[s] = v
        return waits

    def op(self, eng, fn, reads=(), writes=(), dma_sem=None, inc=True, incv=None, record=True,
           skip_own=None):
        if skip_own is None:
            skip_own = (eng == "pe")
        deps = {}
        for t in reads:
            if t.w is not None and deps.get(t.w[0], 0) < t.w[1]:
                deps[t.w[0]] = t.w[1]
        for t in writes:
            if t.w is not None and deps.get(t.w[0], 0) < t.w[1]:
                deps[t.w[0]] = t.w[1]
            for s, v in t.r.items():
                if deps.get(s, 0) < v:
                    deps[s] = v
        waits = self._waits(eng, deps, skip_own)
        if dma_sem is not None:
            csem, iv = dma_sem, (16 if incv is None else incv)
        else:
            csem, iv = self.esem[eng], 1
        if inc:
            csem.v += iv
            val = csem.v
        else:
            val = csem.v + iv
        self.q[eng].append((waits, fn, csem if inc else None, iv))
        if record:
            self.record(reads, writes, (csem, val))
        return (csem, val)

    def record(self, reads, writes, sv):
        csem, val = sv
        for t in reads:
            if t.r.get(csem, 0) < val:
                t.r[csem] = val
        for t in writes:
            t.w = (csem, val)
            t.r = {}

    def barrier(self):
        svs = [(s, s.v) for s in self.allsems if s.v > 0]
        for e in ENGINES:
            waits = self._waits(e, dict(svs), False)
            if waits:
                self.q[e].append((waits, None, None, 0))
        for e in ENGINES:
            if self.esem[e].v > 12000:
                s = Sem(self.sem_alloc("c_" + e))
                self.esem[e] = s
                self.allsems.append(s)

    def emit(self, block):
        def run(eh, items):
            for waits, fn, csem, iv in items:
                for s, v in waits:
                    eh.wait_ge(s.h, v)
                if fn is None:
                    continue
                ins = fn(eh)
                if csem is not None:
                    ins.then_inc(csem.h, iv)

        m = {"pe": block.tensor, "act": block.scalar, "dve": block.vector,
             "pool": block.gpsimd, "sp": block.sync}
        for e in ENGINES:
            items = self.q[e]
            if items:
                m[e](lambda eh, items=items: run(eh, items))


class Region:
    def __init__(self, t, nwords):
        self.t, self.n, self.off = t, nwords, 0

    def reset(self):
        self.off = 0

    def alloc(self, free_shape, dt):
        nel = int(np.prod(free_shape))
        nbytes = nel * (4 if dt == F32 else 2)
        words = (nbytes + 31) // 32 * 8
        assert self.off + words <= self.n, ("region overflow", self.off, words, self.n)
        v = self.t[:, self.off:self.off + words]
        self.off += words
        if dt != F32:
            v = v.bitcast(dt)
        v = v[:, 0:nel]
        if len(free_shape) == 2:
            v = v.rearrange("p (a b) -> p a b", b=free_shape[1])
        elif len(free_shape) == 3:
            v = v.rearrange("p (a b c) -> p a b c", b=free_shape[1], c=free_shape[2])
        return v


LAYERS = [
    dict(kind="hgrn", NH=16, ND=1, NE=1, escale=1.0),
    dict(kind="gla", NH=4, ND=2, NE=4, escale=1.0 / 16.0),
]


def build_program(n_layers=2, dbg=None, ncores=NCORES):
    nc = bass.Bass("TRN2", target_bir_lowering=False)
    dt_in = lambda name, shape: nc.dram_tensor(name, shape, F32, kind="ExternalInput").ap()
    x_d = dt_in("x", [TC, D])
    pm_d = dt_in("pm", [128, 8])
    nw_d = dt_in("nw", [128, 8, KC])
    lbl_d = dt_in("lbl", [128, 3, KC])
    hgn_d = dt_in("hgn", [128, 1])
    glan_d = dt_in("glan", [128, 4])
    bgk_d = dt_in("bgk", [128, 8])
    wgk_d = dt_in("wgk", [16, 1024])
    wr_d = dt_in("wr", [128, KC, 16])
    hg_in_d = dt_in("hg_in", [64, 128, KC, 128])
    hg_out_d = dt_in("hg_out", [16, 128, KC, 128])
    gl_in_d = dt_in("gl_in", [48, 128, KC, 128]) if n_layers > 1 else None
    gl_out_d = dt_in("gl_out", [16, 128, KC, 128]) if n_layers > 1 else None
    up_d = dt_in("w_up", [n_layers, 64, 128, KC, 128])
    dn_d = dt_in("w_dn", [n_layers, 16, 128, 64, 128])
    out_d = nc.dram_tensor("out", [TC, D], F32, kind="ExternalOutput").ap()
    dbg_d = nc.dram_tensor("dbg", [128, KC, TC], F32, kind="ExternalOutput").ap() if dbg else None
    H_d = nc.dram_tensor("Hs", [128, KC, TC], F32).ap()
    OL_d = nc.dram_tensor("OLs", [128, KC, TC], F32).ap()
    QS_d = nc.dram_tensor("QSs", [128, KC, TC], BF16).ap()
    CCG = {0: 8, 1: 1}
    ccw = {0: 8 * 129, 1: 2 * 513}
    ccn = {0: 2, 1: 4}
    cc_in = {l: [nc.dram_tensor(f"cc_in{l}_{g}", [128, ccw[l]], F32) for g in range(ccn[l])] for l in range(2)}
    cc_out = {l: [nc.dram_tensor(f"cc_out{l}_{g}", [4 * 128, ccw[l]], F32) for g in range(ccn[l])] for l in range(2)}

    with ExitStack() as es:
        sb = lambda name, shape, dt: es.enter_context(nc.sbuf_tensor(name, shape, dt))
        S = Sched(lambda name: es.enter_context(nc.semaphore(name)))
        tk = S.tk

        RA = sb("RA", [128, KC, TC], BF16)
        RB = sb("RB", [128, KC * TC // 2], F32)
        NCW = 18176
        RCt = sb("RC", [128, NCW], F32)
        RC = Region(RCt, NCW)
        cst = sb("cst", [128, 1792], F32)
        CR = Region(cst, 1792)
        banks = [es.enter_context(nc.psum_tensor(f"bank{i}", [128, 512], F32)) for i in range(8)]
        tb = [tk("bank", i) for i in range(8)]

        RBb = RB[:].bitcast(BF16)
        onT = RBb.rearrange("p (a t) -> p a t", t=TC)
        uT = RBb.rearrange("p (a t) -> p a t", t=BT)
        slots1 = RBb.rearrange("p (s a c) -> p s a c", a=KC, c=128)
        hnT = RA

        def act(out, in_, func, reads, writes, **kw):
            S.op("act", lambda e: e.activation(out=out, in_=in_, func=func, **kw), reads, writes)

        def tt(out, in0, in1, op, reads, writes, eng="dve"):
            S.op(eng, lambda e: e.tensor_tensor(out=out, in0=in0, in1=in1, op=op), reads, writes)

        def ts(out, in0, s1, s2, op0, op1, reads, writes, eng="dve"):
            if s2 is None:
                S.op(eng, lambda e: e.tensor_scalar(out=out, in0=in0, scalar1=s1, scalar2=None, op0=op0), reads, writes)
            else:
                S.op(eng, lambda e: e.tensor_scalar(out=out, in0=in0, scalar1=s1, scalar2=s2, op0=op0, op1=op1), reads, writes)

        def stt(out, in0, scalar, in1, op0, op1, reads, writes):
            S.op("dve", lambda e: e.scalar_tensor_tensor(out=out, in0=in0, scalar=scalar, in1=in1, op0=op0, op1=op1),
                 reads, writes)

        def cp(out, in_, reads, writes, eng="dve"):
            S.op(eng, lambda e: e.tensor_copy(out=out, in_=in_), reads, writes)

        def mm_chain(out, pairs, reads, writes, start=True, stop=True):
            n = len(pairs)
            for i, (l, r) in enumerate(pairs):
                st = start and i == 0
                sp_ = stop and i == n - 1
                fn = (lambda e, l=l, r=r, st=st, sp_=sp_: e.matmul(out, lhsT=l, rhs=r, start=st, stop=sp_))
                if n == 1:
                    S.op("pe", fn, reads, writes)
                elif i == 0:
                    S.op("pe", fn, reads, writes, inc=False, record=False)
                elif i < n - 1:
                    S.op("pe", fn, inc=False, record=False)
                else:
                    sv = S.op("pe", fn, record=False)
                    S.record(reads, writes, sv)

        def transpose(out, in_, ident, reads, writes):
            S.op("pe", lambda e: e.transpose(out=out, in_=in_, identity=ident), reads, writes)

        dsem = {}

        def dma(eng, out, in_, reads, writes, semname):
            s = dsem.get(semname)
            if s is None:
                s = dsem[semname] = S.new_dma_sem("d_" + semname)
            return S.op(eng, lambda e: e.dma_start(out=out, in_=in_), reads, writes, dma_sem=s)

        ident_f = CR.alloc([128], F32)
        ident_b = CR.alloc([128], BF16)
        ones_b = CR.alloc([128], BF16)
        mask4 = CR.alloc([4, 128], F32)
        pm = CR.alloc([8], F32)
        nw = CR.alloc([8, KC], F32)
        lbt = CR.alloc([3, KC], F32)
        lb = CR.alloc([KC], F32)
        oml = CR.alloc([KC], F32)
        lnoml = CR.alloc([KC], F32)
        hgn = CR.alloc([1], F32)
        glan = CR.alloc([4], F32)
        bgk = CR.alloc([8], F32)
        epsc = CR.alloc([1], F32)
        wgk_b = CR.alloc([1024], BF16)
        wr_b = CR.alloc([KC, 16], BF16)
        tmpc = CR.alloc([KC], F32)
        tc_ = tk("consts")

        S.op("pool", lambda e: e.memset(ident_f, 1.0), (), [tc_])
        S.op("pool", lambda e: e.affine_select(out=ident_f, in_=ident_f, pattern=[[-1, 128]], compare_op=ALU.is_equal,
                                               fill=0.0, base=0, channel_multiplier=1), [tc_], [tc_])
        S.op("pool", lambda e: e.tensor_copy(out=ident_b, in_=ident_f), [tc_], [tc_])
        S.op("pool", lambda e: e.memset(ones_b, 1.0), (), [tc_])
        S.op("pool", lambda e: e.memset(epsc, EPS), (), [tc_])
        m0 = mask4[:, 0, :]
        S.op("pool", lambda e: e.memset(m0, 1.0), (), [tc_])
        S.op("pool", lambda e: e.affine_select(out=m0, in_=m0, pattern=[[1, 128]], compare_op=ALU.is_ge,
                                               fill=0.0, base=0, channel_multiplier=-1), [tc_], [tc_])
        S.op("pool", lambda e: e.memset(mask4[0:64, 0, 64:128], 0.0), [tc_], [tc_])
        for i in range(1, 4):
            S.op("pool", lambda e, i=i: e.tensor_copy(out=mask4[:, i, :], in_=mask4[:, 0, :]), [tc_], [tc_])
        for dst, src in ((pm, pm_d), (nw, nw_d), (lbt, lbl_d), (hgn, hgn_d), (glan, glan_d), (bgk, bgk_d)):
            dma("sp", dst, src, (), [tc_], "const")
        dma("pool", wgk_b[0:16, :], wgk_d, (), [tc_], "const2")
        dma("pool", wr_b, wr_d, (), [tc_], "const2")
        act(lbt, lbt, AF.Exp, [tc_], [tc_])
        tt(tmpc, lbt[:, 0, :], lbt[:, 1, :], ALU.add, [tc_], [tc_])
        tt(tmpc, tmpc, lbt[:, 2, :], ALU.add, [tc_], [tc_])
        S.op("dve", lambda e: e.reciprocal(out=tmpc, in_=tmpc), [tc_], [tc_])
        tt(lb, lbt[:, 0, :], tmpc, ALU.mult, [tc_], [tc_])
        ts(oml, lb, -1.0, 1.0, ALU.mult, ALU.add, [tc_], [tc_])
        act(lnoml, oml, AF.Ln, [tc_], [tc_])
        S.barrier()

        class WStream:
            def __init__(self):
                self.plan = []
                self.pos = 0
                self.issued = 0
                self.slots = None
                self.base = 0

            def set_slots(self, slot_aps, name):
                self.slots = slot_aps
                self.name = name
                self.base = self.pos
                self.issued = self.pos

            def get(self, limit, hold=None):
                i = self.pos
                self.pos += 1
                if hold is None:
                    hold = i
                n = min(hold + len(self.slots), limit, len(self.plan))
                while self.issued < n:
                    j = self.issued
                    sj = (j - self.base) % len(self.slots)
                    dma("pool", self.slots[sj], self.plan[j], (), [tk("slot", self.name, sj)], f"w{self.name}{sj}")
                    self.issued += 1
                si = (i - self.base) % len(self.slots)
                return self.slots[si], tk("slot", self.name, si)

        W = WStream()
        stage_end = {}

        def plan_units():
            for l in range(n_layers):
                L = LAYERS[l]
                hg = L["kind"] == "hgrn"
                for hd in range(L["NH"]):
                    if hg:
                        W.plan += [hg_in_d[hd], hg_in_d[16 + hd], hg_in_d[32 + hd]]
                    else:
                        W.plan += [gl_in_d[2 * hd + i] for i in range(2)] + [gl_in_d[8 + 2 * hd + i] for i in range(2)] \
                            + [gl_in_d[16 + 4 * hd + i] for i in range(4)]
                stage_end[("mix1", l)] = len(W.plan)
                for hd in range(L["NH"]):
                    W.plan += [(hg_in_d[48 + hd] if hg else gl_in_d[32 + 4 * hd + ec]) for ec in range(L["NE"])]
                stage_end[("mix2", l)] = len(W.plan)
                wd = hg_out_d if l == 0 else gl_out_d
                for b in range(NB):
                    W.plan += [wd[n] for n in range(KC)]
                stage_end[("out", l)] = len(W.plan)
                for b in range(NB):
                    W.plan += [up_d[l, fu] for fu in range(64)]
                    W.plan += [dn_d[l, n, :, g * 16:(g + 1) * 16, :] for n in range(KC) for g in range(4)]
                stage_end[("mlp", l)] = len(W.plan)

        plan_units()

        def rstd_from_ssq(ssq_bank, n_feat, rs, rs_t, reads):
            act(rs, ssq_bank, AF.Sqrt, reads, [rs_t], scale=1.0 / n_feat, bias=epsc)
            S.op("dve", lambda e: e.reciprocal(out=rs, in_=rs), [rs_t], [rs_t])

        def cp_any(out, in_, reads, writes, i):
            if i % 2 == 0:
                cp(out, in_, reads, writes)
            else:
                act(out, in_, AF.Copy, reads, writes)

        def finish_block(mT, b, layer_next, last):
            tm = [tk("m", kc) for kc in range(KC)]
            bsl = slice(b * BT, (b + 1) * BT)
            if last:
                ostg = [RC.alloc([512], F32) for _ in range(2)]
                i = 0
                pt = banks[2][:].rearrange("p (a c) -> p a c", c=128)
                for tt_ in range(4):
                    for g in range(4):
                        for j in range(4):
                            kc = g * 4 + j
                            transpose(pt[:, j, :], mT[:, kc, tt_ * 128:(tt_ + 1) * 128], ident_f, [tm[kc], tc_], [tb[2]])
                        o = ostg[i % 2]
                        cp_any(o, banks[2][:], [tb[2]], [tk("ostg", i % 2)], i)
                        r0 = b * BT + tt_ * 128
                        dma("sp", out_d[r0:r0 + 128, g * 512:(g + 1) * 512], o, [tk("ostg", i % 2)], (), f"ostg{i % 2}")
                        i += 1
                return
            for g in range(4):
                dma("sp", H_d[:, g * 4:(g + 1) * 4, bsl], mT[:, g * 4:(g + 1) * 4, :], [tm[g * 4 + j] for j in range(4)],
                    [tk("H", g, b)], f"mst{g}")
            if layer_next is None:
                return
            sqs = [RC.alloc([512], BF16) for _ in range(2)]
            for kc in range(KC):
                q = sqs[kc % 2]
                act(q, mT[:, kc, :], AF.Square, [tm[kc]], [tk("sqs", kc % 2)])
                mm_chain(banks[3][:], [(ones_b, q)], [tk("sqs", kc % 2), tc_], [tb[3]], start=(kc == 0), stop=(kc == KC - 1))
            rs = RC.alloc([512], F32)
            rstd_from_ssq(banks[3][:], D, rs, tk("rs"), [tb[3], tc_])
            for kc in range(KC):
                stt(hnT[:, kc, bsl], mT[:, kc, :], nw[:, layer_next, kc:kc + 1], rs, ALU.mult, ALU.mult,
                    [tm[kc], tk("rs"), tc_], [tk("hn", b)])

        def post_block(mT, b, wpost_idx, layer_next, last, defer=None, defer_arg=None):
            tm = [tk("m", kc) for kc in range(KC)]
            bsl = slice(b * BT, (b + 1) * BT)
            rsm = RC.alloc([512], F32)
            rstd_from_ssq(banks[3][:], D, rsm, tk("rsm"), [tb[3], tc_])
            hch = [RC.alloc([2, 512], F32) for _ in range(2)]

            for i, kc in enumerate(range(0, KC, 2)):
                hc = hch[i % 2]
                th = tk("hch", i % 2)
                dma("sp", hc, H_d[:, kc:kc + 2, bsl], [tk("H", kc // 4, b)], [th], f"hch{i % 2}")
                for j in range(2):
                    stt(mT[:, kc + j, :], mT[:, kc + j, :], nw[:, wpost_idx, kc + j:kc + j + 1], rsm, ALU.mult, ALU.mult,
                        [tm[kc + j], tk("rsm"), tc_], [tm[kc + j]])
                    tt(mT[:, kc + j, :], mT[:, kc + j, :], hc[:, j, :], ALU.add, [tm[kc + j], th], [tm[kc + j]])
            if defer is not None:
                defer(defer_arg, 0, 32)
            if int(os.environ.get('K_OUT', 9)) >= 3:
                finish_block(mT, b, layer_next, last)
            if defer is not None:
                defer(defer_arg, 32, 64)

        def evac_with_ssq(mT, sqs, kc, pj, tpj, first, lastc, i):
            q = sqs[i % 2]
            cp(mT[:, kc, :], pj, [tpj], [tk("m", kc)])
            act(q, mT[:, kc, :], AF.Square, [tk("m", kc)], [tk("sqs", i % 2)])
            mm_chain(banks[3][:], [(ones_b, q)], [tk("sqs", i % 2), tc_], [tb[3]], start=first, stop=lastc)

        def dump(src_fn, n, dt_is_bf16):
            RC.reset()
            stg = RC.alloc([TC], F32)
            for kc in range(n):
                cp(stg, src_fn(kc), [], [tk("dstg")])
                dma("sp", dbg_d[:, kc, :], stg, [tk("dstg")], (), "dstg")

        def stage_pre():
            RC.reset()
            mT = RC.alloc([KC, 512], F32)
            xts = [RC.alloc([D], F32) for _ in range(2)]
            pt = banks[2][:].rearrange("p (a c) -> p a c", c=128)
            for b in range(NB):
                for tt_ in range(4):
                    ti = b * 4 + tt_
                    xt = xts[ti % 2]
                    txt = tk("xt", ti % 2)
                    dma("sp", xt, x_d[ti * 128:(ti + 1) * 128, :], (), [txt], f"xt{ti % 2}")
                    for g in range(4):
                        for j in range(4):
                            kc = g * 4 + j
                            transpose(pt[:, j, :], xt[:, kc * 128:(kc + 1) * 128], ident_f, [txt, tc_], [tb[2]])
                        tms = [tk("m", g * 4 + j) for j in range(4)]
                        cp_any(mT[:, g * 4:(g + 1) * 4, tt_ * 128:(tt_ + 1) * 128], pt, [tb[2]], tms, g)
                finish_block(mT, b, 0, False)

        def stage_mix1(l):
            L = LAYERS[l]
            NH, ND, NE, esc = L["NH"], L["ND"], L["NE"], L["escale"]
            hg = L["kind"] == "hgrn"
            EW = NE * 128
            RC.reset()
            W.set_slots([slots1[:, i] for i in range(16)], "p1")
            lim = stage_end[("mix1", l)]
            T = lambda dt=F32: RC.alloc([512], dt)
            NSET = 2 if hg else 1
            qTs = [[T() for _ in range(ND)] for _ in range(NSET)]
            kTs = [[T() for _ in range(ND)] for _ in range(NSET)]
            lfs = [[T() for _ in range(ND)] for _ in range(NSET)]
            bs = [T() for _ in range(ND)]
            bp = T()
            ex = [T() for _ in range(2)]
            qt_b = [T(BF16) for _ in range(ND)]
            kt_b = [T(BF16) for _ in range(ND)]
            qs_b = [[T(BF16) for _ in range(ND)] for _ in range(2)]
            vTs = [[T(BF16) for _ in range(NE)] for _ in range(NSET)]
            vtm = RC.alloc([4, EW], BF16)
            ktm2 = [RC.alloc([4, ND * 128], BF16) for _ in range(2)]
            sT4 = RC.alloc([4, 128], BF16)
            Sfull = RC.alloc([ND, EW + 1], F32)
            Sst = Sfull[:, :, 0:EW]
            atot = Sfull[:, :, EW:EW + 1]
            NSB = 8 if hg else 4
            Sbf = [RC.alloc([ND, EW], BF16) for _ in range(NSB)]
            oloc = [RC.alloc([NE, 512], F32) for _ in range(2 if hg else 1)]
            carry = RC.alloc([ND, 2], F32)
            dd = RC.alloc([ND, 8], F32)
            Ac = RC.alloc([ND, 8], F32)
            rT = None if hg else RC.alloc([TC], BF16)
            ones_f = RC.alloc([512], F32)
            S.op("dve", lambda e: e.memset(ones_f, 1.0), (), [tk("ones_f")])
            S.op("dve", lambda e: e.memset(ktm2[0][64:128], 0.0), (), [tk("ktm")])
            S.op("dve", lambda e: e.memset(ktm2[1][0:64], 0.0), (), [tk("ktm")])
            pj_i = [0]
            hn_rhs = lambda kc, b: hnT[:, kc, b * BT:(b + 1) * BT]
            ptb = banks[2][:].bitcast(BF16).rearrange("p (a c) -> p a c", c=128)
            sc4 = banks[3][:].rearrange("p (a c) -> p a c", c=128)
            cw = EW + 1
            HPG = CCG[l]
            cins = [c_.ap().rearrange("p (u w) -> p u w", w=cw) for c_ in cc_in[l]]

            if not hg:
                for b in range(NB):
                    pj, tpj = banks[b % 2][0:16, :], tb[b % 2]
                    mm_chain(pj, [(wr_b[:, kc, :], hn_rhs(kc, b)) for kc in range(KC)], [tc_, tk("hn", b)], [tpj])
                    cp(rT[0:16, b * BT:(b + 1) * BT], pj, [tpj], [tk("rT")])

            NHH = min(NH, int(os.environ.get('K_HEADS', NH)))
            slots_by_head = {}

            def emit_P(it):
                hd_, b_ = it // NB, it % NB
                if hd_ not in slots_by_head:
                    hold_ = W.pos
                    slots_by_head[hd_] = {key: [W.get(lim, hold_) for _ in range(n)] for key, n in (("q", ND), ("k", ND), ("v", NE))}
                for key, bank in (("q", 5), ("k", 6), ("v", 7)):
                    slot, tslot = slots_by_head[hd_][key][0]
                    mm_chain(banks[bank][:], [(slot[:, kc, :], hn_rhs(kc, b_)) for kc in range(KC)], [tslot, tk("hn", b_)], [tb[bank]])

            def emit_EV(s_):
                act(qTs[s_][0], banks[5][:], AF.Silu, [tb[5]], [tk("qT", 0, s_)])
                act(lfs[s_][0], banks[6][:], AF.Sigmoid, [tb[6]], [tk("lf", 0, s_)])
                act(kTs[s_][0], banks[6][:], AF.Sigmoid, [tb[6]], [tk("kT", 0, s_)], scale=-1.0)
                act(vTs[s_][0], banks[7][:], AF.Copy, [tb[7]], [tk("vT", 0, s_)])

            for hd in range(NHH):
                if not hg:
                    hold = W.pos
                    slots_h = {key: [W.get(lim, hold) for _ in range(n)] for key, n in (("q", ND), ("k", ND), ("v", NE))}
                for b in range(NB):
                    bsl = slice(b * BT, (b + 1) * BT)
                    par = (hd * NB + b) % 2
                    it = hd * NB + b
                    sx = it % NSET
                    qT, kT, lf, vT = qTs[sx], kTs[sx], lfs[sx], vTs[sx]
                    if hg:
                        if it == 0:
                            emit_P(0)
                            emit_EV(0)
                        if it + 1 < NHH * NB:
                            emit_P(it + 1)

                    def proj(slot_t, evac):
                        slot, tslot = slot_t
                        bi = pj_i[0] % 2
                        pj_i[0] += 1
                        pj, tpj = banks[bi][:], tb[bi]
                        mm_chain(pj, [(slot[:, kc, :], hn_rhs(kc, b)) for kc in range(KC)], [tslot, tk("hn", b)], [tpj])
                        evac(pj, tpj)

                    CUT = int(os.environ.get('K_CUT', 9))
                    for dc in range(ND):
                        col = hd * ND + dc
                        if hg:
                            act(lf[dc], lf[dc], AF.Ln, [tk("lf", dc, sx), tc_], [tk("lf", dc, sx)], scale=oml[:, col:col + 1],
                                bias=lb[:, col:col + 1])
                        else:
                            proj(slots_h["q"][dc], lambda pj, tpj: act(qT[dc], pj, AF.Copy, [tpj], [tk("qT", dc, sx)], scale=1.0 / 16.0))
                            proj(slots_h["k"][dc], lambda pj, tpj: act(kT[dc], pj, AF.Copy, [tpj], [tk("kT", dc, sx)]))
                            bi = pj_i[0] % 2
                            pj_i[0] += 1
                            pj, tpj = banks[bi][:], tb[bi]
                            c0 = hd * 256 + dc * 128
                            mm_chain(pj, [(wgk_b[0:16, c0:c0 + 128], rT[0:16, bsl])], [tc_, tk("rT")], [tpj])
                            act(lf[dc], pj, AF.Sigmoid, [tpj, tc_], [tk("lf", dc, sx)], bias=bgk[:, col:col + 1])
                            act(lf[dc], lf[dc], AF.Ln, [tk("lf", dc, sx)], [tk("lf", dc, sx)])
                        if CUT <= 1:
                            continue
                        init = 0.0 if b == 0 else carry[:, dc, 0:1]
                        S.op("dve", lambda e, dc=dc, init=init, lfd=lf[dc]: e.tensor_tensor_scan(
                            out=bs[dc], data0=ones_f, data1=lfd, initial=init, op0=ALU.mult, op1=ALU.add),
                            [tk("lf", dc, sx), tk("ones_f"), tk("carry", dc)], [tk("bs", dc)])
                        bs3 = bs[dc].rearrange("p (c k) -> p c k", k=64)
                        rcol = bs3[:, :, 63]
                        if b == 0:
                            cp(dd[:, dc, 0:1], rcol[:, 0:1], [tk("bs", dc)], [tk("dd", dc)])
                        else:
                            tt(dd[:, dc, 0:1], rcol[:, 0:1], carry[:, dc, 0:1], ALU.subtract, [tk("bs", dc), tk("carry", dc)],
                               [tk("dd", dc)])
                        tt(dd[:, dc, 1:8], rcol[:, 1:8], rcol[:, 0:7], ALU.subtract, [tk("bs", dc)], [tk("dd", dc)])
                        act(Ac[:, dc, :], dd[:, dc, :], AF.Exp, [tk("dd", dc)], [tk("Ac", dc)], scale=esc)
                        cp(carry[:, dc, 0:1], rcol[:, 7:8], [tk("bs", dc)], [tk("carry", dc)])
                        bp3 = bp.rearrange("p (c k) -> p c k", k=64)
                        tt(bp3, bs3, bs3[:, :, 63:64].to_broadcast([128, 8, 64]), ALU.subtract, [tk("bs", dc)], [tk("bp")])
                        act(ex[0], bp, AF.Exp, [tk("bp")], [tk("ex", 0)], scale=esc)
                        tt(qt_b[dc], qT[dc], ex[0], ALU.mult, [tk("qT", dc, sx), tk("ex", 0)], [tk("qt_b", dc)])
                        if hg:
                            act(ex[1], bp, AF.Exp, [tk("bp"), tc_], [tk("ex", 1)], scale=-esc, bias=lnoml[:, col:col + 1])
                        else:
                            act(ex[1], bp, AF.Exp, [tk("bp")], [tk("ex", 1)], scale=-esc)
                        tt(kt_b[dc], kT[dc], ex[1], ALU.mult, [tk("kT", dc, sx), tk("ex", 1)], [tk("kt_b", dc)])
                        act(ex[0], bs[dc], AF.Exp, [tk("bs", dc)], [tk("ex", 0)], scale=esc)
                        tt(qs_b[par][dc], qT[dc], ex[0], ALU.mult, [tk("qT", dc, sx), tk("ex", 0)], [tk("qs_b", par, dc)])
                        dma("sp", QS_d[:, col, bsl], qs_b[par][dc], [tk("qs_b", par, dc)], [tk("QS", col, b)], f"qs{par}{dc}")
                        if b == NB - 1:
                            act(atot[:, dc, :], rcol[:, 7:8], AF.Exp, [tk("bs", dc)], [tk("atot")], scale=esc)
                        if CUT <= 2:
                            continue
                        for j in range(4):
                            transpose(ptb[:, j, :], kt_b[dc][:, j * 128:(j + 1) * 128], ident_b, [tk("kt_b", dc), tc_], [tb[2]])
                        cp(ktm2[0][0:64, :, dc * 128:(dc + 1) * 128], ptb[0:64, 0:4, :], [tb[2]], [tk("ktm")])
                        cp(ktm2[1][64:128, :, dc * 128:(dc + 1) * 128], ptb[64:128, 0:4, :], [tb[2]], [tk("ktm")])
                    if CUT <= 3:
                        continue
                    for ec in range(NE):
                        if not hg:
                            proj(slots_h["v"][ec], lambda pj, tpj: act(vT[ec], pj, AF.Copy, [tpj], [tk("vT", ec, sx)]))
                        for j in range(4):
                            transpose(ptb[:, j, :], vT[ec][:, j * 128:(j + 1) * 128], ident_b, [tk("vT", ec, sx), tc_], [tb[2]])
                        cp(vtm[:, :, ec * 128:(ec + 1) * 128], ptb[:, 0:4, :], [tb[2]], [tk("vtm")])
                    if CUT <= 4:
                        continue
                    if b == 0:
                        S.op("dve", lambda e: e.memset(Sst, 0.0), (), [tk("Sst")])
                    qk_reads = [tk("kt_b", d_) for d_ in range(ND)] + [tk("qt_b", d_) for d_ in range(ND)]
                    for j in range(4):
                        mm_chain(sc4[:, j, :], [(kt_b[d_][:, j * 128:(j + 1) * 128], qt_b[d_][:, j * 128:(j + 1) * 128])
                                               for d_ in range(ND)], qk_reads, [tb[3]])
                    tt(sT4, sc4, mask4, ALU.mult, [tb[3], tc_], [tk("sT4")])

                    if CUT <= 5:
                        continue

                    def Ureg(c, d_):
                        if hg:
                            return banks[c // 4][:, (c % 4) * 128:(c % 4 + 1) * 128], tb[c // 4]
                        return banks[d_][:, 0:EW], tb[d_]

                    def U_mm(c):
                        j, half = c // 2, c % 2
                        rows = slice(half * 64, half * 64 + 64)
                        for d_ in range(ND):
                            ur, tu = Ureg(c, d_)
                            mm_chain(ur, [(ktm2[half][:, j, d_ * 128:(d_ + 1) * 128], vtm[:, j, :])], [tk("ktm"), tk("vtm")], [tu])

                    if hg:
                        for c in range(8):
                            U_mm(c)
                    for c in range(8):
                        j, half = c // 2, c % 2
                        rows = slice(half * 64, half * 64 + 64)
                        sb_ = Sbf[c % NSB]
                        tsb = tk("Sbf", c % NSB)
                        if not hg:
                            U_mm(c)
                        for d_ in range(ND):
                            ts(sb_[:, d_, :], Sst[:, d_, :], Ac[:, d_, c:c + 1], None, ALU.mult, None,
                               [tk("Sst"), tk("Ac", d_)], [tsb])
                        for d_ in range(ND):
                            ur, tu = Ureg(c, d_)
                            stt(Sst[:, d_, :], Sst[:, d_, :], Ac[:, d_, c:c + 1], ur, ALU.mult, ALU.add,
                                [tk("Sst"), tk("Ac", d_), tu], [tk("Sst")])
                        for ec in range(NE):
                            pairs = [(vtm[:, j, ec * 128:(ec + 1) * 128], sT4[:, j, half * 64:half * 64 + 64])]
                            pairs += [(sb_[:, d_, ec * 128:(ec + 1) * 128], qt_b[d_][:, c * 64:(c + 1) * 64]) for d_ in range(ND)]
                            mm_chain(banks[4 + ec][:, c * 64:(c + 1) * 64], pairs,
                                     [tk("vtm"), tk("sT4"), tsb] + [tk("qt_b", d_) for d_ in range(ND)], [tb[4 + ec]])
                    if CUT <= 6:
                        continue
                    oi = par % len(oloc)
                    ol, tol = oloc[oi], tk("oloc", oi)
                    for ec in range(NE):
                        cp_any(ol[:, ec, :], banks[4 + ec][:], [tb[4 + ec]], [tol], ec)
                    dma("sp", OL_d[:, hd * NE:(hd + 1) * NE, bsl], ol, [tol], [tk("OL", hd, b)], f"ol{oi}")
                    if hg and it + 1 < NHH * NB:
                        emit_EV((it + 1) % NSET)
                hq = hd % HPG
                dma("sp", cins[hd // HPG][:, hq * ND:(hq + 1) * ND, :], Sfull, [tk("Sst"), tk("atot")], [tk("ccin", hd // HPG)], "ccS")

        def stage_mix2(l):
            L = LAYERS[l]
            NH, ND, NE, esc = L["NH"], L["ND"], L["NE"], L["escale"]
            hg = L["kind"] == "hgrn"
            EW = NE * 128
            cw = EW + 1
            RC.reset()
            wsl = [RC.alloc([KC, 128], BF16) for _ in range(6)]
            W.set_slots(wsl, "p2")
            lim = stage_end[("mix2", l)]
            HPG = CCG[l]
            for g in range(ccn[l]):
                ccs = S.new_dma_sem(f"cc{l}_{g}")
                S.op("pool", lambda e, g=g: e.collective_compute("AllGather", ALU.bypass,
                                                                 replica_groups=[[0, 1, 2, 3], [4, 5, 6, 7]][:ncores // 4],
                                                                 ins=[cc_in[l][g].ap()], outs=[cc_out[l][g].ap()]),
                     [tk("ccin", g)], [tk("ccout", g)], dma_sem=ccs, incv=1)
            gath = RC.alloc([4, ND, cw], F32)
            Tst = RC.alloc([ND, EW], F32)
            tmpS = RC.alloc([ND, EW], F32)
            aeff = RC.alloc([ND, 1], F32)
            Sst_b = RC.alloc([ND, EW], BF16)
            ol = RC.alloc([NE, 512], F32)
            qs = RC.alloc([ND, 512], BF16)
            sqs = [RC.alloc([512], BF16) for _ in range(2)]
            rs = RC.alloc([512], F32)
            gT = [RC.alloc([512], BF16) for _ in range(2)]
            tmpo = RC.alloc([512], F32)
            gain = hgn if hg else glan
            couts = [c_.ap().rearrange("(r p) (u w) -> p r u w", p=128, w=cw) for c_ in cc_out[l]]
            pj_i = 0
            for hd in range(min(NH, int(os.environ.get('K_HEADS', NH)))):
                hold = W.pos
                gslots = [W.get(lim, hold) for _ in range(NE)]
                for r in range(3):
                    hq = hd % HPG
                    dma("sp", gath[:, r], couts[hd // HPG][:, r, hq * ND:(hq + 1) * ND, :], [tk("ccout", hd // HPG)], [tk("gath")], "gath")
                for dc in range(ND):
                    ts(Tst[:, dc, :], gath[:, 0, dc, 0:EW], pm[:, 0:1], None, ALU.mult, None, [tk("gath"), tc_], [tk("Tst")])
                    for r in (1, 2):
                        ts(aeff[:, dc, :], gath[:, r, dc, EW:EW + 1], pm[:, r:r + 1], pm[:, 4 + r:5 + r], ALU.mult, ALU.add,
                           [tk("gath"), tc_], [tk("aeff")])
                        ts(tmpS[:, dc, :], gath[:, r, dc, 0:EW], pm[:, r:r + 1], None, ALU.mult, None, [tk("gath"), tc_],
                           [tk("tmpS")])
                        stt(Tst[:, dc, :], Tst[:, dc, :], aeff[:, dc, :], tmpS[:, dc, :], ALU.mult, ALU.add,
                            [tk("Tst"), tk("aeff"), tk("tmpS")], [tk("Tst")])
                cp(Sst_b, Tst, [tk("Tst")], [tk("Sst_b")])
                for b in range(NB):
                    bsl = slice(b * BT, (b + 1) * BT)
                    dma("sp", ol, OL_d[:, hd * NE:(hd + 1) * NE, bsl], [tk("OL", hd, b)], [tk("ol2")], "ol2")
                    dma("sp", qs, QS_d[:, hd * ND:(hd + 1) * ND, bsl], [tk("QS", hd * ND + dc, b) for dc in range(ND)], [tk("qs2")], "qs2")
                    for ec in range(NE):
                        mm_chain(banks[4 + ec][:], [(Sst_b[:, dc, ec * 128:(ec + 1) * 128], qs[:, dc, :]) for dc in range(ND)],
                                 [tk("Sst_b"), tk("qs2")], [tb[4 + ec]])
                        tt(ol[:, ec, :], ol[:, ec, :], banks[4 + ec][:], ALU.add, [tk("ol2"), tb[4 + ec]], [tk("ol2")])
                        q = sqs[ec % 2]
                        act(q, ol[:, ec, :], AF.Square, [tk("ol2")], [tk("sqs", ec % 2)])
                        mm_chain(banks[3][:], [(ones_b, q)], [tk("sqs", ec % 2), tc_], [tb[3]], start=(ec == 0), stop=(ec == NE - 1))
                    rstd_from_ssq(banks[3][:], EW, rs, tk("rs"), [tb[3], tc_])
                    for ec in range(NE):
                        slot, tslot = gslots[ec]
                        bi = pj_i % 2
                        pj_i += 1
                        pj, tpj = banks[bi][:], tb[bi]
                        mm_chain(pj, [(slot[:, kc, :], hnT[:, kc, bsl]) for kc in range(KC)], [tslot, tk("hn", b)], [tpj])
                        g_ = gT[ec % 2]
                        act(g_, pj, AF.Silu, [tpj], [tk("gT", ec % 2)])
                        stt(tmpo, ol[:, ec, :], gain[:, ec:ec + 1], rs, ALU.mult, ALU.mult, [tk("ol2"), tk("rs"), tc_], [tk("tmpo")])
                        tt(onT[:, hd * NE + ec, bsl], tmpo, g_, ALU.mult, [tk("tmpo"), tk("gT", ec % 2)], [tk("on", b)])

        def stage_out(l):
            RC.reset()
            wsl = [RC.alloc([KC, 128], BF16) for _ in range(5)]
            W.set_slots(wsl, "p3")
            lim = stage_end[("out", l)]
            mT = RC.alloc([KC, 512], F32)
            sqs = [RC.alloc([512], BF16) for _ in range(2)]
            mark = RC.off
            i = 0
            for b in range(NB):
                bsl = slice(b * BT, (b + 1) * BT)
                for n in range(KC):
                    slot, tslot = W.get(lim)
                    bi = i % 2
                    pj, tpj = banks[bi][:], tb[bi]
                    mm_chain(pj, [(slot[:, hd, :], onT[:, hd, bsl]) for hd in range(KC)], [tslot, tk("on", b)], [tpj])
                    evac_with_ssq(mT, sqs, n, pj, tpj, n == 0, n == KC - 1, i)
                    i += 1
                RC.off = mark
                if int(os.environ.get('K_OUT', 9)) >= 2:
                    post_block(mT, b, 4 * l + 1, 4 * l + 2, False)

        def stage_mlp(l, last_layer):
            RC.reset()
            wsl = [RC.alloc([KC, 128], BF16) for _ in range(4)]
            W.set_slots(wsl, "p4")
            lim = stage_end[("mlp", l)]
            mT = RC.alloc([KC, 512], F32)
            sqs = [RC.alloc([512], BF16) for _ in range(2)]
            rl = [RC.alloc([512], F32) for _ in range(2)]
            mark = RC.off
            cnt = [0]

            def up(b, f0=0, f1=64):
                bsl = slice(b * BT, (b + 1) * BT)
                for fu in range(f0, f1):
                    i = cnt[0]
                    slot, tslot = W.get(lim)
                    pj, tpj = banks[i % 2][:], tb[i % 2]
                    mm_chain(pj, [(slot[:, kc, :], hnT[:, kc, bsl]) for kc in range(KC)], [tslot, tk("hn", b)], [tpj])
                    r_ = rl[i % 2]
                    act(r_, pj, AF.Relu, [tpj], [tk("rl", i % 2)])
                    tt(uT[:, fu, :], r_, r_, ALU.mult, [tk("rl", i % 2)], [tk("u", fu)])
                    cnt[0] += 1

            def down(b):
                for n in range(KC):
                    i = cnt[0]
                    pj, tpj = banks[i % 2][:], tb[i % 2]
                    for g in range(4):
                        slot, tslot = W.get(lim)
                        mm_chain(pj, [(slot[:, j, :], uT[:, g * 16 + j, :]) for j in range(16)],
                                 [tslot] + [tk("u", g * 16 + j) for j in range(16)], [tpj], start=(g == 0), stop=(g == 3))
                    evac_with_ssq(mT, sqs, n, pj, tpj, n == 0, n == KC - 1, i)
                    cnt[0] += 1

            up(0)
            for b in range(NB):
                down(b)
                RC.off = mark
                post_block(mT, b, 4 * l + 3, None if last_layer else 4 * (l + 1), last_layer, defer=(up if b + 1 < NB else None), defer_arg=b + 1)

        def run_stages():
            stage_pre()
            S.barrier()
            if dbg == "pre":
                dump(lambda kc: hnT[:, kc, :], KC, True)
                return
            for l in range(n_layers):
                stage_mix1(l)
                S.barrier()
                if os.environ.get('K_STOP') == 'mix1':
                    dump(lambda kc: hnT[:, kc, :], KC, True)
                    return
                stage_mix2(l)
                S.barrier()
                if dbg == f"mix{l}":
                    dump(lambda kc: onT[:, kc, :], KC, True)
                    return
                stage_out(l)
                S.barrier()
                if dbg == f"out{l}":
                    dump(lambda kc: hnT[:, kc, :], KC, True)
                    return
                stage_mlp(l, l == n_layers - 1)
                S.barrier()

        run_stages()
        S.barrier()
        with nc.Block() as block:
            S.emit(block)
    return nc


def _tile_w(Wm, ncols):
    K, N = Wm.shape
    return np.ascontiguousarray(Wm.reshape(K // 128, 128, N // ncols, ncols).transpose(2, 1, 0, 3))


def _pcol(v):
    return np.ascontiguousarray(v.reshape(-1, 128).T)


_CACHE = {}


def prepare_inputs(inputs):
    f = lambda k: np.asarray(inputs[k], dtype=np.float32)
    nwl = []
    for l in range(2):
        for k in ("norm_mix_pre", "norm_mix_post", "norm_mlp_pre", "norm_mlp_post"):
            nwl.append(_pcol(f(k)[l]))
    nw = np.ascontiguousarray(np.stack(nwl, 1))
    lbl = np.ascontiguousarray(np.stack([_pcol(f("hgrn_lb_logits")[r]) for r in range(3)], 1))
    gw = f("gla_w_in")[0]
    shared = dict(
        nw=nw, lbl=lbl,
        hgn=np.ascontiguousarray(f("hgrn_norm")[0].reshape(128, 1)),
        glan=_pcol(f("gla_norm")[0]),
        bgk=_pcol(f("gla_b_gk")[0]),
        wgk=np.ascontiguousarray(f("gla_w_gk")[0]),
        wr=np.ascontiguousarray(gw[:, 6144:6160].reshape(16, 128, 16).transpose(1, 0, 2)),
        hg_in=_tile_w(f("hgrn_w_in")[0], 128),
        hg_out=_tile_w(f("hgrn_w_out")[0], 128),
        gl_in=_tile_w(np.ascontiguousarray(gw[:, :6144]), 128),
        gl_out=_tile_w(f("gla_w_out")[0], 128),
        w_up=np.stack([_tile_w(f("mlp_w_up")[l], 128) for l in range(2)]),
        w_dn=np.stack([_tile_w(f("mlp_w_down")[l], 128) for l in range(2)]),
    )
    x = f("x")
    in_maps = []
    for c in range(NCORES):
        b, p = c // 4, c % 4
        pmv = np.zeros((128, 8), np.float32)
        for r in range(4):
            pmv[:, r] = 1.0 if r < p else 0.0
            pmv[:, 4 + r] = 0.0 if r < p else 1.0
        d = dict(shared)
        d["x"] = np.ascontiguousarray(x[b, p * TC:(p + 1) * TC, :])
        d["pm"] = pmv
        in_maps.append(d)
    return in_maps


def kernel(**inputs):
    in_maps = prepare_inputs(inputs)
    if "nc" not in _CACHE:
        _CACHE["nc"] = build_program()
    res = run_bass_kernel_spmd(_CACHE["nc"], in_maps, core_ids=list(range(NCORES)))
    x = inputs["x"]
    out = np.empty(x.shape, np.float32)
    for c in range(NCORES):
        b, p = c // 4, c % 4
        out[b, p * TC:(p + 1) * TC, :] = res.results[c]["out"]
    return out
```

```python
from contextlib import ExitStack
import os
import math
import numpy as np
import concourse.bass as bass
import concourse.mybir as mybir
from concourse.bass_utils import run_bass_kernel_spmd

F32 = mybir.dt.float32
BF16 = mybir.dt.bfloat16
AF = mybir.ActivationFunctionType
ALU = mybir.AluOpType

NCORES = 8
D = 2048
KC = 16
TC = 2048
NB = 4
BT = 512
DFF = 8192
EPS = 1e-6
ENGINES = ("pe", "act", "dve", "pool", "sp")


class Sem:
    __slots__ = ("h", "v")

    def __init__(self, h):
        self.h = h
        self.v = 0


class Tok:
    __slots__ = ("w", "r")

    def __init__(self):
        self.w = None
        self.r = {}


class Sched:
    def __init__(self, sem_alloc):
        self.sem_alloc = sem_alloc
        self.q = {e: [] for e in ENGINES}
        self.esem = {e: Sem(sem_alloc("c_" + e)) for e in ENGINES}
        self.waited = {e: {} for e in ENGINES}
        self.allsems = list(self.esem.values())
        self.toks = {}

    def tk(self, *key):
        t = self.toks.get(key)
        if t is None:
            t = self.toks[key] = Tok()
        return t

    def new_dma_sem(self, name):
        s = Sem(self.sem_alloc(name))
        self.allsems.append(s)
        return s

    def _waits(self, eng, deps, skip_own):
        own = self.esem[eng]
        waits = []
        wd = self.waited[eng]
        for s, v in deps.items():
            if s is own and (skip_own or v > own.v):
                continue
            if wd.get(s, 0) < v:
                waits.append((s, v))
                wd[s] = v
        return waits

    def op(self, eng, fn, reads=(), writes=(), dma_sem=None, inc=True, incv=None, record=True,
           skip_own=None):
        if skip_own is None:
            skip_own = (eng == "pe")
        deps = {}
        for t in reads:
            if t.w is not None and deps.get(t.w[0], 0) < t.w[1]:
                deps[t.w[0]] = t.w[1]
        for t in writes:
            if t.w is not None and deps.get(t.w[0], 0) < t.w[1]:
                deps[t.w[0]] = t.w[1]
            for s, v in t.r.items():
                if deps.get(s, 0) < v:
                    deps[s] = v
        waits = self._waits(eng, deps, skip_own)
        if dma_sem is not None:
            csem, iv = dma_sem, (16 if incv is None else incv)
        else:
            csem, iv = self.esem[eng], 1
        if inc:
            csem.v += iv
            val = csem.v
        else:
            val = csem.v + iv
        self.q[eng].append((waits, fn, csem if inc else None, iv))
        if record:
            self.record(reads, writes, (csem, val))
        return (csem, val)

    def record(self, reads, writes, sv):
        csem, val = sv
        for t in reads:
            if t.r.get(csem, 0) < val:
                t.r[csem] = val
        for t in writes:
            t.w = (csem, val)
            t.r = {}

    def barrier(self):
        svs = [(s, s.v) for s in self.allsems if s.v > 0]
        for e in ENGINES:
            waits = self._waits(e, dict(svs), False)
            if waits:
                self.q[e].append((waits, None, None, 0))
        for e in ENGINES:
            if self.esem[e].v > 12000:
                s = Sem(self.sem_alloc("c_" + e))
                self.esem[e] = s
                self.allsems.append(s)

    def emit(self, block):
        def run(eh, items):
            for waits, fn, csem, iv in items:
                for s, v in waits:
                    eh.wait_ge(s.h, v)
                if fn is None:
                    continue
                ins = fn(eh)
                if csem is not None:
                    ins.then_inc(csem.h, iv)

        m = {"pe": block.tensor, "act": block.scalar, "dve": block.vector,
             "pool": block.gpsimd, "sp": block.sync}
        for e in ENGINES:
            items = self.q[e]
            if items:
                m[e](lambda eh, items=items: run(eh, items))


class Region:
    def __init__(self, t, nwords):
        self.t, self.n, self.off = t, nwords, 0

    def reset(self):
        self.off = 0

    def alloc(self, free_shape, dt):
        nel = int(np.prod(free_shape))
        nbytes = nel * (4 if dt == F32 else 2)
        words = (nbytes + 31) // 32 * 8
        assert self.off + words <= self.n, ("region overflow", self.off, words, self.n)
        v = self.t[:, self.off:self.off + words]
        self.off += words
        if dt != F32:
            v = v.bitcast(dt)
        v = v[:, 0:nel]
        if len(free_shape) == 2:
            v = v.rearrange("p (a b) -> p a b", b=free_shape[1])
        elif len(free_shape) == 3:
            v = v.rearrange("p (a b c) -> p a b c", b=free_shape[1], c=free_shape[2])
        return v


LAYERS = [
    dict(kind="hgrn", NH=16, ND=1, NE=1, escale=1.0),
    dict(kind="gla", NH=4, ND=2, NE=4, escale=1.0 / 16.0),
]


def build_program(n_layers=2, dbg=None, ncores=NCORES):
    nc = bass.Bass("TRN2", target_bir_lowering=False)
    dt_in = lambda name, shape: nc.dram_tensor(name, shape, F32, kind="ExternalInput").ap()
    x_d = dt_in("x", [TC, D])
    pm_d = dt_in("pm", [128, 8])
    nw_d = dt_in("nw", [128, 8, KC])
    lbl_d = dt_in("lbl", [128, 3, KC])
    hgn_d = dt_in("hgn", [128, 1])
    glan_d = dt_in("glan", [128, 4])
    bgk_d = dt_in("bgk", [128, 8])
    wgk_d = dt_in("wgk", [16, 1024])
    wr_d = dt_in("wr", [128, KC, 16])
    hg_in_d = dt_in("hg_in", [64, 128, KC, 128])
    hg_out_d = dt_in("hg_out", [16, 128, KC, 128])
    gl_in_d = dt_in("gl_in", [48, 128, KC, 128]) if n_layers > 1 else None
    gl_out_d = dt_in("gl_out", [16, 128, KC, 128]) if n_layers > 1 else None
    up_d = dt_in("w_up", [n_layers, 64, 128, KC, 128])
    dn_d = dt_in("w_dn", [n_layers, 16, 128, 64, 128])
    out_d = nc.dram_tensor("out", [TC, D], F32, kind="ExternalOutput").ap()
    dbg_d = nc.dram_tensor("dbg", [128, KC, TC], F32, kind="ExternalOutput").ap() if dbg else None
    H_d = nc.dram_tensor("Hs", [128, KC, TC], F32).ap()
    OL_d = nc.dram_tensor("OLs", [128, KC, TC], F32).ap()
    QS_d = nc.dram_tensor("QSs", [128, KC, TC], BF16).ap()
    CCG = {0: 8, 1: 1}
    ccw = {0: 8 * 129, 1: 2 * 513}
    ccn = {0: 2, 1: 4}
    cc_in = {l: [nc.dram_tensor(f"cc_in{l}_{g}", [128, ccw[l]], F32) for g in range(ccn[l])] for l in range(2)}
    cc_out = {l: [nc.dram_tensor(f"cc_out{l}_{g}", [4 * 128, ccw[l]], F32) for g in range(ccn[l])] for l in range(2)}

    with ExitStack() as es:
        sb = lambda name, shape, dt: es.enter_context(nc.sbuf_tensor(name, shape, dt))
        S = Sched(lambda name: es.enter_context(nc.semaphore(name)))
        tk = S.tk

        RA = sb("RA", [128, KC, TC], BF16)
        RB = sb("RB", [128, KC * TC // 2], F32)
        NCW = 18176
        RCt = sb("RC", [128, NCW], F32)
        RC = Region(RCt, NCW)
        cst = sb("cst", [128, 1792], F32)
        CR = Region(cst, 1792)
        banks = [es.enter_context(nc.psum_tensor(f"bank{i}", [128, 512], F32)) for i in range(8)]
        tb = [tk("bank", i) for i in range(8)]

        RBb = RB[:].bitcast(BF16)
        onT = RBb.rearrange("p (a t) -> p a t", t=TC)
        uT = RBb.rearrange("p (a t) -> p a t", t=BT)
        slots1 = RBb.rearrange("p (s a c) -> p s a c", a=KC, c=128)
        hnT = RA

        def act(out, in_, func, reads, writes, **kw):
            S.op("act", lambda e: e.activation(out=out, in_=in_, func=func, **kw), reads, writes)

        def tt(out, in0, in1, op, reads, writes, eng="dve"):
            S.op(eng, lambda e: e.tensor_tensor(out=out, in0=in0, in1=in1, op=op), reads, writes)

        def ts(out, in0, s1, s2, op0, op1, reads, writes, eng="dve"):
            if s2 is None:
                S.op(eng, lambda e: e.tensor_scalar(out=out, in0=in0, scalar1=s1, scalar2=None, op0=op0), reads, writes)
            else:
                S.op(eng, lambda e: e.tensor_scalar(out=out, in0=in0, scalar1=s1, scalar2=s2, op0=op0, op1=op1), reads, writes)

        def stt(out, in0, scalar, in1, op0, op1, reads, writes):
            S.op("dve", lambda e: e.scalar_tensor_tensor(out=out, in0=in0, scalar=scalar, in1=in1, op0=op0, op1=op1),
                 reads, writes)

        def cp(out, in_, reads, writes, eng="dve"):
            S.op(eng, lambda e: e.tensor_copy(out=out, in_=in_), reads, writes)

        def mm_chain(out, pairs, reads, writes, start=True, stop=True):
            n = len(pairs)
            for i, (l, r) in enumerate(pairs):
                st = start and i == 0
                sp_ = stop and i == n - 1
                fn = (lambda e, l=l, r=r, st=st, sp_=sp_: e.matmul(out, lhsT=l, rhs=r, start=st, stop=sp_))
                if n == 1:
                    S.op("pe", fn, reads, writes)
                elif i == 0:
                    S.op("pe", fn, reads, writes, inc=False, record=False)
                elif i < n - 1:
                    S.op("pe", fn, inc=False, record=False)
                else:
                    sv = S.op("pe", fn, record=False)
                    S.record(reads, writes, sv)

        def transpose(out, in_, ident, reads, writes):
            S.op("pe", lambda e: e.transpose(out=out, in_=in_, identity=ident), reads, writes)

        dsem = {}

        def dma(eng, out, in_, reads, writes, semname):
            s = dsem.get(semname)
            if s is None:
                s = dsem[semname] = S.new_dma_sem("d_" + semname)
            return S.op(eng, lambda e: e.dma_start(out=out, in_=in_), reads, writes, dma_sem=s)

        ident_f = CR.alloc([128], F32)
        ident_b = CR.alloc([128], BF16)
        ones_b = CR.alloc([128], BF16)
        mask4 = CR.alloc([4, 128], F32)
        pm = CR.alloc([8], F32)
        nw = CR.alloc([8, KC], F32)
        lbt = CR.alloc([3, KC], F32)
        lb = CR.alloc([KC], F32)
        oml = CR.alloc([KC], F32)
        lnoml = CR.alloc([KC], F32)
        hgn = CR.alloc([1], F32)
        glan = CR.alloc([4], F32)
        bgk = CR.alloc([8], F32)
        epsc = CR.alloc([1], F32)
        wgk_b = CR.alloc([1024], BF16)
        wr_b = CR.alloc([KC, 16], BF16)
        tmpc = CR.alloc([KC], F32)
        tc_ = tk("consts")

        S.op("pool", lambda e: e.memset(ident_f, 1.0), (), [tc_])
        S.op("pool", lambda e: e.affine_select(out=ident_f, in_=ident_f, pattern=[[-1, 128]], compare_op=ALU.is_equal,
                                               fill=0.0, base=0, channel_multiplier=1), [tc_], [tc_])
        S.op("pool", lambda e: e.tensor_copy(out=ident_b, in_=ident_f), [tc_], [tc_])
        S.op("pool", lambda e: e.memset(ones_b, 1.0), (), [tc_])
        S.op("pool", lambda e: e.memset(epsc, EPS), (), [tc_])
        m0 = mask4[:, 0, :]
        S.op("pool", lambda e: e.memset(m0, 1.0), (), [tc_])
        S.op("pool", lambda e: e.affine_select(out=m0, in_=m0, pattern=[[1, 128]], compare_op=ALU.is_ge,
                                               fill=0.0, base=0, channel_multiplier=-1), [tc_], [tc_])
        S.op("pool", lambda e: e.memset(mask4[0:64, 0, 64:128], 0.0), [tc_], [tc_])
        for i in range(1, 4):
            S.op("pool", lambda e, i=i: e.tensor_copy(out=mask4[:, i, :], in_=mask4[:, 0, :]), [tc_], [tc_])
        for dst, src in ((pm, pm_d), (nw, nw_d), (lbt, lbl_d), (hgn, hgn_d), (glan, glan_d), (bgk, bgk_d)):
            dma("sp", dst, src, (), [tc_], "const")
        dma("pool", wgk_b[0:16, :], wgk_d, (), [tc_], "const2")
        dma("pool", wr_b, wr_d, (), [tc_], "const2")
        act(lbt, lbt, AF.Exp, [tc_], [tc_])
        tt(tmpc, lbt[:, 0, :], lbt[:, 1, :], ALU.add, [tc_], [tc_])
        tt(tmpc, tmpc, lbt[:, 2, :], ALU.add, [tc_], [tc_])
        S.op("dve", lambda e: e.reciprocal(out=tmpc, in_=tmpc), [tc_], [tc_])
        tt(lb, lbt[:, 0, :], tmpc, ALU.mult, [tc_], [tc_])
        ts(oml, lb, -1.0, 1.0, ALU.mult, ALU.add, [tc_], [tc_])
        act(lnoml, oml, AF.Ln, [tc_], [tc_])
        S.barrier()

        class WStream:
            def __init__(self):
                self.plan = []
                self.pos = 0
                self.issued = 0
                self.slots = None
                self.base = 0

            def set_slots(self, slot_aps, name):
                self.slots = slot_aps
                self.name = name
                self.base = self.pos
                self.issued = self.pos

            def get(self, limit, hold=None):
                i = self.pos
                self.pos += 1
                if hold is None:
                    hold = i
                n = min(hold + len(self.slots), limit, len(self.plan))
                while self.issued < n:
                    j = self.issued
                    sj = (j - self.base) % len(self.slots)
                    dma("pool", self.slots[sj], self.plan[j], (), [tk("slot", self.name, sj)], f"w{self.name}{sj}")
                    self.issued += 1
                si = (i - self.base) % len(self.slots)
                return self.slots[si], tk("slot", self.name, si)

        W = WStream()
        stage_end = {}

        def plan_units():
            for l in range(n_layers):
                L = LAYERS[l]
                hg = L["kind"] == "hgrn"
                for hd in range(L["NH"]):
                    if hg:
                        W.plan += [hg_in_d[hd], hg_in_d[16 + hd], hg_in_d[32 + hd]]
                    else:
                        W.plan += [gl_in_d[2 * hd + i] for i in range(2)] + [gl_in_d[8 + 2 * hd + i] for i in range(2)] \
                            + [gl_in_d[16 + 4 * hd + i] for i in range(4)]
                stage_end[("mix1", l)] = len(W.plan)
                for hd in range(L["NH"]):
                    W.plan += [(hg_in_d[48 + hd] if hg else gl_in_d[32 + 4 * hd + ec]) for ec in range(L["NE"])]
                stage_end[("mix2", l)] = len(W.plan)
                wd = hg_out_d if l == 0 else gl_out_d
                for b in range(NB):
                    W.plan += [wd[n] for n in range(KC)]
                stage_end[("out", l)] = len(W.plan)
                for b in range(NB):
                    W.plan += [up_d[l, fu] for fu in range(64)]
                    W.plan += [dn_d[l, n, :, g * 16:(g + 1) * 16, :] for n in range(KC) for g in range(4)]
                stage_end[("mlp", l)] = len(W.plan)

        plan_units()

        def rstd_from_ssq(ssq_bank, n_feat, rs, rs_t, reads):
            act(rs, ssq_bank, AF.Sqrt, reads, [rs_t], scale=1.0 / n_feat, bias=epsc)
            S.op("dve", lambda e: e.reciprocal(out=rs, in_=rs), [rs_t], [rs_t])

        def cp_any(out, in_, reads, writes, i):
            if i % 2 == 0:
                cp(out, in_, reads, writes)
            else:
                act(out, in_, AF.Copy, reads, writes)

        def finish_block(mT, b, layer_next, last):
            tm = [tk("m", kc) for kc in range(KC)]
            bsl = slice(b * BT, (b + 1) * BT)
            if last:
                ostg = [RC.alloc([512], F32) for _ in range(2)]
                i = 0
                pt = banks[2][:].rearrange("p (a c) -> p a c", c=128)
                for tt_ in range(4):
                    for g in range(4):
                        for j in range(4):
                            kc = g * 4 + j
                            transpose(pt[:, j, :], mT[:, kc, tt_ * 128:(tt_ + 1) * 128], ident_f, [tm[kc], tc_], [tb[2]])
                        o = ostg[i % 2]
                        cp_any(o, banks[2][:], [tb[2]], [tk("ostg", i % 2)], i)
                        r0 = b * BT + tt_ * 128
                        dma("sp", out_d[r0:r0 + 128, g * 512:(g + 1) * 512], o, [tk("ostg", i % 2)], (), f"ostg{i % 2}")
                        i += 1
                return
            for g in range(4):
                dma("sp", H_d[:, g * 4:(g + 1) * 4, bsl], mT[:, g * 4:(g + 1) * 4, :], [tm[g * 4 + j] for j in range(4)],
                    [tk("H", g, b)], f"mst{g}")
            if layer_next is None:
                return
            sqs = [RC.alloc([512], BF16) for _ in range(2)]
            for kc in range(KC):
                q = sqs[kc % 2]
                act(q, mT[:, kc, :], AF.Square, [tm[kc]], [tk("sqs", kc % 2)])
                mm_chain(banks[3][:], [(ones_b, q)], [tk("sqs", kc % 2), tc_], [tb[3]], start=(kc == 0), stop=(kc == KC - 1))
            rs = RC.alloc([512], F32)
            rstd_from_ssq(banks[3][:], D, rs, tk("rs"), [tb[3], tc_])
            for kc in range(KC):
                stt(hnT[:, kc, bsl], mT[:, kc, :], nw[:, layer_next, kc:kc + 1], rs, ALU.mult, ALU.mult,
                    [tm[kc], tk("rs"), tc_], [tk("hn", b)])

        def post_block(mT, b, wpost_idx, layer_next, last, defer=None, defer_arg=None):
            tm = [tk("m", kc) for kc in range(KC)]
            bsl = slice(b * BT, (b + 1) * BT)
            rsm = RC.alloc([512], F32)
            rstd_from_ssq(banks[3][:], D, rsm, tk("rsm"), [tb[3], tc_])
            hch = [RC.alloc([2, 512], F32) for _ in range(2)]

            for i, kc in enumerate(range(0, KC, 2)):
                hc = hch[i % 2]
                th = tk("hch", i % 2)
                dma("sp", hc, H_d[:, kc:kc + 2, bsl], [tk("H", kc // 4, b)], [th], f"hch{i % 2}")
                for j in range(2):
                    stt(mT[:, kc + j, :], mT[:, kc + j, :], nw[:, wpost_idx, kc + j:kc + j + 1], rsm, ALU.mult, ALU.mult,
                        [tm[kc + j], tk("rsm"), tc_], [tm[kc + j]])
                    tt(mT[:, kc + j, :], mT[:, kc + j, :], hc[:, j, :], ALU.add, [tm[kc + j], th], [tm[kc + j]])
            if defer is not None:
                defer(defer_arg, 0, 32)
            if int(os.environ.get('K_OUT', 9)) >= 3:
                finish_block(mT, b, layer_next, last)
            if defer is not None:
                defer(defer_arg, 32, 64)

        def evac_with_ssq(mT, sqs, kc, pj, tpj, first, lastc, i):
            q = sqs[i % 2]
            cp(mT[:, kc, :], pj, [tpj], [tk("m", kc)])
            act(q, mT[:, kc, :], AF.Square, [tk("m", kc)], [tk("sqs", i % 2)])
            mm_chain(banks[3][:], [(ones_b, q)], [tk("sqs", i % 2), tc_], [tb[3]], start=first, stop=lastc)

        def dump(src_fn, n, dt_is_bf16):
            RC.reset()
            stg = RC.alloc([TC], F32)
            for kc in range(n):
                cp(stg, src_fn(kc), [], [tk("dstg")])
                dma("sp", dbg_d[:, kc, :], stg, [tk("dstg")], (), "dstg")

        def stage_pre():
            RC.reset()
            mT = RC.alloc([KC, 512], F32)
            xts = [RC.alloc([D], F32) for _ in range(2)]
            pt = banks[2][:].rearrange("p (a c) -> p a c", c=128)
            for b in range(NB):
                for tt_ in range(4):
                    ti = b * 4 + tt_
                    xt = xts[ti % 2]
                    txt = tk("xt", ti % 2)
                    dma("sp", xt, x_d[ti * 128:(ti + 1) * 128, :], (), [txt], f"xt{ti % 2}")
                    for g in range(4):
                        for j in range(4):
                            kc = g * 4 + j
                            transpose(pt[:, j, :], xt[:, kc * 128:(kc + 1) * 128], ident_f, [txt, tc_], [tb[2]])
                        tms = [tk("m", g * 4 + j) for j in range(4)]
                        cp_any(mT[:, g * 4:(g + 1) * 4, tt_ * 128:(tt_ + 1) * 128], pt, [tb[2]], tms, g)
                finish_block(mT, b, 0, False)

        def stage_mix1(l):
            L = LAYERS[l]
            NH, ND, NE, esc = L["NH"], L["ND"], L["NE"], L["escale"]
            hg = L["kind"] == "hgrn"
            EW = NE * 128
            RC.reset()
            W.set_slots([slots1[:, i] for i in range(16)], "p1")
            lim = stage_end[("mix1", l)]
            T = lambda dt=F32: RC.alloc([512], dt)
            NSET = 2 if hg else 1
            qTs = [[T() for _ in range(ND)] for _ in range(NSET)]
            kTs = [[T() for _ in range(ND)] for _ in range(NSET)]
            lfs = [[T() for _ in range(ND)] for _ in range(NSET)]
            bs = [T() for _ in range(ND)]
            bp = T()
            ex = [T() for _ in range(2)]
            qt_b = [T(BF16) for _ in range(ND)]
            kt_b = [T(BF16) for _ in range(ND)]
            qs_b = [[T(BF16) for _ in range(ND)] for _ in range(2)]
            vTs = [[T(BF16) for _ in range(NE)] for _ in range(NSET)]
            vtm = RC.alloc([4, EW], BF16)
            ktm2 = [RC.alloc([4, ND * 128], BF16) for _ in range(2)]
            sT4 = RC.alloc([4, 128], BF16)
            Sfull = RC.alloc([ND, EW + 1], F32)
            Sst = Sfull[:, :, 0:EW]
            atot = Sfull[:, :, EW:EW + 1]
            NSB = 8 if hg else 4
            Sbf = [RC.alloc([ND, EW], BF16) for _ in range(NSB)]
            oloc = [RC.alloc([NE, 512], F32) for _ in range(2 if hg else 1)]
            carry = RC.alloc([ND, 2], F32)
            dd = RC.alloc([ND, 8], F32)
            Ac = RC.alloc([ND, 8], F32)
            rT = None if hg else RC.alloc([TC], BF16)
            ones_f = RC.alloc([512], F32)
            S.op("dve", lambda e: e.memset(ones_f, 1.0), (), [tk("ones_f")])
            S.op("dve", lambda e: e.memset(ktm2[0][64:128], 0.0), (), [tk("ktm")])
            S.op("dve", lambda e: e.memset(ktm2[1][0:64], 0.0), (), [tk("ktm")])
            pj_i = [0]
            hn_rhs = lambda kc, b: hnT[:, kc, b * BT:(b + 1) * BT]
            ptb = banks[2][:].bitcast(BF16).rearrange("p (a c) -> p a c", c=128)
            sc4 = banks[3][:].rearrange("p (a c) -> p a c", c=128)
            cw = EW + 1
            HPG = CCG[l]
            cins = [c_.ap().rearrange("p (u w) -> p u w", w=cw) for c_ in cc_in[l]]

            if not hg:
                for b in range(NB):
                    pj, tpj = banks[b % 2][0:16, :], tb[b % 2]
                    mm_chain(pj, [(wr_b[:, kc, :], hn_rhs(kc, b)) for kc in range(KC)], [tc_, tk("hn", b)], [tpj])
                    cp(rT[0:16, b * BT:(b + 1) * BT], pj, [tpj], [tk("rT")])

            NHH = min(NH, int(os.environ.get('K_HEADS', NH)))
            slots_by_head = {}

            def emit_P(it):
                hd_, b_ = it // NB, it % NB
                if hd_ not in slots_by_head:
                    hold_ = W.pos
                    slots_by_head[hd_] = {key: [W.get(lim, hold_) for _ in range(n)] for key, n in (("q", ND), ("k", ND), ("v", NE))}
                for key, bank in (("q", 5), ("k", 6), ("v", 7)):
                    slot, tslot = slots_by_head[hd_][key][0]
                    mm_chain(banks[bank][:], [(slot[:, kc, :], hn_rhs(kc, b_)) for kc in range(KC)], [tslot, tk("hn", b_)], [tb[bank]])

            def emit_EV(s_):
                act(qTs[s_][0], banks[5][:], AF.Silu, [tb[5]], [tk("qT", 0, s_)])
                act(lfs[s_][0], banks[6][:], AF.Sigmoid, [tb[6]], [tk("lf", 0, s_)])
                act(kTs[s_][0], banks[6][:], AF.Sigmoid, [tb[6]], [tk("kT", 0, s_)], scale=-1.0)
                act(vTs[s_][0], banks[7][:], AF.Copy, [tb[7]], [tk("vT", 0, s_)])

            for hd in range(NHH):
                if not hg:
                    hold = W.pos
                    slots_h = {key: [W.get(lim, hold) for _ in range(n)] for key, n in (("q", ND), ("k", ND), ("v", NE))}
                for b in range(NB):
                    bsl = slice(b * BT, (b + 1) * BT)
                    par = (hd * NB + b) % 2
                    it = hd * NB + b
                    sx = it % NSET
                    qT, kT, lf, vT = qTs[sx], kTs[sx], lfs[sx], vTs[sx]
                    if hg:
                        if it == 0:
                            emit_P(0)
                            emit_EV(0)
                        if it + 1 < NHH * NB:
                            emit_P(it + 1)

                    def proj(slot_t, evac):
                        slot, tslot = slot_t
                        bi = pj_i[0] % 2
                        pj_i[0] += 1
                        pj, tpj = banks[bi][:], tb[bi]
                        mm_chain(pj, [(slot[:, kc, :], hn_rhs(kc, b)) for kc in range(KC)], [tslot, tk("hn", b)], [tpj])
                        evac(pj, tpj)

                    CUT = int(os.environ.get('K_CUT', 9))
                    for dc in range(ND):
                        col = hd * ND + dc
                        if hg:
                            act(lf[dc], lf[dc], AF.Ln, [tk("lf", dc, sx), tc_], [tk("lf", dc, sx)], scale=oml[:, col:col + 1],
                                bias=lb[:, col:col + 1])
                        else:
                            proj(slots_h["q"][dc], lambda pj, tpj: act(qT[dc], pj, AF.Copy, [tpj], [tk("qT", dc, sx)], scale=1.0 / 16.0))
                            proj(slots_h["k"][dc], lambda pj, tpj: act(kT[dc], pj, AF.Copy, [tpj], [tk("kT", dc, sx)]))
                            bi = pj_i[0] % 2
                            pj_i[0] += 1
                            pj, tpj = banks[bi][:], tb[bi]
                            c0 = hd * 256 + dc * 128
                            mm_chain(pj, [(wgk_b[0:16, c0:c0 + 128], rT[0:16, bsl])], [tc_, tk("rT")], [tpj])
                            act(lf[dc], pj, AF.Sigmoid, [tpj, tc_], [tk("lf", dc, sx)], bias=bgk[:, col:col + 1])
                            act(lf[dc], lf[dc], AF.Ln, [tk("lf", dc, sx)], [tk("lf", dc, sx)])
                        if CUT <= 1:
                            continue
                        init = 0.0 if b == 0 else carry[:, dc, 0:1]
                        S.op("dve", lambda e, dc=dc, init=init, lfd=lf[dc]: e.tensor_tensor_scan(
                            out=bs[dc], data0=ones_f, data1=lfd, initial=init, op0=ALU.mult, op1=ALU.add),
                            [tk("lf", dc, sx), tk("ones_f"), tk("carry", dc)], [tk("bs", dc)])
                        bs3 = bs[dc].rearrange("p (c k) -> p c k", k=64)
                        rcol = bs3[:, :, 63]
                        if b == 0:
                            cp(dd[:, dc, 0:1], rcol[:, 0:1], [tk("bs", dc)], [tk("dd", dc)])
                        else:
                            tt(dd[:, dc, 0:1], rcol[:, 0:1], carry[:, dc, 0:1], ALU.subtract, [tk("bs", dc), tk("carry", dc)],
                               [tk("dd", dc)])
                        tt(dd[:, dc, 1:8], rcol[:, 1:8], rcol[:, 0:7], ALU.subtract, [tk("bs", dc)], [tk("dd", dc)])
                        act(Ac[:, dc, :], dd[:, dc, :], AF.Exp, [tk("dd", dc)], [tk("Ac", dc)], scale=esc)
                        cp(carry[:, dc, 0:1], rcol[:, 7:8], [tk("bs", dc)], [tk("carry", dc)])
                        bp3 = bp.rearrange("p (c k) -> p c k", k=64)
                        tt(bp3, bs3, bs3[:, :, 63:64].to_broadcast([128, 8, 64]), ALU.subtract, [tk("bs", dc)], [tk("bp")])
                        act(ex[0], bp, AF.Exp, [tk("bp")], [tk("ex", 0)], scale=esc)
                        tt(qt_b[dc], qT[dc], ex[0], ALU.mult, [tk("qT", dc, sx), tk("ex", 0)], [tk("qt_b", dc)])
                        if hg:
                            act(ex[1], bp, AF.Exp, [tk("bp"), tc_], [tk("ex", 1)], scale=-esc, bias=lnoml[:, col:col + 1])
                        else:
                            act(ex[1], bp, AF.Exp, [tk("bp")], [tk("ex", 1)], scale=-esc)
                        tt(kt_b[dc], kT[dc], ex[1], ALU.mult, [tk("kT", dc, sx), tk("ex", 1)], [tk("kt_b", dc)])
                        act(ex[0], bs[dc], AF.Exp, [tk("bs", dc)], [tk("ex", 0)], scale=esc)
                        tt(qs_b[par][dc], qT[dc], ex[0], ALU.mult, [tk("qT", dc, sx), tk("ex", 0)], [tk("qs_b", par, dc)])
                        dma("sp", QS_d[:, col, bsl], qs_b[par][dc], [tk("qs_b", par, dc)], [tk("QS", col, b)], f"qs{par}{dc}")
                        if b == NB - 1:
                            act(atot[:, dc, :], rcol[:, 7:8], AF.Exp, [tk("bs", dc)], [tk("atot")], scale=esc)
                        if CUT <= 2:
                            continue
                        for j in range(4):
                            transpose(ptb[:, j, :], kt_b[dc][:, j * 128:(j + 1) * 128], ident_b, [tk("kt_b", dc), tc_], [tb[2]])
                        cp(ktm2[0][0:64, :, dc * 128:(dc + 1) * 128], ptb[0:64, 0:4, :], [tb[2]], [tk("ktm")])
                        cp(ktm2[1][64:128, :, dc * 128:(dc + 1) * 128], ptb[64:128, 0:4, :], [tb[2]], [tk("ktm")])
                    if CUT <= 3:
                        continue
                    for ec in range(NE):
                        if not hg:
                            proj(slots_h["v"][ec], lambda pj, tpj: act(vT[ec], pj, AF.Copy, [tpj], [tk("vT", ec, sx)]))
                        for j in range(4):
                            transpose(ptb[:, j, :], vT[ec][:, j * 128:(j + 1) * 128], ident_b, [tk("vT", ec, sx), tc_], [tb[2]])
                        cp(vtm[:, :, ec * 128:(ec + 1) * 128], ptb[:, 0:4, :], [tb[2]], [tk("vtm")])
                    if CUT <= 4:
                        continue
                    if b == 0:
                        S.op("dve", lambda e: e.memset(Sst, 0.0), (), [tk("Sst")])
                    qk_reads = [tk("kt_b", d_) for d_ in range(ND)] + [tk("qt_b", d_) for d_ in range(ND)]
                    for j in range(4):
                        mm_chain(sc4[:, j, :], [(kt_b[d_][:, j * 128:(j + 1) * 128], qt_b[d_][:, j * 128:(j + 1) * 128])
                                               for d_ in range(ND)], qk_reads, [tb[3]])
                    tt(sT4, sc4, mask4, ALU.mult, [tb[3], tc_], [tk("sT4")])

                    if CUT <= 5:
                        continue

                    def Ureg(c, d_):
                        if hg:
                            return banks[c // 4][:, (c % 4) * 128:(c % 4 + 1) * 128], tb[c // 4]
                        return banks[d_][:, 0:EW], tb[d_]

                    def U_mm(c):
                        j, half = c // 2, c % 2
                        rows = slice(half * 64, half * 64 + 64)
                        for d_ in range(ND):
                            ur, tu = Ureg(c, d_)
                            mm_chain(ur, [(ktm2[half][:, j, d_ * 128:(d_ + 1) * 128], vtm[:, j, :])], [tk("ktm"), tk("vtm")], [tu])

                    if hg:
                        for c in range(8):
                            U_mm(c)
                    for c in range(8):
                        j, half = c // 2, c % 2
                        rows = slice(half * 64, half * 64 + 64)
                        sb_ = Sbf[c % NSB]
                        tsb = tk("Sbf", c % NSB)
                        if not hg:
                            U_mm(c)
                        for d_ in range(ND):
                            ts(sb_[:, d_, :], Sst[:, d_, :], Ac[:, d_, c:c + 1], None, ALU.mult, None,
                               [tk("Sst"), tk("Ac", d_)], [tsb])
                        for d_ in range(ND):
                            ur, tu = Ureg(c, d_)
                            stt(Sst[:, d_, :], Sst[:, d_, :], Ac[:, d_, c:c + 1], ur, ALU.mult, ALU.add,
                                [tk("Sst"), tk("Ac", d_), tu], [tk("Sst")])
                        for ec in range(NE):
                            pairs = [(vtm[:, j, ec * 128:(ec + 1) * 128], sT4[:, j, half * 64:half * 64 + 64])]
                            pairs += [(sb_[:, d_, ec * 128:(ec + 1) * 128], qt_b[d_][:, c * 64:(c + 1) * 64]) for d_ in range(ND)]
                            mm_chain(banks[4 + ec][:, c * 64:(c + 1) * 64], pairs,
                                     [tk("vtm"), tk("sT4"), tsb] + [tk("qt_b", d_) for d_ in range(ND)], [tb[4 + ec]])
                    if CUT <= 6:
                        continue
                    oi = par % len(oloc)
                    ol, tol = oloc[oi], tk("oloc", oi)
                    for ec in range(NE):
                        cp_any(ol[:, ec, :], banks[4 + ec][:], [tb[4 + ec]], [tol], ec)
                    dma("sp", OL_d[:, hd * NE:(hd + 1) * NE, bsl], ol, [tol], [tk("OL", hd, b)], f"ol{oi}")
                    if hg and it + 1 < NHH * NB:
                        emit_EV((it + 1) % NSET)
                hq = hd % HPG
                dma("sp", cins[hd // HPG][:, hq * ND:(hq + 1) * ND, :], Sfull, [tk("Sst"), tk("atot")], [tk("ccin", hd // HPG)], "ccS")

        def stage_mix2(l):
            L = LAYERS[l]
            NH, ND, NE, esc = L["NH"], L["ND"], L["NE"], L["escale"]
            hg = L["kind"] == "hgrn"
            EW = NE * 128
            cw = EW + 1
            RC.reset()
            wsl = [RC.alloc([KC, 128], BF16) for _ in range(6)]
            W.set_slots(wsl, "p2")
            lim = stage_end[("mix2", l)]
            HPG = CCG[l]
            for g in range(ccn[l]):
                ccs = S.new_dma_sem(f"cc{l}_{g}")
                S.op("pool", lambda e, g=g: e.collective_compute("AllGather", ALU.bypass,
                                                                 replica_groups=[[0, 1, 2, 3], [4, 5, 6, 7]][:ncores // 4],
                                                                 ins=[cc_in[l][g].ap()], outs=[cc_out[l][g].ap()]),
                     [tk("ccin", g)], [tk("ccout", g)], dma_sem=ccs, incv=1)
            gath = RC.alloc([4, ND, cw], F32)
            Tst = RC.alloc([ND, EW], F32)
            tmpS = RC.alloc([ND, EW], F32)
            aeff = RC.alloc([ND, 1], F32)
            Sst_b = RC.alloc([ND, EW], BF16)
            ol = RC.alloc([NE, 512], F32)
            qs = RC.alloc([ND, 512], BF16)
            sqs = [RC.alloc([512], BF16) for _ in range(2)]
            rs = RC.alloc([512], F32)
            gT = [RC.alloc([512], BF16) for _ in range(2)]
            tmpo = RC.alloc([512], F32)
            gain = hgn if hg else glan
            if hg:
                hg_gT = [RC.alloc([512], BF16) for _ in range(4)]
                hg_ol = [RC.alloc([NE, 512], F32) for _ in range(2)]
                hg_qs = [RC.alloc([ND, 512], BF16) for _ in range(2)]
                hg_gbanks = [0, 1, 5, 6]
            couts = [c_.ap().rearrange("(r p) (u w) -> p r u w", p=128, w=cw) for c_ in cc_out[l]]
            pj_i = 0
            for hd in range(min(NH, int(os.environ.get('K_HEADS', NH)))):
                hold = W.pos
                gslots = [W.get(lim, hold) for _ in range(NE)]
                for r in range(3):
                    hq = hd % HPG
                    dma("sp", gath[:, r], couts[hd // HPG][:, r, hq * ND:(hq + 1) * ND, :], [tk("ccout", hd // HPG)], [tk("gath")], "gath")
                for dc in range(ND):
                    ts(Tst[:, dc, :], gath[:, 0, dc, 0:EW], pm[:, 0:1], None, ALU.mult, None, [tk("gath"), tc_], [tk("Tst")])
                    for r in (1, 2):
                        ts(aeff[:, dc, :], gath[:, r, dc, EW:EW + 1], pm[:, r:r + 1], pm[:, 4 + r:5 + r], ALU.mult, ALU.add,
                           [tk("gath"), tc_], [tk("aeff")])
                        ts(tmpS[:, dc, :], gath[:, r, dc, 0:EW], pm[:, r:r + 1], None, ALU.mult, None, [tk("gath"), tc_],
                           [tk("tmpS")])
                        stt(Tst[:, dc, :], Tst[:, dc, :], aeff[:, dc, :], tmpS[:, dc, :], ALU.mult, ALU.add,
                            [tk("Tst"), tk("aeff"), tk("tmpS")], [tk("Tst")])
                cp(Sst_b, Tst, [tk("Tst")], [tk("Sst_b")])
                if hg:
                    slot, tslot = gslots[0]
                    for b in range(NB):
                        bsl = slice(b * BT, (b + 1) * BT)
                        gb = hg_gbanks[b]
                        mm_chain(banks[gb][:], [(slot[:, kc, :], hnT[:, kc, bsl]) for kc in range(KC)], [tslot, tk("hn", b)], [tb[gb]])
                        act(hg_gT[b], banks[gb][:], AF.Silu, [tb[gb]], [tk("gT4", b)])
                    for b in range(NB):
                        bsl = slice(b * BT, (b + 1) * BT)
                        o_, q_ = hg_ol[b % 2], hg_qs[b % 2]
                        to_, tq_ = tk("ol2", b % 2), tk("qs2", b % 2)
                        dma("sp", o_, OL_d[:, hd:hd + 1, bsl], [tk("OL", hd, b)], [to_], f"ol2{b % 2}")
                        dma("sp", q_, QS_d[:, hd:hd + 1, bsl], [tk("QS", hd, b)], [tq_], f"qs2{b % 2}")
                        mm_chain(banks[4][:], [(Sst_b[:, 0, :], q_[:, 0, :])], [tk("Sst_b"), tq_], [tb[4]])
                        tt(o_[:, 0, :], o_[:, 0, :], banks[4][:], ALU.add, [to_, tb[4]], [to_])
                        q = sqs[b % 2]
                        act(q, o_[:, 0, :], AF.Square, [to_], [tk("sqs", b % 2)])
                        mm_chain(banks[3][:], [(ones_b, q)], [tk("sqs", b % 2), tc_], [tb[3]])
                        rstd_from_ssq(banks[3][:], EW, rs, tk("rs"), [tb[3], tc_])
                        stt(tmpo, o_[:, 0, :], gain[:, 0:1], rs, ALU.mult, ALU.mult, [to_, tk("rs"), tc_], [tk("tmpo")])
                        tt(onT[:, hd, bsl], tmpo, hg_gT[b], ALU.mult, [tk("tmpo"), tk("gT4", b)], [tk("on", b)])
                    continue
                for b in range(NB):
                    bsl = slice(b * BT, (b + 1) * BT)
                    dma("sp", ol, OL_d[:, hd * NE:(hd + 1) * NE, bsl], [tk("OL", hd, b)], [tk("ol2")], "ol2")
                    dma("sp", qs, QS_d[:, hd * ND:(hd + 1) * ND, bsl], [tk("QS", hd * ND + dc, b) for dc in range(ND)], [tk("qs2")], "qs2")
                    for ec in range(NE):
                        mm_chain(banks[4 + ec][:], [(Sst_b[:, dc, ec * 128:(ec + 1) * 128], qs[:, dc, :]) for dc in range(ND)],
                                 [tk("Sst_b"), tk("qs2")], [tb[4 + ec]])
                        tt(ol[:, ec, :], ol[:, ec, :], banks[4 + ec][:], ALU.add, [tk("ol2"), tb[4 + ec]], [tk("ol2")])
                        q = sqs[ec % 2]
                        act(q, ol[:, ec, :], AF.Square, [tk("ol2")], [tk("sqs", ec % 2)])
                        mm_chain(banks[3][:], [(ones_b, q)], [tk("sqs", ec % 2), tc_], [tb[3]], start=(ec == 0), stop=(ec == NE - 1))
                    rstd_from_ssq(banks[3][:], EW, rs, tk("rs"), [tb[3], tc_])
                    for ec in range(NE):
                        slot, tslot = gslots[ec]
                        bi = pj_i % 2
                        pj_i += 1
                        pj, tpj = banks[bi][:], tb[bi]
                        mm_chain(pj, [(slot[:, kc, :], hnT[:, kc, bsl]) for kc in range(KC)], [tslot, tk("hn", b)], [tpj])
                        g_ = gT[ec % 2]
                        act(g_, pj, AF.Silu, [tpj], [tk("gT", ec % 2)])
                        stt(tmpo, ol[:, ec, :], gain[:, ec:ec + 1], rs, ALU.mult, ALU.mult, [tk("ol2"), tk("rs"), tc_], [tk("tmpo")])
                        tt(onT[:, hd * NE + ec, bsl], tmpo, g_, ALU.mult, [tk("tmpo"), tk("gT", ec % 2)], [tk("on", b)])

        def stage_out(l):
            RC.reset()
            wsl = [RC.alloc([KC, 128], BF16) for _ in range(5)]
            W.set_slots(wsl, "p3")
            lim = stage_end[("out", l)]
            mT = RC.alloc([KC, 512], F32)
            sqs = [RC.alloc([512], BF16) for _ in range(2)]
            mark = RC.off
            i = 0
            for b in range(NB):
                bsl = slice(b * BT, (b + 1) * BT)
                for n in range(KC):
                    slot, tslot = W.get(lim)
                    bi = i % 2
                    pj, tpj = banks[bi][:], tb[bi]
                    mm_chain(pj, [(slot[:, hd, :], onT[:, hd, bsl]) for hd in range(KC)], [tslot, tk("on", b)], [tpj])
                    evac_with_ssq(mT, sqs, n, pj, tpj, n == 0, n == KC - 1, i)
                    i += 1
                RC.off = mark
                if int(os.environ.get('K_OUT', 9)) >= 2:
                    post_block(mT, b, 4 * l + 1, 4 * l + 2, False)

        def stage_mlp(l, last_layer):
            RC.reset()
            wsl = [RC.alloc([KC, 128], BF16) for _ in range(4)]
            W.set_slots(wsl, "p4")
            lim = stage_end[("mlp", l)]
            mT = RC.alloc([KC, 512], F32)
            sqs = [RC.alloc([512], BF16) for _ in range(2)]
            rl = [RC.alloc([512], F32) for _ in range(2)]
            mark = RC.off
            cnt = [0]

            def up(b, f0=0, f1=64):
                bsl = slice(b * BT, (b + 1) * BT)
                for fu in range(f0, f1):
                    i = cnt[0]
                    slot, tslot = W.get(lim)
                    pj, tpj = banks[i % 2][:], tb[i % 2]
                    mm_chain(pj, [(slot[:, kc, :], hnT[:, kc, bsl]) for kc in range(KC)], [tslot, tk("hn", b)], [tpj])
                    r_ = rl[i % 2]
                    act(r_, pj, AF.Relu, [tpj], [tk("rl", i % 2)])
                    tt(uT[:, fu, :], r_, r_, ALU.mult, [tk("rl", i % 2)], [tk("u", fu)])
                    cnt[0] += 1

            def down(b):
                for n in range(KC):
                    i = cnt[0]
                    pj, tpj = banks[i % 2][:], tb[i % 2]
                    for g in range(4):
                        slot, tslot = W.get(lim)
                        mm_chain(pj, [(slot[:, j, :], uT[:, g * 16 + j, :]) for j in range(16)],
                                 [tslot] + [tk("u", g * 16 + j) for j in range(16)], [tpj], start=(g == 0), stop=(g == 3))
                    evac_with_ssq(mT, sqs, n, pj, tpj, n == 0, n == KC - 1, i)
                    cnt[0] += 1

            up(0)
            for b in range(NB):
                down(b)
                RC.off = mark
                post_block(mT, b, 4 * l + 3, None if last_layer else 4 * (l + 1), last_layer, defer=(up if b + 1 < NB else None), defer_arg=b + 1)

        def run_stages():
            stage_pre()
            S.barrier()
            if dbg == "pre":
                dump(lambda kc: hnT[:, kc, :], KC, True)
                return
            for l in range(n_layers):
                stage_mix1(l)
                S.barrier()
                if os.environ.get('K_STOP') == 'mix1':
                    dump(lambda kc: hnT[:, kc, :], KC, True)
                    return
                stage_mix2(l)
                S.barrier()
                if dbg == f"mix{l}":
                    dump(lambda kc: onT[:, kc, :], KC, True)
                    return
                stage_out(l)
                S.barrier()
                if dbg == f"out{l}":
                    dump(lambda kc: hnT[:, kc, :], KC, True)
                    return
                stage_mlp(l, l == n_layers - 1)
                S.barrier()

        run_stages()
        S.barrier()
        with nc.Block() as block:
            S.emit(block)
    return nc


def _tile_w(Wm, ncols):
    K, N = Wm.shape
    return np.ascontiguousarray(Wm.reshape(K // 128, 128, N // ncols, ncols).transpose(2, 1, 0, 3))


def _pcol(v):
    return np.ascontiguousarray(v.reshape(-1, 128).T)


_CACHE = {}


def prepare_inputs(inputs):
    f = lambda k: np.asarray(inputs[k], dtype=np.float32)
    nwl = []
    for l in range(2):
        for k in ("norm_mix_pre", "norm_mix_post", "norm_mlp_pre", "norm_mlp_post"):
            nwl.append(_pcol(f(k)[l]))
    nw = np.ascontiguousarray(np.stack(nwl, 1))
    lbl = np.ascontiguousarray(np.stack([_pcol(f("hgrn_lb_logits")[r]) for r in range(3)], 1))
    gw = f("gla_w_in")[0]
    shared = dict(
        nw=nw, lbl=lbl,
        hgn=np.ascontiguousarray(f("hgrn_norm")[0].reshape(128, 1)),
        glan=_pcol(f("gla_norm")[0]),
        bgk=_pcol(f("gla_b_gk")[0]),
        wgk=np.ascontiguousarray(f("gla_w_gk")[0]),
        wr=np.ascontiguousarray(gw[:, 6144:6160].reshape(16, 128, 16).transpose(1, 0, 2)),
        hg_in=_tile_w(f("hgrn_w_in")[0], 128),
        hg_out=_tile_w(f("hgrn_w_out")[0], 128),
        gl_in=_tile_w(np.ascontiguousarray(gw[:, :6144]), 128),
        gl_out=_tile_w(f("gla_w_out")[0], 128),
        w_up=np.stack([_tile_w(f("mlp_w_up")[l], 128) for l in range(2)]),
        w_dn=np.stack([_tile_w(f("mlp_w_down")[l], 128) for l in range(2)]),
    )
    x = f("x")
    in_maps = []
    for c in range(NCORES):
        b, p = c // 4, c % 4
        pmv = np.zeros((128, 8), np.float32)
        for r in range(4):
            pmv[:, r] = 1.0 if r < p else 0.0
            pmv[:, 4 + r] = 0.0 if r < p else 1.0
        d = dict(shared)
        d["x"] = np.ascontiguousarray(x[b, p * TC:(p + 1) * TC, :])
        d["pm"] = pmv
        in_maps.append(d)
    return in_maps


def kernel(**inputs):
    in_maps = prepare_inputs(inputs)
    if "nc" not in _CACHE:
        _CACHE["nc"] = build_program()
    res = run_bass_kernel_spmd(_CACHE["nc"], in_maps, core_ids=list(range(NCORES)))
    x = inputs["x"]
    out = np.empty(x.shape, np.float32)
    for c in range(NCORES):
        b, p = c // 4, c % 4
        out[b, p * TC:(p + 1) * TC, :] = res.results[c]["out"]
    return out
```

```python
from contextlib import ExitStack
import os
import math
import numpy as np
import concourse.bass as bass
import concourse.mybir as mybir
from concourse.bass_utils import run_bass_kernel_spmd

F32 = mybir.dt.float32
BF16 = mybir.dt.bfloat16
AF = mybir.ActivationFunctionType
ALU = mybir.AluOpType

NCORES = 8
D = 2048
KC = 16
TC = 2048
NB = 4
BT = 512
DFF = 8192
EPS = 1e-6
ENGINES = ("pe", "act", "dve", "pool", "sp")


class Sem:
    __slots__ = ("h", "v")

    def __init__(self, h):
        self.h = h
        self.v = 0


class Tok:
    __slots__ = ("w", "r")

    def __init__(self):
        self.w = None
        self.r = {}


class Sched:
    def __init__(self, sem_alloc):
        self.sem_alloc = sem_alloc
        self.q = {e: [] for e in ENGINES}
        self.esem = {e: Sem(sem_alloc("c_" + e)) for e in ENGINES}
        self.waited = {e: {} for e in ENGINES}
        self.allsems = list(self.esem.values())
        self.toks = {}

    def tk(self, *key):
        t = self.toks.get(key)
        if t is None:
            t = self.toks[key] = Tok()
        return t

    def new_dma_sem(self, name):
        s = Sem(self.sem_alloc(name))
        self.allsems.append(s)
        return s

    def _waits(self, eng, deps, skip_own):
        own = self.esem[eng]
        waits = []
        wd = self.waited[eng]
        for s, v in deps.items():
            if s is own and (skip_own or v > own.v):
                continue
            if wd.get(s, 0) < v:
                waits.append((s, v))
                wd[s] = v
        return waits

    def op(self, eng, fn, reads=(), writes=(), dma_sem=None, inc=True, incv=None, record=True,
           skip_own=None):
        if skip_own is None:
            skip_own = (eng == "pe")
        deps = {}
        for t in reads:
            if t.w is not None and deps.get(t.w[0], 0) < t.w[1]:
                deps[t.w[0]] = t.w[1]
        for t in writes:
            if t.w is not None and deps.get(t.w[0], 0) < t.w[1]:
                deps[t.w[0]] = t.w[1]
            for s, v in t.r.items():
                if deps.get(s, 0) < v:
                    deps[s] = v
        waits = self._waits(eng, deps, skip_own)
        if dma_sem is not None:
            csem, iv = dma_sem, (16 if incv is None else incv)
        else:
            csem, iv = self.esem[eng], 1
        if inc:
            csem.v += iv
            val = csem.v
        else:
            val = csem.v + iv
        self.q[eng].append((waits, fn, csem if inc else None, iv))
        if record:
            self.record(reads, writes, (csem, val))
        return (csem, val)

    def record(self, reads, writes, sv):
        csem, val = sv
        for t in reads:
            if t.r.get(csem, 0) < val:
                t.r[csem] = val
        for t in writes:
            t.w = (csem, val)
            t.r = {}

    def barrier(self):
        svs = [(s, s.v) for s in self.allsems if s.v > 0]
        for e in ENGINES:
            waits = self._waits(e, dict(svs), False)
            if waits:
                self.q[e].append((waits, None, None, 0))
        for e in ENGINES:
            if self.esem[e].v > 12000:
                s = Sem(self.sem_alloc("c_" + e))
                self.esem[e] = s
                self.allsems.append(s)

    def emit(self, block):
        def run(eh, items):
            for waits, fn, csem, iv in items:
                for s, v in waits:
                    eh.wait_ge(s.h, v)
                if fn is None:
                    continue
                ins = fn(eh)
                if csem is not None:
                    ins.then_inc(csem.h, iv)

        m = {"pe": block.tensor, "act": block.scalar, "dve": block.vector,
             "pool": block.gpsimd, "sp": block.sync}
        for e in ENGINES:
            items = self.q[e]
            if items:
                m[e](lambda eh, items=items: run(eh, items))


class Region:
    def __init__(self, t, nwords):
        self.t, self.n, self.off = t, nwords, 0

    def reset(self):
        self.off = 0

    def alloc(self, free_shape, dt):
        nel = int(np.prod(free_shape))
        nbytes = nel * (4 if dt == F32 else 2)
        words = (nbytes + 31) // 32 * 8
        assert self.off + words <= self.n, ("region overflow", self.off, words, self.n)
        v = self.t[:, self.off:self.off + words]
        self.off += words
        if dt != F32:
            v = v.bitcast(dt)
        v = v[:, 0:nel]
        if len(free_shape) == 2:
            v = v.rearrange("p (a b) -> p a b", b=free_shape[1])
        elif len(free_shape) == 3:
            v = v.rearrange("p (a b c) -> p a b c", b=free_shape[1], c=free_shape[2])
        return v


LAYERS = [
    dict(kind="hgrn", NH=16, ND=1, NE=1, escale=1.0),
    dict(kind="gla", NH=4, ND=2, NE=4, escale=1.0 / 16.0),
]


def build_program(n_layers=2, dbg=None, ncores=NCORES):
    nc = bass.Bass("TRN2", target_bir_lowering=False)
    dt_in = lambda name, shape: nc.dram_tensor(name, shape, F32, kind="ExternalInput").ap()
    x_d = dt_in("x", [TC, D])
    pm_d = dt_in("pm", [128, 8])
    nw_d = dt_in("nw", [128, 8, KC])
    lbl_d = dt_in("lbl", [128, 3, KC])
    hgn_d = dt_in("hgn", [128, 1])
    glan_d = dt_in("glan", [128, 4])
    bgk_d = dt_in("bgk", [128, 8])
    wgk_d = dt_in("wgk", [16, 1024])
    wr_d = dt_in("wr", [128, KC, 16])
    hg_in_d = dt_in("hg_in", [64, 128, KC, 128])
    hg_out_d = dt_in("hg_out", [16, 128, KC, 128])
    gl_in_d = dt_in("gl_in", [48, 128, KC, 128]) if n_layers > 1 else None
    gl_out_d = dt_in("gl_out", [16, 128, KC, 128]) if n_layers > 1 else None
    up_d = dt_in("w_up", [n_layers, 64, 128, KC, 128])
    dn_d = dt_in("w_dn", [n_layers, 16, 128, 64, 128])
    out_d = nc.dram_tensor("out", [TC, D], F32, kind="ExternalOutput").ap()
    dbg_d = nc.dram_tensor("dbg", [128, KC, TC], F32, kind="ExternalOutput").ap() if dbg else None
    H_d = nc.dram_tensor("Hs", [128, KC, TC], F32).ap()
    OL_d = nc.dram_tensor("OLs", [128, KC, TC], F32).ap()
    QS_d = nc.dram_tensor("QSs", [128, KC, TC], BF16).ap()
    CCG = {0: 8, 1: 1}
    ccw = {0: 8 * 129, 1: 2 * 513}
    ccn = {0: 2, 1: 4}
    cc_in = {l: [nc.dram_tensor(f"cc_in{l}_{g}", [128, ccw[l]], F32) for g in range(ccn[l])] for l in range(2)}
    cc_out = {l: [nc.dram_tensor(f"cc_out{l}_{g}", [4 * 128, ccw[l]], F32) for g in range(ccn[l])] for l in range(2)}

    with ExitStack() as es:
        sb = lambda name, shape, dt: es.enter_context(nc.sbuf_tensor(name, shape, dt))
        S = Sched(lambda name: es.enter_context(nc.semaphore(name)))
        tk = S.tk

        RA = sb("RA", [128, KC, TC], BF16)
        RB = sb("RB", [128, KC * TC // 2], F32)
        NCW = 18176
        RCt = sb("RC", [128, NCW], F32)
        RC = Region(RCt, NCW)
        cst = sb("cst", [128, 1792], F32)
        CR = Region(cst, 1792)
        banks = [es.enter_context(nc.psum_tensor(f"bank{i}", [128, 512], F32)) for i in range(8)]
        tb = [tk("bank", i) for i in range(8)]

        RBb = RB[:].bitcast(BF16)
        onT = RBb.rearrange("p (a t) -> p a t", t=TC)
        uT = RBb.rearrange("p (a t) -> p a t", t=BT)
        slots1 = RBb.rearrange("p (s a c) -> p s a c", a=KC, c=128)
        hnT = RA

        def act(out, in_, func, reads, writes, **kw):
            S.op("act", lambda e: e.activation(out=out, in_=in_, func=func, **kw), reads, writes)

        def tt(out, in0, in1, op, reads, writes, eng="dve"):
            S.op(eng, lambda e: e.tensor_tensor(out=out, in0=in0, in1=in1, op=op), reads, writes)

        def ts(out, in0, s1, s2, op0, op1, reads, writes, eng="dve"):
            if s2 is None:
                S.op(eng, lambda e: e.tensor_scalar(out=out, in0=in0, scalar1=s1, scalar2=None, op0=op0), reads, writes)
            else:
                S.op(eng, lambda e: e.tensor_scalar(out=out, in0=in0, scalar1=s1, scalar2=s2, op0=op0, op1=op1), reads, writes)

        def stt(out, in0, scalar, in1, op0, op1, reads, writes):
            S.op("dve", lambda e: e.scalar_tensor_tensor(out=out, in0=in0, scalar=scalar, in1=in1, op0=op0, op1=op1),
                 reads, writes)

        def cp(out, in_, reads, writes, eng="dve"):
            S.op(eng, lambda e: e.tensor_copy(out=out, in_=in_), reads, writes)

        def mm_chain(out, pairs, reads, writes, start=True, stop=True):
            n = len(pairs)
            for i, (l, r) in enumerate(pairs):
                st = start and i == 0
                sp_ = stop and i == n - 1
                fn = (lambda e, l=l, r=r, st=st, sp_=sp_: e.matmul(out, lhsT=l, rhs=r, start=st, stop=sp_))
                if n == 1:
                    S.op("pe", fn, reads, writes)
                elif i == 0:
                    S.op("pe", fn, reads, writes, inc=False, record=False)
                elif i < n - 1:
                    S.op("pe", fn, inc=False, record=False)
                else:
                    sv = S.op("pe", fn, record=False)
                    S.record(reads, writes, sv)

        def transpose(out, in_, ident, reads, writes):
            S.op("pe", lambda e: e.transpose(out=out, in_=in_, identity=ident), reads, writes)

        dsem = {}

        def dma(eng, out, in_, reads, writes, semname):
            s = dsem.get(semname)
            if s is None:
                s = dsem[semname] = S.new_dma_sem("d_" + semname)
            return S.op(eng, lambda e: e.dma_start(out=out, in_=in_), reads, writes, dma_sem=s)

        ident_f = CR.alloc([128], F32)
        ident_b = CR.alloc([128], BF16)
        ones_b = CR.alloc([128], BF16)
        mask4 = CR.alloc([4, 128], F32)
        pm = CR.alloc([8], F32)
        nw = CR.alloc([8, KC], F32)
        lbt = CR.alloc([3, KC], F32)
        lb = CR.alloc([KC], F32)
        oml = CR.alloc([KC], F32)
        lnoml = CR.alloc([KC], F32)
        hgn = CR.alloc([1], F32)
        glan = CR.alloc([4], F32)
        bgk = CR.alloc([8], F32)
        epsc = CR.alloc([1], F32)
        wgk_b = CR.alloc([1024], BF16)
        wr_b = CR.alloc([KC, 16], BF16)
        tmpc = CR.alloc([KC], F32)
        tc_ = tk("consts")

        S.op("pool", lambda e: e.memset(ident_f, 1.0), (), [tc_])
        S.op("pool", lambda e: e.affine_select(out=ident_f, in_=ident_f, pattern=[[-1, 128]], compare_op=ALU.is_equal,
                                               fill=0.0, base=0, channel_multiplier=1), [tc_], [tc_])
        S.op("pool", lambda e: e.tensor_copy(out=ident_b, in_=ident_f), [tc_], [tc_])
        S.op("pool", lambda e: e.memset(ones_b, 1.0), (), [tc_])
        S.op("pool", lambda e: e.memset(epsc, EPS), (), [tc_])
        m0 = mask4[:, 0, :]
        S.op("pool", lambda e: e.memset(m0, 1.0), (), [tc_])
        S.op("pool", lambda e: e.affine_select(out=m0, in_=m0, pattern=[[1, 128]], compare_op=ALU.is_ge,
                                               fill=0.0, base=0, channel_multiplier=-1), [tc_], [tc_])
        S.op("pool", lambda e: e.memset(mask4[0:64, 0, 64:128], 0.0), [tc_], [tc_])
        for i in range(1, 4):
            S.op("pool", lambda e, i=i: e.tensor_copy(out=mask4[:, i, :], in_=mask4[:, 0, :]), [tc_], [tc_])
        for dst, src in ((pm, pm_d), (nw, nw_d), (lbt, lbl_d), (hgn, hgn_d), (glan, glan_d), (bgk, bgk_d)):
            dma("sp", dst, src, (), [tc_], "const")
        dma("pool", wgk_b[0:16, :], wgk_d, (), [tc_], "const2")
        dma("pool", wr_b, wr_d, (), [tc_], "const2")
        act(lbt, lbt, AF.Exp, [tc_], [tc_])
        tt(tmpc, lbt[:, 0, :], lbt[:, 1, :], ALU.add, [tc_], [tc_])
        tt(tmpc, tmpc, lbt[:, 2, :], ALU.add, [tc_], [tc_])
        S.op("dve", lambda e: e.reciprocal(out=tmpc, in_=tmpc), [tc_], [tc_])
        tt(lb, lbt[:, 0, :], tmpc, ALU.mult, [tc_], [tc_])
        ts(oml, lb, -1.0, 1.0, ALU.mult, ALU.add, [tc_], [tc_])
        act(lnoml, oml, AF.Ln, [tc_], [tc_])
        S.barrier()

        class WStream:
            def __init__(self):
                self.plan = []
                self.pos = 0
                self.issued = 0
                self.slots = None
                self.base = 0

            def set_slots(self, slot_aps, name):
                self.slots = slot_aps
                self.name = name
                self.base = self.pos
                self.issued = self.pos

            def get(self, limit, hold=None):
                i = self.pos
                self.pos += 1
                if hold is None:
                    hold = i
                n = min(hold + len(self.slots), limit, len(self.plan))
                while self.issued < n:
                    j = self.issued
                    sj = (j - self.base) % len(self.slots)
                    dma("pool", self.slots[sj], self.plan[j], (), [tk("slot", self.name, sj)], f"w{self.name}{sj}")
                    self.issued += 1
                si = (i - self.base) % len(self.slots)
                return self.slots[si], tk("slot", self.name, si)

        W = WStream()
        stage_end = {}

        def plan_units():
            for l in range(n_layers):
                L = LAYERS[l]
                hg = L["kind"] == "hgrn"
                for hd in range(L["NH"]):
                    if hg:
                        W.plan += [hg_in_d[hd], hg_in_d[16 + hd], hg_in_d[32 + hd]]
                    else:
                        W.plan += [gl_in_d[2 * hd + i] for i in range(2)] + [gl_in_d[8 + 2 * hd + i] for i in range(2)] \
                            + [gl_in_d[16 + 4 * hd + i] for i in range(4)]
                stage_end[("mix1", l)] = len(W.plan)
                for hd in range(L["NH"]):
                    W.plan += [(hg_in_d[48 + hd] if hg else gl_in_d[32 + 4 * hd + ec]) for ec in range(L["NE"])]
                stage_end[("mix2", l)] = len(W.plan)
                wd = hg_out_d if l == 0 else gl_out_d
                for b in range(NB):
                    W.plan += [wd[n] for n in range(KC)]
                stage_end[("out", l)] = len(W.plan)
                for b in range(NB):
                    W.plan += [up_d[l, fu] for fu in range(64)]
                    W.plan += [dn_d[l, n, :, g * 16:(g + 1) * 16, :] for n in range(KC) for g in range(4)]
                stage_end[("mlp", l)] = len(W.plan)

        plan_units()

        def rstd_from_ssq(ssq_bank, n_feat, rs, rs_t, reads):
            act(rs, ssq_bank, AF.Sqrt, reads, [rs_t], scale=1.0 / n_feat, bias=epsc)
            S.op("dve", lambda e: e.reciprocal(out=rs, in_=rs), [rs_t], [rs_t])

        def cp_any(out, in_, reads, writes, i):
            if i % 2 == 0:
                cp(out, in_, reads, writes)
            else:
                act(out, in_, AF.Copy, reads, writes)

        def finish_block(mT, b, layer_next, last):
            tm = [tk("m", kc) for kc in range(KC)]
            bsl = slice(b * BT, (b + 1) * BT)
            if last:
                ostg = [RC.alloc([512], F32) for _ in range(2)]
                i = 0
                pt = banks[2][:].rearrange("p (a c) -> p a c", c=128)
                for tt_ in range(4):
                    for g in range(4):
                        for j in range(4):
                            kc = g * 4 + j
                            transpose(pt[:, j, :], mT[:, kc, tt_ * 128:(tt_ + 1) * 128], ident_f, [tm[kc], tc_], [tb[2]])
                        o = ostg[i % 2]
                        cp_any(o, banks[2][:], [tb[2]], [tk("ostg", i % 2)], i)
                        r0 = b * BT + tt_ * 128
                        dma("sp", out_d[r0:r0 + 128, g * 512:(g + 1) * 512], o, [tk("ostg", i % 2)], (), f"ostg{i % 2}")
                        i += 1
                return
            for g in range(4):
                dma("sp", H_d[:, g * 4:(g + 1) * 4, bsl], mT[:, g * 4:(g + 1) * 4, :], [tm[g * 4 + j] for j in range(4)],
                    [tk("H", g, b)], f"mst{g}")
            if layer_next is None:
                return
            sqs = [RC.alloc([512], BF16) for _ in range(2)]
            for kc in range(KC):
                q = sqs[kc % 2]
                act(q, mT[:, kc, :], AF.Square, [tm[kc]], [tk("sqs", kc % 2)])
                mm_chain(banks[3][:], [(ones_b, q)], [tk("sqs", kc % 2), tc_], [tb[3]], start=(kc == 0), stop=(kc == KC - 1))
            rs = RC.alloc([512], F32)
            rstd_from_ssq(banks[3][:], D, rs, tk("rs"), [tb[3], tc_])
            for kc in range(KC):
                stt(hnT[:, kc, bsl], mT[:, kc, :], nw[:, layer_next, kc:kc + 1], rs, ALU.mult, ALU.mult,
                    [tm[kc], tk("rs"), tc_], [tk("hn", b)])

        def post_block(mT, b, wpost_idx, layer_next, last, defer=None, defer_arg=None):
            tm = [tk("m", kc) for kc in range(KC)]
            bsl = slice(b * BT, (b + 1) * BT)
            rsm = RC.alloc([512], F32)
            rstd_from_ssq(banks[3][:], D, rsm, tk("rsm"), [tb[3], tc_])
            hch = [RC.alloc([2, 512], F32) for _ in range(2)]

            for i, kc in enumerate(range(0, KC, 2)):
                hc = hch[i % 2]
                th = tk("hch", i % 2)
                dma("sp", hc, H_d[:, kc:kc + 2, bsl], [tk("H", kc // 4, b)], [th], f"hch{i % 2}")
                for j in range(2):
                    stt(mT[:, kc + j, :], mT[:, kc + j, :], nw[:, wpost_idx, kc + j:kc + j + 1], rsm, ALU.mult, ALU.mult,
                        [tm[kc + j], tk("rsm"), tc_], [tm[kc + j]])
                    tt(mT[:, kc + j, :], mT[:, kc + j, :], hc[:, j, :], ALU.add, [tm[kc + j], th], [tm[kc + j]])
            if defer is not None:
                defer(defer_arg, 0, 32)
            if int(os.environ.get('K_OUT', 9)) >= 3:
                finish_block(mT, b, layer_next, last)
            if defer is not None:
                defer(defer_arg, 32, 64)

        def evac_with_ssq(mT, sqs, kc, pj, tpj, first, lastc, i):
            q = sqs[i % 2]
            cp(mT[:, kc, :], pj, [tpj], [tk("m", kc)])
            act(q, mT[:, kc, :], AF.Square, [tk("m", kc)], [tk("sqs", i % 2)])
            mm_chain(banks[3][:], [(ones_b, q)], [tk("sqs", i % 2), tc_], [tb[3]], start=first, stop=lastc)

        def dump(src_fn, n, dt_is_bf16):
            RC.reset()
            stg = RC.alloc([TC], F32)
            for kc in range(n):
                cp(stg, src_fn(kc), [], [tk("dstg")])
                dma("sp", dbg_d[:, kc, :], stg, [tk("dstg")], (), "dstg")

        def stage_pre():
            RC.reset()
            mT = RC.alloc([KC, 512], F32)
            xts = [RC.alloc([D], F32) for _ in range(2)]
            pt = banks[2][:].rearrange("p (a c) -> p a c", c=128)
            for b in range(NB):
                for tt_ in range(4):
                    ti = b * 4 + tt_
                    xt = xts[ti % 2]
                    txt = tk("xt", ti % 2)
                    dma("sp", xt, x_d[ti * 128:(ti + 1) * 128, :], (), [txt], f"xt{ti % 2}")
                    for g in range(4):
                        for j in range(4):
                            kc = g * 4 + j
                            transpose(pt[:, j, :], xt[:, kc * 128:(kc + 1) * 128], ident_f, [txt, tc_], [tb[2]])
                        tms = [tk("m", g * 4 + j) for j in range(4)]
                        cp_any(mT[:, g * 4:(g + 1) * 4, tt_ * 128:(tt_ + 1) * 128], pt, [tb[2]], tms, g)
                finish_block(mT, b, 0, False)

        def stage_mix1(l):
            L = LAYERS[l]
            NH, ND, NE, esc = L["NH"], L["ND"], L["NE"], L["escale"]
            hg = L["kind"] == "hgrn"
            EW = NE * 128
            RC.reset()
            W.set_slots([slots1[:, i] for i in range(16)], "p1")
            lim = stage_end[("mix1", l)]
            T = lambda dt=F32: RC.alloc([512], dt)
            NSET = 2 if hg else 1
            qTs = [[T() for _ in range(ND)] for _ in range(NSET)]
            kTs = [[T() for _ in range(ND)] for _ in range(NSET)]
            lfs = [[T() for _ in range(ND)] for _ in range(NSET)]
            bs = [T() for _ in range(ND)]
            bp = T()
            ex = [T() for _ in range(2)]
            qt_b = [T(BF16) for _ in range(ND)]
            kt_b = [T(BF16) for _ in range(ND)]
            qs_b = [[T(BF16) for _ in range(ND)] for _ in range(2)]
            vTs = [[T(BF16) for _ in range(NE)] for _ in range(NSET)]
            vtm = RC.alloc([4, EW], BF16)
            ktm2 = [RC.alloc([4, ND * 128], BF16) for _ in range(2)]
            sT4 = RC.alloc([4, 128], BF16)
            Sfull = RC.alloc([ND, EW + 1], F32)
            Sst = Sfull[:, :, 0:EW]
            atot = Sfull[:, :, EW:EW + 1]
            NSB = 8 if hg else 4
            Sbf = [RC.alloc([ND, EW], BF16) for _ in range(NSB)]
            oloc = [RC.alloc([NE, 512], F32) for _ in range(2 if hg else 1)]
            carry = RC.alloc([ND, 2], F32)
            dd = RC.alloc([ND, 8], F32)
            Ac = RC.alloc([ND, 8], F32)
            rT = None if hg else RC.alloc([TC], BF16)
            ones_f = RC.alloc([512], F32)
            S.op("dve", lambda e: e.memset(ones_f, 1.0), (), [tk("ones_f")])
            S.op("dve", lambda e: e.memset(ktm2[0][64:128], 0.0), (), [tk("ktm")])
            S.op("dve", lambda e: e.memset(ktm2[1][0:64], 0.0), (), [tk("ktm")])
            pj_i = [0]
            hn_rhs = lambda kc, b: hnT[:, kc, b * BT:(b + 1) * BT]
            ptb = banks[2][:].bitcast(BF16).rearrange("p (a c) -> p a c", c=128)
            sc4 = banks[3][:].rearrange("p (a c) -> p a c", c=128)
            cw = EW + 1
            HPG = CCG[l]
            cins = [c_.ap().rearrange("p (u w) -> p u w", w=cw) for c_ in cc_in[l]]

            if not hg:
                for b in range(NB):
                    pj, tpj = banks[b % 2][0:16, :], tb[b % 2]
                    mm_chain(pj, [(wr_b[:, kc, :], hn_rhs(kc, b)) for kc in range(KC)], [tc_, tk("hn", b)], [tpj])
                    cp(rT[0:16, b * BT:(b + 1) * BT], pj, [tpj], [tk("rT")])

            NHH = min(NH, int(os.environ.get('K_HEADS', NH)))
            slots_by_head = {}

            def emit_P(it):
                hd_, b_ = it // NB, it % NB
                if hd_ not in slots_by_head:
                    hold_ = W.pos
                    slots_by_head[hd_] = {key: [W.get(lim, hold_) for _ in range(n)] for key, n in (("q", ND), ("k", ND), ("v", NE))}
                for key, bank in (("q", 5), ("k", 6), ("v", 7)):
                    slot, tslot = slots_by_head[hd_][key][0]
                    mm_chain(banks[bank][:], [(slot[:, kc, :], hn_rhs(kc, b_)) for kc in range(KC)], [tslot, tk("hn", b_)], [tb[bank]])

            def emit_EV(s_):
                act(qTs[s_][0], banks[5][:], AF.Silu, [tb[5]], [tk("qT", 0, s_)])
                act(lfs[s_][0], banks[6][:], AF.Sigmoid, [tb[6]], [tk("lf", 0, s_)])
                act(kTs[s_][0], banks[6][:], AF.Sigmoid, [tb[6]], [tk("kT", 0, s_)], scale=-1.0)
                act(vTs[s_][0], banks[7][:], AF.Copy, [tb[7]], [tk("vT", 0, s_)])

            for hd in range(NHH):
                if not hg:
                    hold = W.pos
                    slots_h = {key: [W.get(lim, hold) for _ in range(n)] for key, n in (("q", ND), ("k", ND), ("v", NE))}
                for b in range(NB):
                    bsl = slice(b * BT, (b + 1) * BT)
                    par = (hd * NB + b) % 2
                    it = hd * NB + b
                    sx = it % NSET
                    qT, kT, lf, vT = qTs[sx], kTs[sx], lfs[sx], vTs[sx]
                    if hg:
                        if it == 0:
                            emit_P(0)
                            emit_EV(0)
                        if it + 1 < NHH * NB:
                            emit_P(it + 1)

                    def proj(slot_t, evac):
                        slot, tslot = slot_t
                        bi = pj_i[0] % 2
                        pj_i[0] += 1
                        pj, tpj = banks[bi][:], tb[bi]
                        mm_chain(pj, [(slot[:, kc, :], hn_rhs(kc, b)) for kc in range(KC)], [tslot, tk("hn", b)], [tpj])
                        evac(pj, tpj)

                    CUT = int(os.environ.get('K_CUT', 9))
                    for dc in range(ND):
                        col = hd * ND + dc
                        if hg:
                            act(lf[dc], lf[dc], AF.Ln, [tk("lf", dc, sx), tc_], [tk("lf", dc, sx)], scale=oml[:, col:col + 1],
                                bias=lb[:, col:col + 1])
                        else:
                            proj(slots_h["q"][dc], lambda pj, tpj: act(qT[dc], pj, AF.Copy, [tpj], [tk("qT", dc, sx)], scale=1.0 / 16.0))
                            proj(slots_h["k"][dc], lambda pj, tpj: act(kT[dc], pj, AF.Copy, [tpj], [tk("kT", dc, sx)]))
                            bi = pj_i[0] % 2
                            pj_i[0] += 1
                            pj, tpj = banks[bi][:], tb[bi]
                            c0 = hd * 256 + dc * 128
                            mm_chain(pj, [(wgk_b[0:16, c0:c0 + 128], rT[0:16, bsl])], [tc_, tk("rT")], [tpj])
                            act(lf[dc], pj, AF.Sigmoid, [tpj, tc_], [tk("lf", dc, sx)], bias=bgk[:, col:col + 1])
                            act(lf[dc], lf[dc], AF.Ln, [tk("lf", dc, sx)], [tk("lf", dc, sx)])
                        if CUT <= 1:
                            continue
                        init = 0.0 if b == 0 else carry[:, dc, 0:1]
                        S.op("dve", lambda e, dc=dc, init=init, lfd=lf[dc]: e.tensor_tensor_scan(
                            out=bs[dc], data0=ones_f, data1=lfd, initial=init, op0=ALU.mult, op1=ALU.add),
                            [tk("lf", dc, sx), tk("ones_f"), tk("carry", dc)], [tk("bs", dc)])
                        bs3 = bs[dc].rearrange("p (c k) -> p c k", k=64)
                        rcol = bs3[:, :, 63]
                        if b == 0:
                            cp(dd[:, dc, 0:1], rcol[:, 0:1], [tk("bs", dc)], [tk("dd", dc)])
                        else:
                            tt(dd[:, dc, 0:1], rcol[:, 0:1], carry[:, dc, 0:1], ALU.subtract, [tk("bs", dc), tk("carry", dc)],
                               [tk("dd", dc)])
                        tt(dd[:, dc, 1:8], rcol[:, 1:8], rcol[:, 0:7], ALU.subtract, [tk("bs", dc)], [tk("dd", dc)])
                        act(Ac[:, dc, :], dd[:, dc, :], AF.Exp, [tk("dd", dc)], [tk("Ac", dc)], scale=esc)
                        cp(carry[:, dc, 0:1], rcol[:, 7:8], [tk("bs", dc)], [tk("carry", dc)])
                        bp3 = bp.rearrange("p (c k) -> p c k", k=64)
                        tt(bp3, bs3, bs3[:, :, 63:64].to_broadcast([128, 8, 64]), ALU.subtract, [tk("bs", dc)], [tk("bp")])
                        act(ex[0], bp, AF.Exp, [tk("bp")], [tk("ex", 0)], scale=esc)
                        tt(qt_b[dc], qT[dc], ex[0], ALU.mult, [tk("qT", dc, sx), tk("ex", 0)], [tk("qt_b", dc)])
                        if hg:
                            act(ex[1], bp, AF.Exp, [tk("bp"), tc_], [tk("ex", 1)], scale=-esc, bias=lnoml[:, col:col + 1])
                        else:
                            act(ex[1], bp, AF.Exp, [tk("bp")], [tk("ex", 1)], scale=-esc)
                        tt(kt_b[dc], kT[dc], ex[1], ALU.mult, [tk("kT", dc, sx), tk("ex", 1)], [tk("kt_b", dc)])
                        act(ex[0], bs[dc], AF.Exp, [tk("bs", dc)], [tk("ex", 0)], scale=esc)
                        tt(qs_b[par][dc], qT[dc], ex[0], ALU.mult, [tk("qT", dc, sx), tk("ex", 0)], [tk("qs_b", par, dc)])
                        dma("sp", QS_d[:, col, bsl], qs_b[par][dc], [tk("qs_b", par, dc)], [tk("QS", col, b)], f"qs{par}{dc}")
                        if b == NB - 1:
                            act(atot[:, dc, :], rcol[:, 7:8], AF.Exp, [tk("bs", dc)], [tk("atot")], scale=esc)
                        if CUT <= 2:
                            continue
                        for j in range(4):
                            transpose(ptb[:, j, :], kt_b[dc][:, j * 128:(j + 1) * 128], ident_b, [tk("kt_b", dc), tc_], [tb[2]])
                        cp(ktm2[0][0:64, :, dc * 128:(dc + 1) * 128], ptb[0:64, 0:4, :], [tb[2]], [tk("ktm")])
                        cp(ktm2[1][64:128, :, dc * 128:(dc + 1) * 128], ptb[64:128, 0:4, :], [tb[2]], [tk("ktm")])
                    if CUT <= 3:
                        continue
                    for ec in range(NE):
                        if not hg:
                            proj(slots_h["v"][ec], lambda pj, tpj: act(vT[ec], pj, AF.Copy, [tpj], [tk("vT", ec, sx)]))
                        for j in range(4):
                            transpose(ptb[:, j, :], vT[ec][:, j * 128:(j + 1) * 128], ident_b, [tk("vT", ec, sx), tc_], [tb[2]])
                        cp(vtm[:, :, ec * 128:(ec + 1) * 128], ptb[:, 0:4, :], [tb[2]], [tk("vtm")])
                    if CUT <= 4:
                        continue
                    if b == 0:
                        S.op("dve", lambda e: e.memset(Sst, 0.0), (), [tk("Sst")])
                    qk_reads = [tk("kt_b", d_) for d_ in range(ND)] + [tk("qt_b", d_) for d_ in range(ND)]
                    for j in range(4):
                        mm_chain(sc4[:, j, :], [(kt_b[d_][:, j * 128:(j + 1) * 128], qt_b[d_][:, j * 128:(j + 1) * 128])
                                               for d_ in range(ND)], qk_reads, [tb[3]])
                    tt(sT4, sc4, mask4, ALU.mult, [tb[3], tc_], [tk("sT4")])

                    if CUT <= 5:
                        continue

                    def Ureg(c, d_):
                        if hg:
                            return banks[c // 4][:, (c % 4) * 128:(c % 4 + 1) * 128], tb[c // 4]
                        return banks[d_][:, 0:EW], tb[d_]

                    def U_mm(c):
                        j, half = c // 2, c % 2
                        rows = slice(half * 64, half * 64 + 64)
                        for d_ in range(ND):
                            ur, tu = Ureg(c, d_)
                            mm_chain(ur, [(ktm2[half][:, j, d_ * 128:(d_ + 1) * 128], vtm[:, j, :])], [tk("ktm"), tk("vtm")], [tu])

                    if hg:
                        for c in range(8):
                            U_mm(c)
                    for c in range(8):
                        j, half = c // 2, c % 2
                        rows = slice(half * 64, half * 64 + 64)
                        sb_ = Sbf[c % NSB]
                        tsb = tk("Sbf", c % NSB)
                        if not hg:
                            U_mm(c)
                        for d_ in range(ND):
                            ts(sb_[:, d_, :], Sst[:, d_, :], Ac[:, d_, c:c + 1], None, ALU.mult, None,
                               [tk("Sst"), tk("Ac", d_)], [tsb])
                        for d_ in range(ND):
                            ur, tu = Ureg(c, d_)
                            stt(Sst[:, d_, :], Sst[:, d_, :], Ac[:, d_, c:c + 1], ur, ALU.mult, ALU.add,
                                [tk("Sst"), tk("Ac", d_), tu], [tk("Sst")])
                        for ec in range(NE):
                            pairs = [(vtm[:, j, ec * 128:(ec + 1) * 128], sT4[:, j, half * 64:half * 64 + 64])]
                            pairs += [(sb_[:, d_, ec * 128:(ec + 1) * 128], qt_b[d_][:, c * 64:(c + 1) * 64]) for d_ in range(ND)]
                            mm_chain(banks[4 + ec][:, c * 64:(c + 1) * 64], pairs,
                                     [tk("vtm"), tk("sT4"), tsb] + [tk("qt_b", d_) for d_ in range(ND)], [tb[4 + ec]])
                    if CUT <= 6:
                        continue
                    oi = par % len(oloc)
                    ol, tol = oloc[oi], tk("oloc", oi)
                    for ec in range(NE):
                        cp_any(ol[:, ec, :], banks[4 + ec][:], [tb[4 + ec]], [tol], ec)
                    dma("sp", OL_d[:, hd * NE:(hd + 1) * NE, bsl], ol, [tol], [tk("OL", hd, b)], f"ol{oi}")
                    if hg and it + 1 < NHH * NB:
                        emit_EV((it + 1) % NSET)
                hq = hd % HPG
                dma("sp", cins[hd // HPG][:, hq * ND:(hq + 1) * ND, :], Sfull, [tk("Sst"), tk("atot")], [tk("ccin", hd // HPG)], "ccS")

        def stage_mix2(l):
            L = LAYERS[l]
            NH, ND, NE, esc = L["NH"], L["ND"], L["NE"], L["escale"]
            hg = L["kind"] == "hgrn"
            EW = NE * 128
            cw = EW + 1
            RC.reset()
            wsl = [RC.alloc([KC, 128], BF16) for _ in range(3 if hg else 6)]
            W.set_slots(wsl, "p2")
            lim = stage_end[("mix2", l)]
            HPG = CCG[l]
            for g in range(ccn[l]):
                ccs = S.new_dma_sem(f"cc{l}_{g}")
                S.op("pool", lambda e, g=g: e.collective_compute("AllGather", ALU.bypass,
                                                                 replica_groups=[[0, 1, 2, 3], [4, 5, 6, 7]][:ncores // 4],
                                                                 ins=[cc_in[l][g].ap()], outs=[cc_out[l][g].ap()]),
                     [tk("ccin", g)], [tk("ccout", g)], dma_sem=ccs, incv=1)
            gath = RC.alloc([4, ND, cw], F32)
            Tst = RC.alloc([ND, EW], F32)
            tmpS = RC.alloc([ND, EW], F32)
            aeff = RC.alloc([ND, 1], F32)
            Sst_b = RC.alloc([ND, EW], BF16)
            ol = RC.alloc([NE, 512], F32)
            qs = RC.alloc([ND, 512], BF16)
            sqs = [RC.alloc([512], BF16) for _ in range(2)]
            rs = RC.alloc([512], F32)
            gT = [RC.alloc([512], BF16) for _ in range(2)]
            tmpo = RC.alloc([512], F32)
            gain = hgn if hg else glan
            if hg:
                hg_gT = [RC.alloc([512], BF16) for _ in range(4)]
                hg_ol = [RC.alloc([NE, 512], F32) for _ in range(4)]
                hg_qs = [RC.alloc([ND, 512], BF16) for _ in range(4)]
                hg_sq = [RC.alloc([512], BF16) for _ in range(4)]
                hg_rs = [RC.alloc([512], F32) for _ in range(4)]
                hg_tmp = [RC.alloc([512], F32) for _ in range(4)]
                hg_gbanks = [0, 1, 5, 6]
                hg_cbanks = [4, 7, 2, 3]
            couts = [c_.ap().rearrange("(r p) (u w) -> p r u w", p=128, w=cw) for c_ in cc_out[l]]
            pj_i = 0
            for hd in range(min(NH, int(os.environ.get('K_HEADS', NH)))):
                hold = W.pos
                gslots = [W.get(lim, hold) for _ in range(NE)]
                for r in range(3):
                    hq = hd % HPG
                    dma("sp", gath[:, r], couts[hd // HPG][:, r, hq * ND:(hq + 1) * ND, :], [tk("ccout", hd // HPG)], [tk("gath")], "gath")
                for dc in range(ND):
                    ts(Tst[:, dc, :], gath[:, 0, dc, 0:EW], pm[:, 0:1], None, ALU.mult, None, [tk("gath"), tc_], [tk("Tst")])
                    for r in (1, 2):
                        ts(aeff[:, dc, :], gath[:, r, dc, EW:EW + 1], pm[:, r:r + 1], pm[:, 4 + r:5 + r], ALU.mult, ALU.add,
                           [tk("gath"), tc_], [tk("aeff")])
                        ts(tmpS[:, dc, :], gath[:, r, dc, 0:EW], pm[:, r:r + 1], None, ALU.mult, None, [tk("gath"), tc_],
                           [tk("tmpS")])
                        stt(Tst[:, dc, :], Tst[:, dc, :], aeff[:, dc, :], tmpS[:, dc, :], ALU.mult, ALU.add,
                            [tk("Tst"), tk("aeff"), tk("tmpS")], [tk("Tst")])
                cp(Sst_b, Tst, [tk("Tst")], [tk("Sst_b")])
                if hg:
                    slot, tslot = gslots[0]
                    BS = [slice(b * BT, (b + 1) * BT) for b in range(NB)]
                    for b in range(NB):
                        gb = hg_gbanks[b]
                        mm_chain(banks[gb][:], [(slot[:, kc, :], hnT[:, kc, BS[b]]) for kc in range(KC)], [tslot, tk("hn", b)], [tb[gb]])
                        act(hg_gT[b], banks[gb][:], AF.Silu, [tb[gb]], [tk("gT4", b)])
                    for b in range(NB):
                        dma("sp", hg_ol[b], OL_d[:, hd:hd + 1, BS[b]], [tk("OL", hd, b)], [tk("ol2", b)], f"ol2{b}")
                        dma("sp", hg_qs[b], QS_d[:, hd:hd + 1, BS[b]], [tk("QS", hd, b)], [tk("qs2", b)], f"qs2{b}")
                    for b in range(NB):
                        cb = hg_cbanks[b]
                        mm_chain(banks[cb][:], [(Sst_b[:, 0, :], hg_qs[b][:, 0, :])], [tk("Sst_b"), tk("qs2", b)], [tb[cb]])
                    for b in range(NB):
                        cb = hg_cbanks[b]
                        tt(hg_ol[b][:, 0, :], hg_ol[b][:, 0, :], banks[cb][:], ALU.add, [tk("ol2", b), tb[cb]], [tk("ol2", b)])
                    for b in range(NB):
                        act(hg_sq[b], hg_ol[b][:, 0, :], AF.Square, [tk("ol2", b)], [tk("sq4", b)])
                    for b in range(NB):
                        gb = hg_gbanks[b]
                        mm_chain(banks[gb][:], [(ones_b, hg_sq[b])], [tk("sq4", b), tc_], [tb[gb]])
                    for b in range(NB):
                        gb = hg_gbanks[b]
                        act(hg_rs[b], banks[gb][:], AF.Sqrt, [tb[gb], tc_], [tk("rs4", b)], scale=1.0 / EW, bias=epsc)
                    for b in range(NB):
                        S.op("dve", lambda e, b=b: e.reciprocal(out=hg_rs[b], in_=hg_rs[b]), [tk("rs4", b)], [tk("rs4", b)])
                    for b in range(NB):
                        stt(hg_tmp[b], hg_ol[b][:, 0, :], gain[:, 0:1], hg_rs[b], ALU.mult, ALU.mult, [tk("ol2", b), tk("rs4", b), tc_],
                            [tk("tmp4", b)])
                    for b in range(NB):
                        tt(onT[:, hd, BS[b]], hg_tmp[b], hg_gT[b], ALU.mult, [tk("tmp4", b), tk("gT4", b)], [tk("on", b)])
                    continue
                for b in range(NB):
                    bsl = slice(b * BT, (b + 1) * BT)
                    dma("sp", ol, OL_d[:, hd * NE:(hd + 1) * NE, bsl], [tk("OL", hd, b)], [tk("ol2")], "ol2")
                    dma("sp", qs, QS_d[:, hd * ND:(hd + 1) * ND, bsl], [tk("QS", hd * ND + dc, b) for dc in range(ND)], [tk("qs2")], "qs2")
                    for ec in range(NE):
                        mm_chain(banks[4 + ec][:], [(Sst_b[:, dc, ec * 128:(ec + 1) * 128], qs[:, dc, :]) for dc in range(ND)],
                                 [tk("Sst_b"), tk("qs2")], [tb[4 + ec]])
                        tt(ol[:, ec, :], ol[:, ec, :], banks[4 + ec][:], ALU.add, [tk("ol2"), tb[4 + ec]], [tk("ol2")])
                        q = sqs[ec % 2]
                        act(q, ol[:, ec, :], AF.Square, [tk("ol2")], [tk("sqs", ec % 2)])
                        mm_chain(banks[3][:], [(ones_b, q)], [tk("sqs", ec % 2), tc_], [tb[3]], start=(ec == 0), stop=(ec == NE - 1))
                    rstd_from_ssq(banks[3][:], EW, rs, tk("rs"), [tb[3], tc_])
                    for ec in range(NE):
                        slot, tslot = gslots[ec]
                        bi = pj_i % 2
                        pj_i += 1
                        pj, tpj = banks[bi][:], tb[bi]
                        mm_chain(pj, [(slot[:, kc, :], hnT[:, kc, bsl]) for kc in range(KC)], [tslot, tk("hn", b)], [tpj])
                        g_ = gT[ec % 2]
                        act(g_, pj, AF.Silu, [tpj], [tk("gT", ec % 2)])
                        stt(tmpo, ol[:, ec, :], gain[:, ec:ec + 1], rs, ALU.mult, ALU.mult, [tk("ol2"), tk("rs"), tc_], [tk("tmpo")])
                        tt(onT[:, hd * NE + ec, bsl], tmpo, g_, ALU.mult, [tk("tmpo"), tk("gT", ec % 2)], [tk("on", b)])

        def stage_out(l):
            RC.reset()
            wsl = [RC.alloc([KC, 128], BF16) for _ in range(5)]
            W.set_slots(wsl, "p3")
            lim = stage_end[("out", l)]
            mT = RC.alloc([KC, 512], F32)
            sqs = [RC.alloc([512], BF16) for _ in range(2)]
            mark = RC.off
            i = 0
            for b in range(NB):
                bsl = slice(b * BT, (b + 1) * BT)
                for n in range(KC):
                    slot, tslot = W.get(lim)
                    bi = i % 2
                    pj, tpj = banks[bi][:], tb[bi]
                    mm_chain(pj, [(slot[:, hd, :], onT[:, hd, bsl]) for hd in range(KC)], [tslot, tk("on", b)], [tpj])
                    evac_with_ssq(mT, sqs, n, pj, tpj, n == 0, n == KC - 1, i)
                    i += 1
                RC.off = mark
                if int(os.environ.get('K_OUT', 9)) >= 2:
                    post_block(mT, b, 4 * l + 1, 4 * l + 2, False)

        def stage_mlp(l, last_layer):
            RC.reset()
            wsl = [RC.alloc([KC, 128], BF16) for _ in range(4)]
            W.set_slots(wsl, "p4")
            lim = stage_end[("mlp", l)]
            mT = RC.alloc([KC, 512], F32)
            sqs = [RC.alloc([512], BF16) for _ in range(2)]
            rl = [RC.alloc([512], F32) for _ in range(2)]
            mark = RC.off
            cnt = [0]

            def up(b, f0=0, f1=64):
                bsl = slice(b * BT, (b + 1) * BT)
                for fu in range(f0, f1):
                    i = cnt[0]
                    slot, tslot = W.get(lim)
                    pj, tpj = banks[i % 2][:], tb[i % 2]
                    mm_chain(pj, [(slot[:, kc, :], hnT[:, kc, bsl]) for kc in range(KC)], [tslot, tk("hn", b)], [tpj])
                    r_ = rl[i % 2]
                    act(r_, pj, AF.Relu, [tpj], [tk("rl", i % 2)])
                    tt(uT[:, fu, :], r_, r_, ALU.mult, [tk("rl", i % 2)], [tk("u", fu)])
                    cnt[0] += 1

            def down(b):
                for n in range(KC):
                    i = cnt[0]
                    pj, tpj = banks[i % 2][:], tb[i % 2]
                    for g in range(4):
                        slot, tslot = W.get(lim)
                        mm_chain(pj, [(slot[:, j, :], uT[:, g * 16 + j, :]) for j in range(16)],
                                 [tslot] + [tk("u", g * 16 + j) for j in range(16)], [tpj], start=(g == 0), stop=(g == 3))
                    evac_with_ssq(mT, sqs, n, pj, tpj, n == 0, n == KC - 1, i)
                    cnt[0] += 1

            up(0)
            for b in range(NB):
                down(b)
                RC.off = mark
                post_block(mT, b, 4 * l + 3, None if last_layer else 4 * (l + 1), last_layer, defer=(up if b + 1 < NB else None), defer_arg=b + 1)

        def run_stages():
            stage_pre()
            S.barrier()
            if dbg == "pre":
                dump(lambda kc: hnT[:, kc, :], KC, True)
                return
            for l in range(n_layers):
                stage_mix1(l)
                S.barrier()
                if os.environ.get('K_STOP') == 'mix1':
                    dump(lambda kc: hnT[:, kc, :], KC, True)
                    return
                stage_mix2(l)
                S.barrier()
                if dbg == f"mix{l}":
                    dump(lambda kc: onT[:, kc, :], KC, True)
                    return
                stage_out(l)
                S.barrier()
                if dbg == f"out{l}":
                    dump(lambda kc: hnT[:, kc, :], KC, True)
                    return
                stage_mlp(l, l == n_layers - 1)
                S.barrier()

        run_stages()
        S.barrier()
        with nc.Block() as block:
            S.emit(block)
    return nc


def _tile_w(Wm, ncols):
    K, N = Wm.shape
    return np.ascontiguousarray(Wm.reshape(K // 128, 128, N // ncols, ncols).transpose(2, 1, 0, 3))


def _pcol(v):
    return np.ascontiguousarray(v.reshape(-1, 128).T)


_CACHE = {}


def prepare_inputs(inputs):
    f = lambda k: np.asarray(inputs[k], dtype=np.float32)
    nwl = []
    for l in range(2):
        for k in ("norm_mix_pre", "norm_mix_post", "norm_mlp_pre", "norm_mlp_post"):
            nwl.append(_pcol(f(k)[l]))
    nw = np.ascontiguousarray(np.stack(nwl, 1))
    lbl = np.ascontiguousarray(np.stack([_pcol(f("hgrn_lb_logits")[r]) for r in range(3)], 1))
    gw = f("gla_w_in")[0]
    shared = dict(
        nw=nw, lbl=lbl,
        hgn=np.ascontiguousarray(f("hgrn_norm")[0].reshape(128, 1)),
        glan=_pcol(f("gla_norm")[0]),
        bgk=_pcol(f("gla_b_gk")[0]),
        wgk=np.ascontiguousarray(f("gla_w_gk")[0]),
        wr=np.ascontiguousarray(gw[:, 6144:6160].reshape(16, 128, 16).transpose(1, 0, 2)),
        hg_in=_tile_w(f("hgrn_w_in")[0], 128),
        hg_out=_tile_w(f("hgrn_w_out")[0], 128),
        gl_in=_tile_w(np.ascontiguousarray(gw[:, :6144]), 128),
        gl_out=_tile_w(f("gla_w_out")[0], 128),
        w_up=np.stack([_tile_w(f("mlp_w_up")[l], 128) for l in range(2)]),
        w_dn=np.stack([_tile_w(f("mlp_w_down")[l], 128) for l in range(2)]),
    )
    x = f("x")
    in_maps = []
    for c in range(NCORES):
        b, p = c // 4, c % 4
        pmv = np.zeros((128, 8), np.float32)
        for r in range(4):
            pmv[:, r] = 1.0 if r < p else 0.0
            pmv[:, 4 + r] = 0.0 if r < p else 1.0
        d = dict(shared)
        d["x"] = np.ascontiguousarray(x[b, p * TC:(p + 1) * TC, :])
        d["pm"] = pmv
        in_maps.append(d)
    return in_maps


def kernel(**inputs):
    in_maps = prepare_inputs(inputs)
    if "nc" not in _CACHE:
        _CACHE["nc"] = build_program()
    res = run_bass_kernel_spmd(_CACHE["nc"], in_maps, core_ids=list(range(NCORES)))
    x = inputs["x"]
    out = np.empty(x.shape, np.float32)
    for c in range(NCORES):
        b, p = c // 4, c % 4
        out[b, p * TC:(p + 1) * TC, :] = res.results[c]["out"]
    return out
```

```python
from contextlib import ExitStack
import os
import math
import numpy as np
import concourse.bass as bass
import concourse.mybir as mybir
from concourse.bass_utils import run_bass_kernel_spmd

F32 = mybir.dt.float32
BF16 = mybir.dt.bfloat16
AF = mybir.ActivationFunctionType
ALU = mybir.AluOpType

NCORES = 8
D = 2048
KC = 16
TC = 2048
NB = 4
BT = 512
DFF = 8192
EPS = 1e-6
ENGINES = ("pe", "act", "dve", "pool", "sp")


class Sem:
    __slots__ = ("h", "v")

    def __init__(self, h):
        self.h = h
        self.v = 0


class Tok:
    __slots__ = ("w", "r")

    def __init__(self):
        self.w = None
        self.r = {}


class Sched:
    def __init__(self, sem_alloc):
        self.sem_alloc = sem_alloc
        self.q = {e: [] for e in ENGINES}
        self.esem = {e: Sem(sem_alloc("c_" + e)) for e in ENGINES}
        self.waited = {e: {} for e in ENGINES}
        self.allsems = list(self.esem.values())
        self.toks = {}

    def tk(self, *key):
        t = self.toks.get(key)
        if t is None:
            t = self.toks[key] = Tok()
        return t

    def new_dma_sem(self, name):
        s = Sem(self.sem_alloc(name))
        self.allsems.append(s)
        return s

    def _waits(self, eng, deps, skip_own):
        own = self.esem[eng]
        waits = []
        wd = self.waited[eng]
        for s, v in deps.items():
            if s is own and (skip_own or v > own.v):
                continue
            if wd.get(s, 0) < v:
                waits.append((s, v))
                wd[s] = v
        return waits

    def op(self, eng, fn, reads=(), writes=(), dma_sem=None, inc=True, incv=None, record=True,
           skip_own=None):
        if skip_own is None:
            skip_own = (eng == "pe")
        deps = {}
        for t in reads:
            if t.w is not None and deps.get(t.w[0], 0) < t.w[1]:
                deps[t.w[0]] = t.w[1]
        for t in writes:
            if t.w is not None and deps.get(t.w[0], 0) < t.w[1]:
                deps[t.w[0]] = t.w[1]
            for s, v in t.r.items():
                if deps.get(s, 0) < v:
                    deps[s] = v
        waits = self._waits(eng, deps, skip_own)
        if dma_sem is not None:
            csem, iv = dma_sem, (16 if incv is None else incv)
        else:
            csem, iv = self.esem[eng], 1
        if inc:
            csem.v += iv
            val = csem.v
        else:
            val = csem.v + iv
        self.q[eng].append((waits, fn, csem if inc else None, iv))
        if record:
            self.record(reads, writes, (csem, val))
        return (csem, val)

    def record(self, reads, writes, sv):
        csem, val = sv
        for t in reads:
            if t.r.get(csem, 0) < val:
                t.r[csem] = val
        for t in writes:
            t.w = (csem, val)
            t.r = {}

    def barrier(self):
        svs = [(s, s.v) for s in self.allsems if s.v > 0]
        for e in ENGINES:
            waits = self._waits(e, dict(svs), False)
            if waits:
                self.q[e].append((waits, None, None, 0))
        for e in ENGINES:
            if self.esem[e].v > 12000:
                s = Sem(self.sem_alloc("c_" + e))
                self.esem[e] = s
                self.allsems.append(s)

    def emit(self, block):
        def run(eh, items):
            for waits, fn, csem, iv in items:
                for s, v in waits:
                    eh.wait_ge(s.h, v)
                if fn is None:
                    continue
                ins = fn(eh)
                if csem is not None:
                    ins.then_inc(csem.h, iv)

        m = {"pe": block.tensor, "act": block.scalar, "dve": block.vector,
             "pool": block.gpsimd, "sp": block.sync}
        for e in ENGINES:
            items = self.q[e]
            if items:
                m[e](lambda eh, items=items: run(eh, items))


class Region:
    def __init__(self, t, nwords):
        self.t, self.n, self.off = t, nwords, 0

    def reset(self):
        self.off = 0

    def alloc(self, free_shape, dt):
        nel = int(np.prod(free_shape))
        nbytes = nel * (4 if dt == F32 else 2)
        words = (nbytes + 31) // 32 * 8
        assert self.off + words <= self.n, ("region overflow", self.off, words, self.n)
        v = self.t[:, self.off:self.off + words]
        self.off += words
        if dt != F32:
            v = v.bitcast(dt)
        v = v[:, 0:nel]
        if len(free_shape) == 2:
            v = v.rearrange("p (a b) -> p a b", b=free_shape[1])
        elif len(free_shape) == 3:
            v = v.rearrange("p (a b c) -> p a b c", b=free_shape[1], c=free_shape[2])
        return v


LAYERS = [
    dict(kind="hgrn", NH=16, ND=1, NE=1, escale=1.0),
    dict(kind="gla", NH=4, ND=2, NE=4, escale=1.0 / 16.0),
]


def build_program(n_layers=2, dbg=None, ncores=NCORES):
    nc = bass.Bass("TRN2", target_bir_lowering=False)
    dt_in = lambda name, shape: nc.dram_tensor(name, shape, F32, kind="ExternalInput").ap()
    x_d = dt_in("x", [TC, D])
    pm_d = dt_in("pm", [128, 8])
    nw_d = dt_in("nw", [128, 8, KC])
    lbl_d = dt_in("lbl", [128, 3, KC])
    hgn_d = dt_in("hgn", [128, 1])
    glan_d = dt_in("glan", [128, 4])
    bgk_d = dt_in("bgk", [128, 8])
    wgk_d = dt_in("wgk", [16, 1024])
    wr_d = dt_in("wr", [128, KC, 16])
    hg_in_d = dt_in("hg_in", [64, 128, KC, 128])
    hg_out_d = dt_in("hg_out", [16, 128, KC, 128])
    gl_in_d = dt_in("gl_in", [48, 128, KC, 128]) if n_layers > 1 else None
    gl_out_d = dt_in("gl_out", [16, 128, KC, 128]) if n_layers > 1 else None
    up_d = dt_in("w_up", [n_layers, 64, 128, KC, 128])
    dn_d = dt_in("w_dn", [n_layers, 16, 128, 64, 128])
    out_d = nc.dram_tensor("out", [TC, D], F32, kind="ExternalOutput").ap()
    dbg_d = nc.dram_tensor("dbg", [128, KC, TC], F32, kind="ExternalOutput").ap() if dbg else None
    H_d = nc.dram_tensor("Hs", [128, KC, TC], F32).ap()
    OL_d = nc.dram_tensor("OLs", [128, KC, TC], F32).ap()
    QS_d = nc.dram_tensor("QSs", [128, KC, TC], BF16).ap()
    CCG = {0: 8, 1: 1}
    ccw = {0: 8 * 129, 1: 2 * 513}
    ccn = {0: 2, 1: 4}
    cc_in = {l: [nc.dram_tensor(f"cc_in{l}_{g}", [128, ccw[l]], F32) for g in range(ccn[l])] for l in range(2)}
    cc_out = {l: [nc.dram_tensor(f"cc_out{l}_{g}", [4 * 128, ccw[l]], F32) for g in range(ccn[l])] for l in range(2)}

    with ExitStack() as es:
        sb = lambda name, shape, dt: es.enter_context(nc.sbuf_tensor(name, shape, dt))
        S = Sched(lambda name: es.enter_context(nc.semaphore(name)))
        tk = S.tk

        RA = sb("RA", [128, KC, TC], BF16)
        RB = sb("RB", [128, KC * TC // 2], F32)
        NCW = 18176
        RCt = sb("RC", [128, NCW], F32)
        RC = Region(RCt, NCW)
        cst = sb("cst", [128, 1792], F32)
        CR = Region(cst, 1792)
        banks = [es.enter_context(nc.psum_tensor(f"bank{i}", [128, 512], F32)) for i in range(8)]
        tb = [tk("bank", i) for i in range(8)]

        RBb = RB[:].bitcast(BF16)
        onT = RBb.rearrange("p (a t) -> p a t", t=TC)
        uT = RBb.rearrange("p (a t) -> p a t", t=BT)
        slots1 = RBb.rearrange("p (s a c) -> p s a c", a=KC, c=128)
        hnT = RA

        def act(out, in_, func, reads, writes, **kw):
            S.op("act", lambda e: e.activation(out=out, in_=in_, func=func, **kw), reads, writes)

        def tt(out, in0, in1, op, reads, writes, eng="dve"):
            S.op(eng, lambda e: e.tensor_tensor(out=out, in0=in0, in1=in1, op=op), reads, writes)

        def ts(out, in0, s1, s2, op0, op1, reads, writes, eng="dve"):
            if s2 is None:
                S.op(eng, lambda e: e.tensor_scalar(out=out, in0=in0, scalar1=s1, scalar2=None, op0=op0), reads, writes)
            else:
                S.op(eng, lambda e: e.tensor_scalar(out=out, in0=in0, scalar1=s1, scalar2=s2, op0=op0, op1=op1), reads, writes)

        def stt(out, in0, scalar, in1, op0, op1, reads, writes):
            S.op("dve", lambda e: e.scalar_tensor_tensor(out=out, in0=in0, scalar=scalar, in1=in1, op0=op0, op1=op1),
                 reads, writes)

        def cp(out, in_, reads, writes, eng="dve"):
            S.op(eng, lambda e: e.tensor_copy(out=out, in_=in_), reads, writes)

        def mm_chain(out, pairs, reads, writes, start=True, stop=True):
            n = len(pairs)
            for i, (l, r) in enumerate(pairs):
                st = start and i == 0
                sp_ = stop and i == n - 1
                fn = (lambda e, l=l, r=r, st=st, sp_=sp_: e.matmul(out, lhsT=l, rhs=r, start=st, stop=sp_))
                if n == 1:
                    S.op("pe", fn, reads, writes)
                elif i == 0:
                    S.op("pe", fn, reads, writes, inc=False, record=False)
                elif i < n - 1:
                    S.op("pe", fn, inc=False, record=False)
                else:
                    sv = S.op("pe", fn, record=False)
                    S.record(reads, writes, sv)

        def transpose(out, in_, ident, reads, writes):
            S.op("pe", lambda e: e.transpose(out=out, in_=in_, identity=ident), reads, writes)

        dsem = {}

        def dma(eng, out, in_, reads, writes, semname):
            s = dsem.get(semname)
            if s is None:
                s = dsem[semname] = S.new_dma_sem("d_" + semname)
            return S.op(eng, lambda e: e.dma_start(out=out, in_=in_), reads, writes, dma_sem=s)

        ident_f = CR.alloc([128], F32)
        ident_b = CR.alloc([128], BF16)
        ones_b = CR.alloc([128], BF16)
        mask4 = CR.alloc([4, 128], F32)
        pm = CR.alloc([8], F32)
        nw = CR.alloc([8, KC], F32)
        lbt = CR.alloc([3, KC], F32)
        lb = CR.alloc([KC], F32)
        oml = CR.alloc([KC], F32)
        lnoml = CR.alloc([KC], F32)
        hgn = CR.alloc([1], F32)
        glan = CR.alloc([4], F32)
        bgk = CR.alloc([8], F32)
        epsc = CR.alloc([1], F32)
        wgk_b = CR.alloc([1024], BF16)
        wr_b = CR.alloc([KC, 16], BF16)
        tmpc = CR.alloc([KC], F32)
        tc_ = tk("consts")

        S.op("pool", lambda e: e.memset(ident_f, 1.0), (), [tc_])
        S.op("pool", lambda e: e.affine_select(out=ident_f, in_=ident_f, pattern=[[-1, 128]], compare_op=ALU.is_equal,
                                               fill=0.0, base=0, channel_multiplier=1), [tc_], [tc_])
        S.op("pool", lambda e: e.tensor_copy(out=ident_b, in_=ident_f), [tc_], [tc_])
        S.op("pool", lambda e: e.memset(ones_b, 1.0), (), [tc_])
        S.op("pool", lambda e: e.memset(epsc, EPS), (), [tc_])
        m0 = mask4[:, 0, :]
        S.op("pool", lambda e: e.memset(m0, 1.0), (), [tc_])
        S.op("pool", lambda e: e.affine_select(out=m0, in_=m0, pattern=[[1, 128]], compare_op=ALU.is_ge,
                                               fill=0.0, base=0, channel_multiplier=-1), [tc_], [tc_])
        S.op("pool", lambda e: e.memset(mask4[0:64, 0, 64:128], 0.0), [tc_], [tc_])
        for i in range(1, 4):
            S.op("pool", lambda e, i=i: e.tensor_copy(out=mask4[:, i, :], in_=mask4[:, 0, :]), [tc_], [tc_])
        for dst, src in ((pm, pm_d), (nw, nw_d), (lbt, lbl_d), (hgn, hgn_d), (glan, glan_d), (bgk, bgk_d)):
            dma("sp", dst, src, (), [tc_], "const")
        dma("pool", wgk_b[0:16, :], wgk_d, (), [tc_], "const2")
        dma("pool", wr_b, wr_d, (), [tc_], "const2")
        act(lbt, lbt, AF.Exp, [tc_], [tc_])
        tt(tmpc, lbt[:, 0, :], lbt[:, 1, :], ALU.add, [tc_], [tc_])
        tt(tmpc, tmpc, lbt[:, 2, :], ALU.add, [tc_], [tc_])
        S.op("dve", lambda e: e.reciprocal(out=tmpc, in_=tmpc), [tc_], [tc_])
        tt(lb, lbt[:, 0, :], tmpc, ALU.mult, [tc_], [tc_])
        ts(oml, lb, -1.0, 1.0, ALU.mult, ALU.add, [tc_], [tc_])
        act(lnoml, oml, AF.Ln, [tc_], [tc_])
        S.barrier()

        class WStream:
            def __init__(self):
                self.plan = []
                self.pos = 0
                self.issued = 0
                self.slots = None
                self.base = 0

            def set_slots(self, slot_aps, name):
                self.slots = slot_aps
                self.name = name
                self.base = self.pos
                self.issued = self.pos

            def get(self, limit, hold=None):
                i = self.pos
                self.pos += 1
                if hold is None:
                    hold = i
                n = min(hold + len(self.slots), limit, len(self.plan))
                while self.issued < n:
                    j = self.issued
                    sj = (j - self.base) % len(self.slots)
                    dma("pool", self.slots[sj], self.plan[j], (), [tk("slot", self.name, sj)], f"w{self.name}{sj}")
                    self.issued += 1
                si = (i - self.base) % len(self.slots)
                return self.slots[si], tk("slot", self.name, si)

        W = WStream()
        stage_end = {}

        def plan_units():
            for l in range(n_layers):
                L = LAYERS[l]
                hg = L["kind"] == "hgrn"
                for hd in range(L["NH"]):
                    if hg:
                        W.plan += [hg_in_d[hd], hg_in_d[16 + hd], hg_in_d[32 + hd]]
                    else:
                        W.plan += [gl_in_d[2 * hd + i] for i in range(2)] + [gl_in_d[8 + 2 * hd + i] for i in range(2)] \
                            + [gl_in_d[16 + 4 * hd + i] for i in range(4)]
                stage_end[("mix1", l)] = len(W.plan)
                for hd in range(L["NH"]):
                    W.plan += [(hg_in_d[48 + hd] if hg else gl_in_d[32 + 4 * hd + ec]) for ec in range(L["NE"])]
                stage_end[("mix2", l)] = len(W.plan)
                wd = hg_out_d if l == 0 else gl_out_d
                for b in range(NB):
                    W.plan += [wd[n] for n in range(KC)]
                stage_end[("out", l)] = len(W.plan)
                for b in range(NB):
                    W.plan += [up_d[l, fu] for fu in range(64)]
                    W.plan += [dn_d[l, n, :, g * 16:(g + 1) * 16, :] for n in range(KC) for g in range(4)]
                stage_end[("mlp", l)] = len(W.plan)

        plan_units()

        def rstd_from_ssq(ssq_bank, n_feat, rs, rs_t, reads):
            act(rs, ssq_bank, AF.Sqrt, reads, [rs_t], scale=1.0 / n_feat, bias=epsc)
            S.op("dve", lambda e: e.reciprocal(out=rs, in_=rs), [rs_t], [rs_t])

        def cp_any(out, in_, reads, writes, i):
            if i % 2 == 0:
                cp(out, in_, reads, writes)
            else:
                act(out, in_, AF.Copy, reads, writes)

        def finish_block(mT, b, layer_next, last):
            tm = [tk("m", kc) for kc in range(KC)]
            bsl = slice(b * BT, (b + 1) * BT)
            if last:
                ostg = [RC.alloc([512], F32) for _ in range(2)]
                i = 0
                pt = banks[2][:].rearrange("p (a c) -> p a c", c=128)
                for tt_ in range(4):
                    for g in range(4):
                        for j in range(4):
                            kc = g * 4 + j
                            transpose(pt[:, j, :], mT[:, kc, tt_ * 128:(tt_ + 1) * 128], ident_f, [tm[kc], tc_], [tb[2]])
                        o = ostg[i % 2]
                        cp_any(o, banks[2][:], [tb[2]], [tk("ostg", i % 2)], i)
                        r0 = b * BT + tt_ * 128
                        dma("sp", out_d[r0:r0 + 128, g * 512:(g + 1) * 512], o, [tk("ostg", i % 2)], (), f"ostg{i % 2}")
                        i += 1
                return
            for g in range(4):
                dma("sp", H_d[:, g * 4:(g + 1) * 4, bsl], mT[:, g * 4:(g + 1) * 4, :], [tm[g * 4 + j] for j in range(4)],
                    [tk("H", g, b)], f"mst{g}")
            if layer_next is None:
                return
            sqs = [RC.alloc([512], BF16) for _ in range(2)]
            for kc in range(KC):
                q = sqs[kc % 2]
                act(q, mT[:, kc, :], AF.Square, [tm[kc]], [tk("sqs", kc % 2)])
                mm_chain(banks[3][:], [(ones_b, q)], [tk("sqs", kc % 2), tc_], [tb[3]], start=(kc == 0), stop=(kc == KC - 1))
            rs = RC.alloc([512], F32)
            rstd_from_ssq(banks[3][:], D, rs, tk("rs"), [tb[3], tc_])
            for kc in range(KC):
                stt(hnT[:, kc, bsl], mT[:, kc, :], nw[:, layer_next, kc:kc + 1], rs, ALU.mult, ALU.mult,
                    [tm[kc], tk("rs"), tc_], [tk("hn", b)])

        def post_block(mT, b, wpost_idx, layer_next, last, defer=None, defer_arg=None):
            tm = [tk("m", kc) for kc in range(KC)]
            bsl = slice(b * BT, (b + 1) * BT)
            rsm = RC.alloc([512], F32)
            rstd_from_ssq(banks[3][:], D, rsm, tk("rsm"), [tb[3], tc_])
            hch = [RC.alloc([2, 512], F32) for _ in range(2)]

            for i, kc in enumerate(range(0, KC, 2)):
                hc = hch[i % 2]
                th = tk("hch", i % 2)
                dma("sp", hc, H_d[:, kc:kc + 2, bsl], [tk("H", kc // 4, b)], [th], f"hch{i % 2}")
                for j in range(2):
                    stt(mT[:, kc + j, :], mT[:, kc + j, :], nw[:, wpost_idx, kc + j:kc + j + 1], rsm, ALU.mult, ALU.mult,
                        [tm[kc + j], tk("rsm"), tc_], [tm[kc + j]])
                    tt(mT[:, kc + j, :], mT[:, kc + j, :], hc[:, j, :], ALU.add, [tm[kc + j], th], [tm[kc + j]])
            if defer is not None:
                defer(defer_arg, 0, 32)
            if int(os.environ.get('K_OUT', 9)) >= 3:
                finish_block(mT, b, layer_next, last)
            if defer is not None:
                defer(defer_arg, 32, 64)

        def evac_with_ssq(mT, sqs, kc, pj, tpj, first, lastc, i):
            q = sqs[i % 2]
            cp(mT[:, kc, :], pj, [tpj], [tk("m", kc)])
            act(q, mT[:, kc, :], AF.Square, [tk("m", kc)], [tk("sqs", i % 2)])
            mm_chain(banks[3][:], [(ones_b, q)], [tk("sqs", i % 2), tc_], [tb[3]], start=first, stop=lastc)

        def dump(src_fn, n, dt_is_bf16):
            RC.reset()
            stg = RC.alloc([TC], F32)
            for kc in range(n):
                cp(stg, src_fn(kc), [], [tk("dstg")])
                dma("sp", dbg_d[:, kc, :], stg, [tk("dstg")], (), "dstg")

        def stage_pre():
            RC.reset()
            mT = RC.alloc([KC, 512], F32)
            xts = [RC.alloc([D], F32) for _ in range(2)]
            pt = banks[2][:].rearrange("p (a c) -> p a c", c=128)
            for b in range(NB):
                for tt_ in range(4):
                    ti = b * 4 + tt_
                    xt = xts[ti % 2]
                    txt = tk("xt", ti % 2)
                    dma("sp", xt, x_d[ti * 128:(ti + 1) * 128, :], (), [txt], f"xt{ti % 2}")
                    for g in range(4):
                        for j in range(4):
                            kc = g * 4 + j
                            transpose(pt[:, j, :], xt[:, kc * 128:(kc + 1) * 128], ident_f, [txt, tc_], [tb[2]])
                        tms = [tk("m", g * 4 + j) for j in range(4)]
                        cp_any(mT[:, g * 4:(g + 1) * 4, tt_ * 128:(tt_ + 1) * 128], pt, [tb[2]], tms, g)
                finish_block(mT, b, 0, False)

        def stage_mix1(l):
            L = LAYERS[l]
            NH, ND, NE, esc = L["NH"], L["ND"], L["NE"], L["escale"]
            hg = L["kind"] == "hgrn"
            EW = NE * 128
            RC.reset()
            W.set_slots([slots1[:, i] for i in range(16)], "p1")
            lim = stage_end[("mix1", l)]
            T = lambda dt=F32: RC.alloc([512], dt)
            NSET = 2 if hg else 1
            qTs = [[T() for _ in range(ND)] for _ in range(NSET)]
            kTs = [[T() for _ in range(ND)] for _ in range(NSET)]
            lfs = [[T() for _ in range(ND)] for _ in range(NSET)]
            bs = [T() for _ in range(ND)]
            bp = T()
            ex = [T() for _ in range(2)]
            qt_b = [T(BF16) for _ in range(ND)]
            kt_b = [T(BF16) for _ in range(ND)]
            qs_b = [[T(BF16) for _ in range(ND)] for _ in range(2)]
            vTs = [[T(BF16) for _ in range(NE)] for _ in range(NSET)]
            vtm = RC.alloc([4, EW], BF16)
            ktm2 = [RC.alloc([4, ND * 128], BF16) for _ in range(2)]
            sT4 = RC.alloc([4, 128], BF16)
            Sfull = RC.alloc([ND, EW + 1], F32)
            Sst = Sfull[:, :, 0:EW]
            atot = Sfull[:, :, EW:EW + 1]
            NSB = 8 if hg else 4
            Sbf = [RC.alloc([ND, EW], BF16) for _ in range(NSB)]
            oloc = [RC.alloc([NE, 512], F32) for _ in range(2 if hg else 1)]
            carry = RC.alloc([ND, 2], F32)
            dd = RC.alloc([ND, 8], F32)
            Ac = RC.alloc([ND, 8], F32)
            rT = None if hg else RC.alloc([TC], BF16)
            ones_f = RC.alloc([512], F32)
            S.op("dve", lambda e: e.memset(ones_f, 1.0), (), [tk("ones_f")])
            S.op("dve", lambda e: e.memset(ktm2[0][64:128], 0.0), (), [tk("ktm")])
            S.op("dve", lambda e: e.memset(ktm2[1][0:64], 0.0), (), [tk("ktm")])
            pj_i = [0]
            hn_rhs = lambda kc, b: hnT[:, kc, b * BT:(b + 1) * BT]
            ptb = banks[2][:].bitcast(BF16).rearrange("p (a c) -> p a c", c=128)
            sc4 = banks[3][:].rearrange("p (a c) -> p a c", c=128)
            cw = EW + 1
            HPG = CCG[l]
            cins = [c_.ap().rearrange("p (u w) -> p u w", w=cw) for c_ in cc_in[l]]

            if not hg:
                for b in range(NB):
                    pj, tpj = banks[b % 2][0:16, :], tb[b % 2]
                    mm_chain(pj, [(wr_b[:, kc, :], hn_rhs(kc, b)) for kc in range(KC)], [tc_, tk("hn", b)], [tpj])
                    cp(rT[0:16, b * BT:(b + 1) * BT], pj, [tpj], [tk("rT")])

            NHH = min(NH, int(os.environ.get('K_HEADS', NH)))
            slots_by_head = {}

            def emit_P(it):
                hd_, b_ = it // NB, it % NB
                if hd_ not in slots_by_head:
                    hold_ = W.pos
                    slots_by_head[hd_] = {key: [W.get(lim, hold_) for _ in range(n)] for key, n in (("q", ND), ("k", ND), ("v", NE))}
                for key, bank in (("q", 5), ("k", 6), ("v", 7)):
                    slot, tslot = slots_by_head[hd_][key][0]
                    mm_chain(banks[bank][:], [(slot[:, kc, :], hn_rhs(kc, b_)) for kc in range(KC)], [tslot, tk("hn", b_)], [tb[bank]])

            def emit_EV(s_):
                act(qTs[s_][0], banks[5][:], AF.Silu, [tb[5]], [tk("qT", 0, s_)])
                act(lfs[s_][0], banks[6][:], AF.Sigmoid, [tb[6]], [tk("lf", 0, s_)])
                act(kTs[s_][0], banks[6][:], AF.Sigmoid, [tb[6]], [tk("kT", 0, s_)], scale=-1.0)
                act(vTs[s_][0], banks[7][:], AF.Copy, [tb[7]], [tk("vT", 0, s_)])

            for hd in range(NHH):
                if not hg:
                    hold = W.pos
                    slots_h = {key: [W.get(lim, hold) for _ in range(n)] for key, n in (("q", ND), ("k", ND), ("v", NE))}
                for b in range(NB):
                    bsl = slice(b * BT, (b + 1) * BT)
                    par = (hd * NB + b) % 2
                    it = hd * NB + b
                    sx = it % NSET
                    qT, kT, lf, vT = qTs[sx], kTs[sx], lfs[sx], vTs[sx]
                    if hg:
                        if it == 0:
                            emit_P(0)
                            emit_EV(0)
                        if it + 1 < NHH * NB:
                            emit_P(it + 1)

                    def proj(slot_t, evac):
                        slot, tslot = slot_t
                        bi = pj_i[0] % 2
                        pj_i[0] += 1
                        pj, tpj = banks[bi][:], tb[bi]
                        mm_chain(pj, [(slot[:, kc, :], hn_rhs(kc, b)) for kc in range(KC)], [tslot, tk("hn", b)], [tpj])
                        evac(pj, tpj)

                    CUT = int(os.environ.get('K_CUT', 9))
                    for dc in range(ND):
                        col = hd * ND + dc
                        if hg:
                            act(lf[dc], lf[dc], AF.Ln, [tk("lf", dc, sx), tc_], [tk("lf", dc, sx)], scale=oml[:, col:col + 1],
                                bias=lb[:, col:col + 1])
                        else:
                            proj(slots_h["q"][dc], lambda pj, tpj: act(qT[dc], pj, AF.Copy, [tpj], [tk("qT", dc, sx)], scale=1.0 / 16.0))
                            proj(slots_h["k"][dc], lambda pj, tpj: act(kT[dc], pj, AF.Copy, [tpj], [tk("kT", dc, sx)]))
                            bi = pj_i[0] % 2
                            pj_i[0] += 1
                            pj, tpj = banks[bi][:], tb[bi]
                            c0 = hd * 256 + dc * 128
                            mm_chain(pj, [(wgk_b[0:16, c0:c0 + 128], rT[0:16, bsl])], [tc_, tk("rT")], [tpj])
                            act(lf[dc], pj, AF.Sigmoid, [tpj, tc_], [tk("lf", dc, sx)], bias=bgk[:, col:col + 1])
                            act(lf[dc], lf[dc], AF.Ln, [tk("lf", dc, sx)], [tk("lf", dc, sx)])
                        if CUT <= 1:
                            continue
                        init = 0.0 if b == 0 else carry[:, dc, 0:1]
                        S.op("dve", lambda e, dc=dc, init=init, lfd=lf[dc]: e.tensor_tensor_scan(
                            out=bs[dc], data0=ones_f, data1=lfd, initial=init, op0=ALU.mult, op1=ALU.add),
                            [tk("lf", dc, sx), tk("ones_f"), tk("carry", dc)], [tk("bs", dc)])
                        bs3 = bs[dc].rearrange("p (c k) -> p c k", k=64)
                        rcol = bs3[:, :, 63]
                        if b == 0:
                            cp(dd[:, dc, 0:1], rcol[:, 0:1], [tk("bs", dc)], [tk("dd", dc)])
                        else:
                            tt(dd[:, dc, 0:1], rcol[:, 0:1], carry[:, dc, 0:1], ALU.subtract, [tk("bs", dc), tk("carry", dc)],
                               [tk("dd", dc)])
                        tt(dd[:, dc, 1:8], rcol[:, 1:8], rcol[:, 0:7], ALU.subtract, [tk("bs", dc)], [tk("dd", dc)])
                        act(Ac[:, dc, :], dd[:, dc, :], AF.Exp, [tk("dd", dc)], [tk("Ac", dc)], scale=esc)
                        cp(carry[:, dc, 0:1], rcol[:, 7:8], [tk("bs", dc)], [tk("carry", dc)])
                        bp3 = bp.rearrange("p (c k) -> p c k", k=64)
                        tt(bp3, bs3, bs3[:, :, 63:64].to_broadcast([128, 8, 64]), ALU.subtract, [tk("bs", dc)], [tk("bp")])
                        act(ex[0], bp, AF.Exp, [tk("bp")], [tk("ex", 0)], scale=esc)
                        tt(qt_b[dc], qT[dc], ex[0], ALU.mult, [tk("qT", dc, sx), tk("ex", 0)], [tk("qt_b", dc)])
                        if hg:
                            act(ex[1], bp, AF.Exp, [tk("bp"), tc_], [tk("ex", 1)], scale=-esc, bias=lnoml[:, col:col + 1])
                        else:
                            act(ex[1], bp, AF.Exp, [tk("bp")], [tk("ex", 1)], scale=-esc)
                        tt(kt_b[dc], kT[dc], ex[1], ALU.mult, [tk("kT", dc, sx), tk("ex", 1)], [tk("kt_b", dc)])
                        act(ex[0], bs[dc], AF.Exp, [tk("bs", dc)], [tk("ex", 0)], scale=esc)
                        tt(qs_b[par][dc], qT[dc], ex[0], ALU.mult, [tk("qT", dc, sx), tk("ex", 0)], [tk("qs_b", par, dc)])
                        dma("sp", QS_d[:, col, bsl], qs_b[par][dc], [tk("qs_b", par, dc)], [tk("QS", col, b)], f"qs{par}{dc}")
                        if b == NB - 1:
                            act(atot[:, dc, :], rcol[:, 7:8], AF.Exp, [tk("bs", dc)], [tk("atot")], scale=esc)
                        if CUT <= 2:
                            continue
                        for j in range(4):
                            transpose(ptb[:, j, :], kt_b[dc][:, j * 128:(j + 1) * 128], ident_b, [tk("kt_b", dc), tc_], [tb[2]])
                        cp(ktm2[0][0:64, :, dc * 128:(dc + 1) * 128], ptb[0:64, 0:4, :], [tb[2]], [tk("ktm")])
                        cp(ktm2[1][64:128, :, dc * 128:(dc + 1) * 128], ptb[64:128, 0:4, :], [tb[2]], [tk("ktm")])
                    if CUT <= 3:
                        continue
                    for ec in range(NE):
                        if not hg:
                            proj(slots_h["v"][ec], lambda pj, tpj: act(vT[ec], pj, AF.Copy, [tpj], [tk("vT", ec, sx)]))
                        for j in range(4):
                            transpose(ptb[:, j, :], vT[ec][:, j * 128:(j + 1) * 128], ident_b, [tk("vT", ec, sx), tc_], [tb[2]])
                        cp(vtm[:, :, ec * 128:(ec + 1) * 128], ptb[:, 0:4, :], [tb[2]], [tk("vtm")])
                    if CUT <= 4:
                        continue
                    if b == 0:
                        S.op("dve", lambda e: e.memset(Sst, 0.0), (), [tk("Sst")])
                    qk_reads = [tk("kt_b", d_) for d_ in range(ND)] + [tk("qt_b", d_) for d_ in range(ND)]
                    for j in range(4):
                        mm_chain(sc4[:, j, :], [(kt_b[d_][:, j * 128:(j + 1) * 128], qt_b[d_][:, j * 128:(j + 1) * 128])
                                               for d_ in range(ND)], qk_reads, [tb[3]])
                    tt(sT4, sc4, mask4, ALU.mult, [tb[3], tc_], [tk("sT4")])

                    if CUT <= 5:
                        continue

                    def Ureg(c, d_):
                        if hg:
                            return banks[c // 4][:, (c % 4) * 128:(c % 4 + 1) * 128], tb[c // 4]
                        return banks[d_][:, 0:EW], tb[d_]

                    def U_mm(c):
                        j, half = c // 2, c % 2
                        rows = slice(half * 64, half * 64 + 64)
                        for d_ in range(ND):
                            ur, tu = Ureg(c, d_)
                            mm_chain(ur, [(ktm2[half][:, j, d_ * 128:(d_ + 1) * 128], vtm[:, j, :])], [tk("ktm"), tk("vtm")], [tu])

                    if hg:
                        for c in range(8):
                            U_mm(c)
                    for c in range(8):
                        j, half = c // 2, c % 2
                        rows = slice(half * 64, half * 64 + 64)
                        sb_ = Sbf[c % NSB]
                        tsb = tk("Sbf", c % NSB)
                        if not hg:
                            U_mm(c)
                        for d_ in range(ND):
                            ts(sb_[:, d_, :], Sst[:, d_, :], Ac[:, d_, c:c + 1], None, ALU.mult, None,
                               [tk("Sst"), tk("Ac", d_)], [tsb])
                        for d_ in range(ND):
                            ur, tu = Ureg(c, d_)
                            stt(Sst[:, d_, :], Sst[:, d_, :], Ac[:, d_, c:c + 1], ur, ALU.mult, ALU.add,
                                [tk("Sst"), tk("Ac", d_), tu], [tk("Sst")])
                        for ec in range(NE):
                            pairs = [(vtm[:, j, ec * 128:(ec + 1) * 128], sT4[:, j, half * 64:half * 64 + 64])]
                            pairs += [(sb_[:, d_, ec * 128:(ec + 1) * 128], qt_b[d_][:, c * 64:(c + 1) * 64]) for d_ in range(ND)]
                            mm_chain(banks[4 + ec][:, c * 64:(c + 1) * 64], pairs,
                                     [tk("vtm"), tk("sT4"), tsb] + [tk("qt_b", d_) for d_ in range(ND)], [tb[4 + ec]])
                    if CUT <= 6:
                        continue
                    oi = par % len(oloc)
                    ol, tol = oloc[oi], tk("oloc", oi)
                    for ec in range(NE):
                        cp_any(ol[:, ec, :], banks[4 + ec][:], [tb[4 + ec]], [tol], ec)
                    dma("sp", OL_d[:, hd * NE:(hd + 1) * NE, bsl], ol, [tol], [tk("OL", hd, b)], f"ol{oi}")
                    if hg and it + 1 < NHH * NB:
                        emit_EV((it + 1) % NSET)
                hq = hd % HPG
                dma("sp", cins[hd // HPG][:, hq * ND:(hq + 1) * ND, :], Sfull, [tk("Sst"), tk("atot")], [tk("ccin", hd // HPG)], "ccS")

        def stage_mix2(l):
            L = LAYERS[l]
            NH, ND, NE, esc = L["NH"], L["ND"], L["NE"], L["escale"]
            hg = L["kind"] == "hgrn"
            EW = NE * 128
            cw = EW + 1
            RC.reset()
            wsl = [RC.alloc([KC, 128], BF16) for _ in range(3 if hg else 5)]
            W.set_slots(wsl, "p2")
            lim = stage_end[("mix2", l)]
            HPG = CCG[l]
            for g in range(ccn[l]):
                ccs = S.new_dma_sem(f"cc{l}_{g}")
                S.op("pool", lambda e, g=g: e.collective_compute("AllGather", ALU.bypass,
                                                                 replica_groups=[[0, 1, 2, 3], [4, 5, 6, 7]][:ncores // 4],
                                                                 ins=[cc_in[l][g].ap()], outs=[cc_out[l][g].ap()]),
                     [tk("ccin", g)], [tk("ccout", g)], dma_sem=ccs, incv=1)
            gath = RC.alloc([4, ND, cw], F32)
            Tst = RC.alloc([ND, EW], F32)
            tmpS = RC.alloc([ND, EW], F32)
            aeff = RC.alloc([ND, 1], F32)
            Sst_b = RC.alloc([ND, EW], BF16)
            ol = RC.alloc([NE, 512], F32)
            qs = RC.alloc([ND, 512], BF16)
            sqs = [RC.alloc([512], BF16) for _ in range(2)]
            rs = RC.alloc([512], F32)
            gT = [RC.alloc([512], BF16) for _ in range(2 if hg else 4)]
            tmpo = RC.alloc([512], F32)
            gain = hgn if hg else glan
            if hg:
                hg_gT = [RC.alloc([512], BF16) for _ in range(4)]
                hg_ol = [RC.alloc([NE, 512], F32) for _ in range(4)]
                hg_qs = [RC.alloc([ND, 512], BF16) for _ in range(4)]
                hg_sq = [RC.alloc([512], BF16) for _ in range(4)]
                hg_rs = [RC.alloc([512], F32) for _ in range(4)]
                hg_tmp = [RC.alloc([512], F32) for _ in range(4)]
                hg_gbanks = [0, 1, 5, 6]
                hg_cbanks = [4, 7, 2, 3]
            couts = [c_.ap().rearrange("(r p) (u w) -> p r u w", p=128, w=cw) for c_ in cc_out[l]]
            pj_i = 0
            for hd in range(min(NH, int(os.environ.get('K_HEADS', NH)))):
                hold = W.pos
                gslots = [W.get(lim, hold) for _ in range(NE)]
                for r in range(3):
                    hq = hd % HPG
                    dma("sp", gath[:, r], couts[hd // HPG][:, r, hq * ND:(hq + 1) * ND, :], [tk("ccout", hd // HPG)], [tk("gath")], "gath")
                for dc in range(ND):
                    ts(Tst[:, dc, :], gath[:, 0, dc, 0:EW], pm[:, 0:1], None, ALU.mult, None, [tk("gath"), tc_], [tk("Tst")])
                    for r in (1, 2):
                        ts(aeff[:, dc, :], gath[:, r, dc, EW:EW + 1], pm[:, r:r + 1], pm[:, 4 + r:5 + r], ALU.mult, ALU.add,
                           [tk("gath"), tc_], [tk("aeff")])
                        ts(tmpS[:, dc, :], gath[:, r, dc, 0:EW], pm[:, r:r + 1], None, ALU.mult, None, [tk("gath"), tc_],
                           [tk("tmpS")])
                        stt(Tst[:, dc, :], Tst[:, dc, :], aeff[:, dc, :], tmpS[:, dc, :], ALU.mult, ALU.add,
                            [tk("Tst"), tk("aeff"), tk("tmpS")], [tk("Tst")])
                cp(Sst_b, Tst, [tk("Tst")], [tk("Sst_b")])
                if hg:
                    slot, tslot = gslots[0]
                    BS = [slice(b * BT, (b + 1) * BT) for b in range(NB)]
                    for b in range(NB):
                        gb = hg_gbanks[b]
                        mm_chain(banks[gb][:], [(slot[:, kc, :], hnT[:, kc, BS[b]]) for kc in range(KC)], [tslot, tk("hn", b)], [tb[gb]])
                        act(hg_gT[b], banks[gb][:], AF.Silu, [tb[gb]], [tk("gT4", b)])
                    for b in range(NB):
                        dma("sp", hg_ol[b], OL_d[:, hd:hd + 1, BS[b]], [tk("OL", hd, b)], [tk("ol2", b)], f"ol2{b}")
                        dma("sp", hg_qs[b], QS_d[:, hd:hd + 1, BS[b]], [tk("QS", hd, b)], [tk("qs2", b)], f"qs2{b}")
                    for b in range(NB):
                        cb = hg_cbanks[b]
                        mm_chain(banks[cb][:], [(Sst_b[:, 0, :], hg_qs[b][:, 0, :])], [tk("Sst_b"), tk("qs2", b)], [tb[cb]])
                    for b in range(NB):
                        cb = hg_cbanks[b]
                        tt(hg_ol[b][:, 0, :], hg_ol[b][:, 0, :], banks[cb][:], ALU.add, [tk("ol2", b), tb[cb]], [tk("ol2", b)])
                    for b in range(NB):
                        act(hg_sq[b], hg_ol[b][:, 0, :], AF.Square, [tk("ol2", b)], [tk("sq4", b)])
                    for b in range(NB):
                        gb = hg_gbanks[b]
                        mm_chain(banks[gb][:], [(ones_b, hg_sq[b])], [tk("sq4", b), tc_], [tb[gb]])
                    for b in range(NB):
                        gb = hg_gbanks[b]
                        act(hg_rs[b], banks[gb][:], AF.Sqrt, [tb[gb], tc_], [tk("rs4", b)], scale=1.0 / EW, bias=epsc)
                    for b in range(NB):
                        S.op("dve", lambda e, b=b: e.reciprocal(out=hg_rs[b], in_=hg_rs[b]), [tk("rs4", b)], [tk("rs4", b)])
                    for b in range(NB):
                        stt(hg_tmp[b], hg_ol[b][:, 0, :], gain[:, 0:1], hg_rs[b], ALU.mult, ALU.mult, [tk("ol2", b), tk("rs4", b), tc_],
                            [tk("tmp4", b)])
                    for b in range(NB):
                        tt(onT[:, hd, BS[b]], hg_tmp[b], hg_gT[b], ALU.mult, [tk("tmp4", b), tk("gT4", b)], [tk("on", b)])
                    continue
                for b in range(NB):
                    bsl = slice(b * BT, (b + 1) * BT)
                    dma("sp", ol, OL_d[:, hd * NE:(hd + 1) * NE, bsl], [tk("OL", hd, b)], [tk("ol2")], "ol2")
                    dma("sp", qs, QS_d[:, hd * ND:(hd + 1) * ND, bsl], [tk("QS", hd * ND + dc, b) for dc in range(ND)], [tk("qs2")], "qs2")
                    for ec in range(NE):
                        slot, tslot = gslots[ec]
                        gb = (0, 1, 2, 0)[ec]
                        mm_chain(banks[gb][:], [(slot[:, kc, :], hnT[:, kc, bsl]) for kc in range(KC)], [tslot, tk("hn", b)], [tb[gb]])
                        act(gT[ec], banks[gb][:], AF.Silu, [tb[gb]], [tk("gT", ec)])
                    for ec in range(NE):
                        mm_chain(banks[4 + ec][:], [(Sst_b[:, dc, ec * 128:(ec + 1) * 128], qs[:, dc, :]) for dc in range(ND)],
                                 [tk("Sst_b"), tk("qs2")], [tb[4 + ec]])
                        tt(ol[:, ec, :], ol[:, ec, :], banks[4 + ec][:], ALU.add, [tk("ol2"), tb[4 + ec]], [tk("ol2")])
                        q = sqs[ec % 2]
                        act(q, ol[:, ec, :], AF.Square, [tk("ol2")], [tk("sqs", ec % 2)])
                        mm_chain(banks[3][:], [(ones_b, q)], [tk("sqs", ec % 2), tc_], [tb[3]], start=(ec == 0), stop=(ec == NE - 1))
                    rstd_from_ssq(banks[3][:], EW, rs, tk("rs"), [tb[3], tc_])
                    for ec in range(NE):
                        stt(tmpo, ol[:, ec, :], gain[:, ec:ec + 1], rs, ALU.mult, ALU.mult, [tk("ol2"), tk("rs"), tc_], [tk("tmpo")])
                        tt(onT[:, hd * NE + ec, bsl], tmpo, gT[ec], ALU.mult, [tk("tmpo"), tk("gT", ec)], [tk("on", b)])

        def stage_out(l):
            RC.reset()
            wsl = [RC.alloc([KC, 128], BF16) for _ in range(5)]
            W.set_slots(wsl, "p3")
            lim = stage_end[("out", l)]
            mT = RC.alloc([KC, 512], F32)
            sqs = [RC.alloc([512], BF16) for _ in range(2)]
            mark = RC.off
            i = 0
            for b in range(NB):
                bsl = slice(b * BT, (b + 1) * BT)
                for n in range(KC):
                    slot, tslot = W.get(lim)
                    bi = i % 2
                    pj, tpj = banks[bi][:], tb[bi]
                    mm_chain(pj, [(slot[:, hd, :], onT[:, hd, bsl]) for hd in range(KC)], [tslot, tk("on", b)], [tpj])
                    evac_with_ssq(mT, sqs, n, pj, tpj, n == 0, n == KC - 1, i)
                    i += 1
                RC.off = mark
                if int(os.environ.get('K_OUT', 9)) >= 2:
                    post_block(mT, b, 4 * l + 1, 4 * l + 2, False)

        def stage_mlp(l, last_layer):
            RC.reset()
            wsl = [RC.alloc([KC, 128], BF16) for _ in range(4)]
            W.set_slots(wsl, "p4")
            lim = stage_end[("mlp", l)]
            mT = RC.alloc([KC, 512], F32)
            sqs = [RC.alloc([512], BF16) for _ in range(2)]
            rl = [RC.alloc([512], F32) for _ in range(2)]
            mark = RC.off
            cnt = [0]

            def up(b, f0=0, f1=64):
                bsl = slice(b * BT, (b + 1) * BT)
                for fu in range(f0, f1):
                    i = cnt[0]
                    slot, tslot = W.get(lim)
                    pj, tpj = banks[i % 2][:], tb[i % 2]
                    mm_chain(pj, [(slot[:, kc, :], hnT[:, kc, bsl]) for kc in range(KC)], [tslot, tk("hn", b)], [tpj])
                    r_ = rl[i % 2]
                    act(r_, pj, AF.Relu, [tpj], [tk("rl", i % 2)])
                    tt(uT[:, fu, :], r_, r_, ALU.mult, [tk("rl", i % 2)], [tk("u", fu)])
                    cnt[0] += 1

            def down(b):
                for n in range(KC):
                    i = cnt[0]
                    pj, tpj = banks[i % 2][:], tb[i % 2]
                    for g in range(4):
                        slot, tslot = W.get(lim)
                        mm_chain(pj, [(slot[:, j, :], uT[:, g * 16 + j, :]) for j in range(16)],
                                 [tslot] + [tk("u", g * 16 + j) for j in range(16)], [tpj], start=(g == 0), stop=(g == 3))
                    evac_with_ssq(mT, sqs, n, pj, tpj, n == 0, n == KC - 1, i)
                    cnt[0] += 1

            up(0)
            for b in range(NB):
                down(b)
                RC.off = mark
                post_block(mT, b, 4 * l + 3, None if last_layer else 4 * (l + 1), last_layer, defer=(up if b + 1 < NB else None), defer_arg=b + 1)

        def run_stages():
            stage_pre()
            S.barrier()
            if dbg == "pre":
                dump(lambda kc: hnT[:, kc, :], KC, True)
                return
            for l in range(n_layers):
                stage_mix1(l)
                S.barrier()
                if os.environ.get('K_STOP') == 'mix1':
                    dump(lambda kc: hnT[:, kc, :], KC, True)
                    return
                stage_mix2(l)
                S.barrier()
                if dbg == f"mix{l}":
                    dump(lambda kc: onT[:, kc, :], KC, True)
                    return
                stage_out(l)
                S.barrier()
                if dbg == f"out{l}":
                    dump(lambda kc: hnT[:, kc, :], KC, True)
                    return
                stage_mlp(l, l == n_layers - 1)
                S.barrier()

        run_stages()
        S.barrier()
        with nc.Block() as block:
            S.emit(block)
    return nc


def _tile_w(Wm, ncols):
    K, N = Wm.shape
    return np.ascontiguousarray(Wm.reshape(K // 128, 128, N // ncols, ncols).transpose(2, 1, 0, 3))


def _pcol(v):
    return np.ascontiguousarray(v.reshape(-1, 128).T)


_CACHE = {}


def prepare_inputs(inputs):
    f = lambda k: np.asarray(inputs[k], dtype=np.float32)
    nwl = []
    for l in range(2):
        for k in ("norm_mix_pre", "norm_mix_post", "norm_mlp_pre", "norm_mlp_post"):
            nwl.append(_pcol(f(k)[l]))
    nw = np.ascontiguousarray(np.stack(nwl, 1))
    lbl = np.ascontiguousarray(np.stack([_pcol(f("hgrn_lb_logits")[r]) for r in range(3)], 1))
    gw = f("gla_w_in")[0]
    shared = dict(
        nw=nw, lbl=lbl,
        hgn=np.ascontiguousarray(f("hgrn_norm")[0].reshape(128, 1)),
        glan=_pcol(f("gla_norm")[0]),
        bgk=_pcol(f("gla_b_gk")[0]),
        wgk=np.ascontiguousarray(f("gla_w_gk")[0]),
        wr=np.ascontiguousarray(gw[:, 6144:6160].reshape(16, 128, 16).transpose(1, 0, 2)),
        hg_in=_tile_w(f("hgrn_w_in")[0], 128),
        hg_out=_tile_w(f("hgrn_w_out")[0], 128),
        gl_in=_tile_w(np.ascontiguousarray(gw[:, :6144]), 128),
        gl_out=_tile_w(f("gla_w_out")[0], 128),
        w_up=np.stack([_tile_w(f("mlp_w_up")[l], 128) for l in range(2)]),
        w_dn=np.stack([_tile_w(f("mlp_w_down")[l], 128) for l in range(2)]),
    )
    x = f("x")
    in_maps = []
    for c in range(NCORES):
        b, p = c // 4, c % 4
        pmv = np.zeros((128, 8), np.float32)
        for r in range(4):
            pmv[:, r] = 1.0 if r < p else 0.0
            pmv[:, 4 + r] = 0.0 if r < p else 1.0
        d = dict(shared)
        d["x"] = np.ascontiguousarray(x[b, p * TC:(p + 1) * TC, :])
        d["pm"] = pmv
        in_maps.append(d)
    return in_maps


def kernel(**inputs):
    in_maps = prepare_inputs(inputs)
    if "nc" not in _CACHE:
        _CACHE["nc"] = build_program()
    res = run_bass_kernel_spmd(_CACHE["nc"], in_maps, core_ids=list(range(NCORES)))
    x = inputs["x"]
    out = np.empty(x.shape, np.float32)
    for c in range(NCORES):
        b, p = c // 4, c % 4
        out[b, p * TC:(p + 1) * TC, :] = res.results[c]["out"]
    return out
```
